# Optimizing a Trainium2 kernel written in Bass

```python
import jax, jax.numpy as jnp
from jax import lax
import numpy as np

D_MODEL = 2048
BATCH = 8
SEQ = 4096
DEPTH = 4

GRID_W = 64
CTX_LEN = 256
HEAD_DIM = 128
N_Q_HEADS = D_MODEL // 256
N_KV_HEADS = 2
Q_PER_KV = N_Q_HEADS // N_KV_HEADS
ATTN_WIDTH = N_Q_HEADS * HEAD_DIM
KV_WIDTH = N_KV_HEADS * HEAD_DIM
WINDOW = 128
BLOCK = 128
ROPE_BASE = 10000.0
ROPE_FREQS = HEAD_DIM // 4
RNN_WIDTH = D_MODEL // 2
RNN_BLOCKS = 8
RNN_BLOCK_DIM = RNN_WIDTH // RNN_BLOCKS
CONV_W = 4
CONV_PAD = (2, 1)
RGLRU_C = 8.0
D_FF = ((8 * D_MODEL // 3 + 127) // 128) * 128
N_MOD = 9
EPS = 1e-6
OFF_K = ATTN_WIDTH
OFF_V = OFF_K + KV_WIDTH
OFF_XR = OFF_V + KV_WIDTH
OFF_GR = OFF_XR + RNN_WIDTH
OFF_GA = OFF_GR + RNN_WIDTH
OFF_GB = OFF_GA + D_MODEL
IN_WIDTH = OFF_GB + D_MODEL
SPLITS = (OFF_K, OFF_V, OFF_XR, OFF_GR, OFF_GA, OFF_GB)

kernel_name = "hybrid_rglru_swa_macaron_prefix_dit"


def rms_norm(x, g):
    xf = x.astype(jnp.float32)
    y = xf * lax.rsqrt(jnp.mean(xf * xf, axis=-1, keepdims=True) + EPS)
    return (y * g.astype(jnp.float32)).astype(x.dtype)


def modulate(h, shift, scale):
    return h * (1 + scale) + shift


def swiglu(h, w_up, w_down):
    gate, up = jnp.split(h @ w_up, 2, axis=-1)
    return (jax.nn.silu(gate) * up) @ w_down


def axial_rope_tables(n_tokens):
    rows = n_tokens // GRID_W
    row = jnp.repeat(jnp.arange(rows, dtype=jnp.float32), GRID_W)
    col = jnp.tile(jnp.arange(GRID_W, dtype=jnp.float32), rows)
    inv_freq = ROPE_BASE ** (-jnp.arange(ROPE_FREQS, dtype=jnp.float32) / ROPE_FREQS)
    ang = jnp.stack([row[:, None] * inv_freq, col[:, None] * inv_freq], axis=1)
    return jnp.cos(ang), jnp.sin(ang)


def apply_axial_rope(t, cos, sin):
    shp = t.shape
    tr = t.reshape(*shp[:-1], 2, 2, ROPE_FREQS)
    t1, t2 = tr[..., 0, :], tr[..., 1, :]
    cs = cos[:, None].astype(t.dtype)
    sn = sin[:, None].astype(t.dtype)
    out = jnp.stack([t1 * cs - t2 * sn, t2 * cs + t1 * sn], axis=-2)
    return out.reshape(shp)


def windowed_gqa_with_context(q, k, v, kc, vc, sink):
    bsz, s_len = q.shape[:2]
    nb = s_len // BLOCK
    scale = HEAD_DIM ** -0.5
    qb = q.reshape(bsz, nb, BLOCK, N_KV_HEADS, Q_PER_KV, HEAD_DIM)

    def band(t):
        tp = jnp.pad(t, ((0, 0), (BLOCK, BLOCK), (0, 0), (0, 0)))
        tp = tp.reshape(bsz, nb + 2, BLOCK, N_KV_HEADS, HEAD_DIM)
        return jnp.concatenate([tp[:, :-2], tp[:, 1:-1], tp[:, 2:]], axis=2)

    kw, vw = band(k), band(v)
    s_loc = jnp.einsum('bnqhgd,bnkhd->bnhgqk', qb, kw).astype(jnp.float32) * scale
    qpos = jnp.arange(nb)[:, None] * BLOCK + jnp.arange(BLOCK)[None, :]
    kpos = (jnp.arange(nb)[:, None] - 1) * BLOCK + jnp.arange(3 * BLOCK)[None, :]
    kp = kpos[:, None, :]
    valid = (jnp.abs(kp - qpos[:, :, None]) <= WINDOW) & (kp >= 0) & (kp < s_len)
    s_loc = jnp.where(valid[None, :, None, None], s_loc, -jnp.inf)
    s_ctx = jnp.einsum('bnqhgd,bchd->bnhgqc', qb, kc).astype(jnp.float32) * scale
    sk = sink.astype(jnp.float32).reshape(N_KV_HEADS, Q_PER_KV)[None, None, :, :, None, None]
    m = jnp.maximum(jnp.maximum(s_loc.max(-1, keepdims=True), s_ctx.max(-1, keepdims=True)), sk)
    p_loc = jnp.exp(s_loc - m)
    p_ctx = jnp.exp(s_ctx - m)
    denom = p_loc.sum(-1, keepdims=True) + p_ctx.sum(-1, keepdims=True) + jnp.exp(sk - m)
    o = (jnp.einsum('bnhgqk,bnkhd->bnqhgd', (p_loc / denom).astype(v.dtype), vw)
         + jnp.einsum('bnhgqc,bchd->bnqhgd', (p_ctx / denom).astype(v.dtype), vc))
    return o.reshape(bsz, s_len, ATTN_WIDTH)


def context_attention(qc, kc, vc, sink):
    bsz, c_len = qc.shape[:2]
    qg = qc.reshape(bsz, c_len, N_KV_HEADS, Q_PER_KV, HEAD_DIM)
    s = jnp.einsum('bchgd,bkhd->bhgck', qg, kc).astype(jnp.float32) * (HEAD_DIM ** -0.5)
    sk = sink.astype(jnp.float32).reshape(N_KV_HEADS, Q_PER_KV)[None, :, :, None, None]
    m = jnp.maximum(s.max(-1, keepdims=True), sk)
    p = jnp.exp(s - m)
    denom = p.sum(-1, keepdims=True) + jnp.exp(sk - m)
    o = jnp.einsum('bhgck,bkhd->bchgd', (p / denom).astype(vc.dtype), vc)
    return o.reshape(bsz, c_len, ATTN_WIDTH)


def depthwise_conv(x, w, b):
    y = lax.conv_general_dilated(x, w[:, None, :].astype(x.dtype), window_strides=(1,), padding=[CONV_PAD],
                                 dimension_numbers=('NWC', 'WIO', 'NWC'), feature_group_count=x.shape[-1])
    return y + b


def block_diag(x, w, b):
    xb = x.reshape(*x.shape[:-1], RNN_BLOCKS, RNN_BLOCK_DIM)
    y = jnp.einsum('...nd,nde->...ne', xb, w.astype(jnp.float32))
    return y.reshape(x.shape) + b.astype(jnp.float32)


def rglru_coeffs(u, w, b, lam):
    uf = u.astype(jnp.float32)
    r = jax.nn.sigmoid(block_diag(uf, w[0], b[0]))
    i = jax.nn.sigmoid(block_diag(uf, w[1], b[1]))
    log_a = -RGLRU_C * r * jax.nn.softplus(-lam.astype(jnp.float32))
    a = jnp.exp(log_a)
    bx = jnp.sqrt(-jnp.expm1(2.0 * log_a)) * (i * uf)
    return a, bx


def linear_scan(a, bx, h0, reverse):
    def combine(e1, e2):
        a1, b1 = e1
        a2, b2 = e2
        return a1 * a2, a2 * b1 + b2
    a_cum, b_cum = lax.associative_scan(combine, (a, bx), reverse=reverse, axis=1)
    return a_cum * h0[:, None, :] + b_cum


def bidir_rglru(u_lat, u_ctx, conv_w, conv_b, rg_w, rg_b, rg_lambda, need_ctx):
    ul = depthwise_conv(u_lat, conv_w, conv_b)
    uc = depthwise_conv(u_ctx, conv_w, conv_b)
    h0 = jnp.zeros((u_ctx.shape[0], RNN_WIDTH), jnp.float32)
    lat_out, ctx_out = [], []
    for d, reverse in enumerate((False, True)):
        a_c, b_c = rglru_coeffs(uc, rg_w[d], rg_b[d], rg_lambda[d])
        h_ctx = linear_scan(a_c, b_c, h0, reverse)
        h_end = h_ctx[:, 0] if reverse else h_ctx[:, -1]
        a_l, b_l = rglru_coeffs(ul, rg_w[d], rg_b[d], rg_lambda[d])
        lat_out.append(linear_scan(a_l, b_l, h_end, reverse))
        ctx_out.append(h_ctx)
    y_lat = (lat_out[0] + lat_out[1]).astype(u_lat.dtype)
    y_ctx = (ctx_out[0] + ctx_out[1]).astype(u_ctx.dtype) if need_ctx else None
    return y_lat, y_ctx


def token_mixing(h, hc, w_in, sink, conv_w, conv_b, rg_w, rg_b, rg_lambda, w_o_attn, w_o_rnn, w_out,
                 cos, sin, need_ctx):
    bsz, s_len, _ = h.shape
    c_len = hc.shape[1]
    q, k, v, xr, gr, ga, gb = jnp.split(h @ w_in, SPLITS, axis=-1)
    q = apply_axial_rope(q.reshape(bsz, s_len, N_Q_HEADS, HEAD_DIM), cos, sin)
    k = apply_axial_rope(k.reshape(bsz, s_len, N_KV_HEADS, HEAD_DIM), cos, sin)
    v = v.reshape(bsz, s_len, N_KV_HEADS, HEAD_DIM)
    if need_ctx:
        qc, kc, vc, xrc, grc, gac, gbc = jnp.split(hc @ w_in, SPLITS, axis=-1)
    else:
        kc, vc, xrc = jnp.split(hc @ w_in[:, OFF_K:OFF_GR], (KV_WIDTH, 2 * KV_WIDTH), axis=-1)
    kc = kc.reshape(bsz, c_len, N_KV_HEADS, HEAD_DIM)
    vc = vc.reshape(bsz, c_len, N_KV_HEADS, HEAD_DIM)

    y_attn = windowed_gqa_with_context(q, k, v, kc, vc, sink)
    r_lat, r_ctx = bidir_rglru(xr, xrc, conv_w, conv_b, rg_w, rg_b, rg_lambda, need_ctx)

    def merge(ya, yr, g_r, g_a, g_b):
        y_rnn = yr * jax.nn.gelu(g_r)
        return (jax.nn.sigmoid(g_a) * (ya @ w_o_attn) + jax.nn.sigmoid(g_b) * (y_rnn @ w_o_rnn)) @ w_out

    y = merge(y_attn, r_lat, gr, ga, gb)
    yc = merge(context_attention(qc, kc, vc, sink), r_ctx, grc, gac, gbc) if need_ctx else None
    return y, yc


def setup_inputs(seed: int = 0) -> dict:
    key = jax.random.key(seed)
    ks = jax.random.split(key, 20)
    D = D_MODEL
    f32 = jnp.float32

    def nrm(k, shape, fan_in, gain=1.0):
        return jax.random.normal(k, shape, f32) * (gain * fan_in ** -0.5)

    x = jax.random.normal(ks[0], (BATCH, SEQ, D), f32)
    c = jax.random.normal(ks[1], (BATCH, D), f32)
    ctx = jax.random.normal(ks[2], (BATCH, CTX_LEN, D), f32)
    c_ctx = jax.random.normal(ks[3], (D,), f32)
    w_ada = nrm(ks[4], (DEPTH, D, N_MOD * D), D, 0.5)
    b_ada = 0.01 * jax.random.normal(ks[5], (DEPTH, N_MOD * D), f32)
    norm_g = 1.0 + 0.02 * jax.random.normal(ks[6], (DEPTH, 3, D), f32)
    final_g = 1.0 + 0.02 * jax.random.normal(ks[7], (D,), f32)
    w_ffn_up = nrm(ks[8], (DEPTH, 2, D, 2 * D_FF), D)
    w_ffn_down = nrm(ks[9], (DEPTH, 2, D_FF, D), D_FF)
    w_in = nrm(ks[10], (DEPTH, D, IN_WIDTH), D)
    attn_sink = 0.5 * jax.random.normal(ks[11], (DEPTH, N_Q_HEADS), f32)
    conv_w = nrm(ks[12], (DEPTH, CONV_W, RNN_WIDTH), CONV_W)
    conv_b = 0.01 * jax.random.normal(ks[13], (DEPTH, RNN_WIDTH), f32)
    rg_w = nrm(ks[14], (DEPTH, 2, 2, RNN_BLOCKS, RNN_BLOCK_DIM, RNN_BLOCK_DIM), RNN_BLOCK_DIM)
    rg_b = 0.01 * jax.random.normal(ks[15], (DEPTH, 2, 2, RNN_WIDTH), f32)
    a_pow_c = jax.random.uniform(ks[16], (DEPTH, 2, RNN_WIDTH), f32, minval=0.9, maxval=0.999)
    a0 = a_pow_c ** (1.0 / RGLRU_C)
    rg_lambda = jnp.log(a0) - jnp.log1p(-a0)
    w_o_attn = nrm(ks[17], (DEPTH, ATTN_WIDTH, D), ATTN_WIDTH)
    w_o_rnn = nrm(ks[18], (DEPTH, RNN_WIDTH, D), RNN_WIDTH)
    w_out = nrm(ks[19], (DEPTH, D, D), D)
    return {"x": x, "c": c, "ctx": ctx, "c_ctx": c_ctx, "w_ada": w_ada, "b_ada": b_ada,
            "norm_g": norm_g, "final_g": final_g, "w_ffn_up": w_ffn_up, "w_ffn_down": w_ffn_down,
            "w_in": w_in, "attn_sink": attn_sink, "conv_w": conv_w, "conv_b": conv_b,
            "rg_w": rg_w, "rg_b": rg_b, "rg_lambda": rg_lambda, "w_o_attn": w_o_attn,
            "w_o_rnn": w_o_rnn, "w_out": w_out}


def reference(x, c, ctx, c_ctx, w_ada, b_ada, norm_g, final_g, w_ffn_up, w_ffn_down, w_in, attn_sink,
              conv_w, conv_b, rg_w, rg_b, rg_lambda, w_o_attn, w_o_rnn, w_out):
    s_len = x.shape[1]
    cos, sin = axial_rope_tables(s_len)
    sc = jax.nn.silu(c)
    scc = jax.nn.silu(c_ctx)
    xc = ctx
    for l in range(DEPTH):
        need_ctx = l < DEPTH - 1
        mx = jnp.split((sc @ w_ada[l] + b_ada[l])[:, None, :], N_MOD, axis=-1)
        mc = jnp.split((scc @ w_ada[l] + b_ada[l])[None, None, :], N_MOD, axis=-1)
        x = x + 0.5 * mx[2] * swiglu(modulate(rms_norm(x, norm_g[l, 0]), mx[0], mx[1]),
                                     w_ffn_up[l, 0], w_ffn_down[l, 0])
        xc = xc + 0.5 * mc[2] * swiglu(modulate(rms_norm(xc, norm_g[l, 0]), mc[0], mc[1]),
                                       w_ffn_up[l, 0], w_ffn_down[l, 0])
        h = modulate(rms_norm(x, norm_g[l, 1]), mx[3], mx[4])
        hc = modulate(rms_norm(xc, norm_g[l, 1]), mc[3], mc[4])
        y, yc = token_mixing(h, hc, w_in[l], attn_sink[l], conv_w[l], conv_b[l], rg_w[l], rg_b[l],
                             rg_lambda[l], w_o_attn[l], w_o_rnn[l], w_out[l], cos, sin, need_ctx)
        x = x + mx[5] * y
        x = x + 0.5 * mx[8] * swiglu(modulate(rms_norm(x, norm_g[l, 2]), mx[6], mx[7]),
                                     w_ffn_up[l, 1], w_ffn_down[l, 1])
        if need_ctx:
            xc = xc + mc[5] * yc
            xc = xc + 0.5 * mc[8] * swiglu(modulate(rms_norm(xc, norm_g[l, 2]), mc[6], mc[7]),
                                           w_ffn_up[l, 1], w_ffn_down[l, 1])
    return rms_norm(x, final_g)
```

```python
import contextlib
import numpy as np
import concourse.bass as bass
import concourse.mybir as mybir
from concourse.bass_utils import run_bass_kernel_spmd

F32 = mybir.dt.float32
BF16 = mybir.dt.bfloat16
AF = mybir.ActivationFunctionType
ALU = mybir.AluOpType

D = 2048
KC = 16
FF = 5504
FC = 43
CTX = 256
INW = 7680
EPS = 1e-6
NQH = 8
ENGS = ("pe", "act", "dve", "pool", "sp")


class _Op:
    __slots__ = ("eng", "fn", "deps", "kind", "sig", "needed", "dsem")

    def __init__(self, eng, fn, kind):
        self.eng = eng
        self.fn = fn
        self.kind = kind
        self.deps = []
        self.sig = None
        self.needed = False
        self.dsem = None


class Prog:
    NDSEM = 8

    def __init__(self, sems, base):
        self.sems = sems
        self.base = dict(base)
        self.ops = {e: [] for e in ENGS}
        self.last_w = {}
        self.readers = {}
        self.dma_rr = {e: 0 for e in ENGS}
        self.dma_last = {}
        self.psi = 0

    def ps(self):
        i = self.psi % 8
        self.psi += 1
        return i

    def add(self, eng, fn, reads=(), writes=(), kind="c"):
        op = _Op(eng, fn, kind)
        deps = []
        for k in reads:
            w = self.last_w.get(k)
            if w is not None:
                deps.append((w, True))
        for k in writes:
            w = self.last_w.get(k)
            if w is not None:
                deps.append((w, False))
            for r in self.readers.get(k, ()):
                deps.append((r, False))
        if kind == "d":
            i = self.dma_rr[eng] % self.NDSEM
            self.dma_rr[eng] += 1
            op.dsem = ("dma", eng, i)
            prev = self.dma_last.get(op.dsem)
            if prev is not None:
                deps.append((prev, True))
            self.dma_last[op.dsem] = op
        elif kind == "cc":
            op.dsem = "cc"
        seen = set()
        for d, raw in deps:
            if d is op or id(d) in seen:
                continue
            if d.kind == "c" and kind == "c" and d.eng == eng and (eng == "pe" or not raw):
                continue
            seen.add(id(d))
            op.deps.append(d)
            d.needed = True
        for k in writes:
            self.last_w[k] = op
            self.readers[k] = []
        for k in reads:
            self.readers.setdefault(k, []).append(op)
        self.ops[eng].append(op)
        return op

    def finalize(self):
        cnt = dict(self.base)
        for e in ENGS:
            last = None
            for op in self.ops[e]:
                if op.kind == "c":
                    last = op
            if last is not None:
                last.needed = True
            for op in self.ops[e]:
                if op.kind == "d":
                    cnt[op.dsem] += 16
                    op.sig = (op.dsem, cnt[op.dsem])
                elif op.kind == "cc":
                    cnt["cc"] += 1
                    op.sig = ("cc", cnt["cc"])
                elif op.needed:
                    cnt[e] += 1
                    op.sig = (e, cnt[e])
        self.final = cnt

    def emit(self, eng, eo):
        known = {}
        for op in self.ops[eng]:
            for d in op.deps:
                s, v = d.sig
                if known.get(s, -1) < v:
                    eo.wait_ge(self.sems[s], v)
                    known[s] = v
            ins = op.fn(eo)
            if op.kind == "d":
                ins.then_inc(self.sems[op.dsem], 16)
            elif op.kind == "cc":
                ins.then_inc(self.sems["cc"], 1)
            elif op.needed:
                ins.then_inc(self.sems[eng], 1)
        for s, v in self.final.items():
            if v > self.base[s]:
                eo.wait_ge(self.sems[s], v)


class TW:
    def __init__(self, nc, name, nmat, rows, cols, dt):
        self.name = name
        self.nmat = nmat
        self.rows = rows
        self.cols = cols
        self.dt = dt
        self.src = nc.dram_tensor(name + "_s", [nmat * rows, cols], F32, kind="ExternalInput").ap()
        self.nat = [nc.dram_tensor(f"{name}_n{m}", [rows, cols], dt, kind="Internal").ap() for m in range(nmat)]

    def prologue(self, P, m):
        RB = 512
        for r0 in range(0, self.rows, RB):
            r1 = min(self.rows, r0 + RB)
            P.add("pool", lambda e, r0=r0, r1=r1: e.dma_start(
                out=self.nat[m][r0:r1, :], in_=self.src[m * self.rows + r0:m * self.rows + r1, :]),
                writes=[f"{self.name}_n{m}_{r0}"], kind="d")

    def keys(self, m):
        return [f"{self.name}_n{m}_{r0}" for r0 in range(0, self.rows, 512)]

    def load(self, P, eng, dst, m, c0, ncols, writes, dep=False):
        P.add(eng, lambda e: e.dma_start(out=dst, in_=self.nat[m][:, c0:c0 + ncols].rearrange("(k p) c -> p k c", p=128)),
              reads=(self.keys(m) if dep else []), writes=list(writes), kind="d")


def build(S, L, NR):
    T = CTX + S
    NB = T // 128
    tiles = [(0, CTX, True)] + [(CTX + 512 * i, 512, False) for i in range(S // 512)]
    NROW = 48 + 280 * L
    nc = bass.Bass("TRN2", target_bir_lowering=False)

    def din(name, shape, dt=F32):
        return nc.dram_tensor(name, list(shape), dt, kind="ExternalInput").ap()

    def dscr(name, shape, dt):
        return nc.dram_tensor(name, list(shape), dt, kind="Internal").ap()

    x_in = din("x_b", [S, D])
    ctx_in = din("ctx_b", [CTX, D])
    vecs_in = din("vecs", [NROW, 128])
    sink_in = din("sink", [1, L * 8])
    ident_in = din("ident", [128, 128])
    cos_in = din("rope_cos", [128, S])
    sin_in = din("rope_sin", [128, S])
    mask_in = din("masks", [128, 2, 512])
    out = nc.dram_tensor("out", [S, D], F32, kind="ExternalOutput").ap()

    w_ada = TW(nc, "w_ada", L, D, 18432, BF16)
    w_up = TW(nc, "w_up", L * 2, D, 2 * FF, BF16)
    w_dn = TW(nc, "w_dn", L * 2, FF, D, BF16)
    w_in = TW(nc, "w_in", L, D, INW, BF16)
    w_oa = TW(nc, "w_oa", L, 1024, D, BF16)
    w_or = TW(nc, "w_or", L, 1024, D, BF16)
    w_out = TW(nc, "w_out", L, D, D, BF16)
    rg_w = TW(nc, "rg_w", L, 32 * 128, 128, F32)
    allw = [w_ada, w_up, w_dn, w_in, w_oa, w_or, w_out, rg_w]

    xT = dscr("xT", [D, T], F32)
    qT = dscr("qT", [1024, T], BF16)
    kT = dscr("kT", [256, T], BF16)
    vS = dscr("vS", [T, 256], BF16)
    uT = dscr("uT", [1024, T], F32)
    ggT = dscr("ggT", [1024, T], F32)
    yaT = dscr("yaT", [1024, T], BF16)
    yrT = dscr("yrT", [1024, T], BF16)

    es = contextlib.ExitStack()
    with es:
        semnames = list(ENGS) + ["cc"] + [("dma", e, i) for e in ENGS for i in range(Prog.NDSEM)]
        sems = {}
        for k in semnames:
            nm = k if isinstance(k, str) else f"d_{k[1]}_{k[2]}"
            sems[k] = es.enter_context(nc.semaphore("s_" + nm))
        state = {"base": {k: 0 for k in semnames}}

        def sb(stack, name, shape, dt=F32):
            state["nid"] = state.get("nid", 0) + 1
            return stack.enter_context(nc.sbuf_tensor(f"sb{state['nid']}_{name}", list(shape), dt))

        PS = [es.enter_context(nc.psum_tensor(f"psb{i}", [128, 512], F32)) for i in range(8)]

        ident = sb(es, "ident", [128, 128])
        ones_bf = sb(es, "ones_bf", [128, 128], BF16)
        cst = sb(es, "cst", [128, 4])
        vT = sb(es, "vecsT", [128, NROW])
        mod = sb(es, "mod", [128, L, 144, 2])
        Aco = sb(es, "Aco", [128, L, 3, 16, 2])
        Gco = sb(es, "Gco", [128, L, 3, 16, 2])
        clam = sb(es, "clam", [128, L, 16])
        skr = sb(es, "skr", [1, L * 8])
        scT = sb(es, "scT", [128, 16, 2], BF16)

        def run_phase(P):
            P.finalize()
            with nc.Block() as blk:
                @blk.tensor
                def _(e):
                    P.emit("pe", e)

                @blk.scalar
                def _(e):
                    P.emit("act", e)

                @blk.vector
                def _(e):
                    P.emit("dve", e)

                @blk.gpsimd
                def _(e):
                    P.emit("pool", e)

                @blk.sync
                def _(e):
                    P.emit("sp", e)
            state["base"] = dict(P.final)

        def newP():
            return Prog(sems, state["base"])

        def voff(l, what):
            return 48 + l * 280 + {"norm_g": 0, "b_ada": 48, "conv_w": 192, "conv_b": 224, "rg_b": 232, "rg_lam": 264}[what]

        def load_x(P, xs, t0, N, tag="xs"):
            P.add("sp", lambda e: e.dma_start(out=xs[:, :, :N], in_=xT[:, t0:t0 + N].rearrange("(k p) n -> p k n", p=128)),
                  reads=["xT"], writes=[tag], kind="d")

        def store_x(P, xs, t0, N, tag="xs"):
            P.add("sp", lambda e: e.dma_start(out=xT[:, t0:t0 + N].rearrange("(k p) n -> p k n", p=128), in_=xs[:, :, :N]),
                  reads=[tag], writes=["xT"], kind="d")

        def rstd_tile(P, xs, N, sq, rstd, tag="xs"):
            pi = P.ps()
            for k in range(KC):
                P.add("act", lambda e, k=k: e.activation(out=sq[k % 2][:, :N], in_=xs[:, k, :N], func=AF.Square),
                      reads=[tag], writes=[f"sq{k % 2}"])
                P.add("pe", lambda e, k=k: e.matmul(PS[pi][:, :N], lhsT=ones_bf[:], rhs=sq[k % 2][:, :N], start=(k == 0), stop=(k == KC - 1)),
                      reads=[f"sq{k % 2}"], writes=[f"ps{pi}"])
            P.add("act", lambda e: e.activation(out=rstd[:, :N], in_=PS[pi][:, :N], func=AF.Sqrt, scale=1.0 / D, bias=cst[:, 0:1]),
                  reads=[f"ps{pi}"], writes=["rstd"])
            P.add("dve", lambda e: e.reciprocal(out=rstd[:, :N], in_=rstd[:, :N]), reads=["rstd"], writes=["rstd"])

        def norm_tile(P, xs, N, l, j, col, sq, rstd, tmp, hT, tkey="tmp"):
            rstd_tile(P, xs, N, sq, rstd)
            for k in range(KC):
                P.add("dve", lambda e, k=k: e.scalar_tensor_tensor(
                    out=tmp[k % 2][:, :N], in0=xs[:, k, :N], scalar=Aco[:, l, j, k, col:col + 1], in1=rstd[:, :N],
                    op0=ALU.mult, op1=ALU.mult), reads=["xs", "rstd"], writes=[f"{tkey}{k % 2}"])
                P.add("act", lambda e, k=k: e.activation(
                    out=hT[:, k, :N], in_=tmp[k % 2][:, :N], func=AF.Identity,
                    bias=mod[:, l, (3 * j) * 16 + k, col:col + 1], scale=1.0), reads=[f"{tkey}{k % 2}"], writes=[f"hT{k}"])

        HT_KEYS = [f"hT{k}" for k in range(KC)]

        def cast_group(P, l, which):
            if l >= L:
                return
            if which == "up":
                w_up.prologue(P, 2 * l)
                w_up.prologue(P, 2 * l + 1)
            elif which == "dn":
                w_dn.prologue(P, 2 * l)
                w_dn.prologue(P, 2 * l + 1)
            else:
                for w in (w_in, rg_w, w_oa, w_or, w_out):
                    w.prologue(P, l)

        with contextlib.ExitStack() as ph:
            P = newP()
            rows = sb(ph, "rows", [128, 2, 128])
            lamw = sb(ph, "lamw", [128, 8, 16])
            wadat = [sb(ph, f"wadat{i}", [128, 16, 1024], BF16) for i in range(2)]
            xin = sb(ph, "xin", [128, 4, D])
            xs0 = sb(ph, "xs0", [128, KC, 512])

            for m in range(L):
                w_ada.prologue(P, m)
            cast_group(P, 0, "up")
            cast_group(P, 0, "dn")
            cast_group(P, 0, "mix")

            P.add("sp", lambda e: e.dma_start(out=ident[:], in_=ident_in), writes=["ident"], kind="d")
            P.add("sp", lambda e: e.dma_start(out=skr[:], in_=sink_in), writes=["skr"], kind="d")
            P.add("dve", lambda e: e.memset(ones_bf[:], 1.0), writes=["ones_bf"])
            P.add("dve", lambda e: e.memset(cst[:, 0:1], EPS), writes=["cst"])
            P.add("dve", lambda e: e.memset(cst[:, 1:2], 1.0), writes=["cst"])
            P.add("dve", lambda e: e.memset(cst[:, 2:3], 0.0), writes=["cst"])
            P.add("act", lambda e: e.activation(out=skr[:], in_=skr[:], func=AF.Exp), reads=["skr"], writes=["skr"])

            for i, r0 in enumerate(range(0, NROW, 128)):
                n = min(128, NROW - r0)
                b = i % 2
                pi = P.ps()
                P.add("sp", lambda e, r0=r0, n=n, b=b: e.dma_start(out=rows[:n, b, :], in_=vecs_in[r0:r0 + n, :]), writes=[f"rows{b}"], kind="d")
                P.add("pe", lambda e, n=n, b=b, pi=pi: e.transpose(out=PS[pi][:, :n], in_=rows[:n, b, :], identity=ident[:n, :n]),
                      reads=[f"rows{b}", "ident"], writes=[f"ps{pi}"])
                P.add("dve", lambda e, r0=r0, n=n, pi=pi: e.tensor_copy(out=vT[:, r0:r0 + n], in_=PS[pi][:, :n]), reads=[f"ps{pi}"], writes=["vT"])
            for col in range(2):
                P.add("act", lambda e, col=col: e.activation(out=scT[:, :, col], in_=vT[:, col * 16:(col + 1) * 16], func=AF.Silu),
                      reads=["vT"], writes=["scT"])
            for l in range(L):
                lam = vT[:, voff(l, "rg_lam"):voff(l, "rg_lam") + 16]
                W = lambda i: lamw[:, i, :]
                seq = [
                    ("dve", lambda e, lam=lam: e.tensor_scalar(out=W(0), in0=lam, scalar1=-1.0, scalar2=None, op0=ALU.mult)),
                    ("dve", lambda e, lam=lam: e.tensor_tensor(out=W(0), in0=W(0), in1=lam, op=ALU.max)),
                    ("act", lambda e: e.activation(out=W(1), in_=W(0), func=AF.Exp, scale=-1.0)),
                    ("dve", lambda e: e.tensor_scalar(out=W(2), in0=W(1), scalar1=2.0, scalar2=None, op0=ALU.add)),
                    ("dve", lambda e: e.reciprocal(out=W(2), in_=W(2))),
                    ("dve", lambda e: e.tensor_tensor(out=W(2), in0=W(2), in1=W(1), op=ALU.mult)),
                    ("dve", lambda e: e.tensor_tensor(out=W(3), in0=W(2), in1=W(2), op=ALU.mult)),
                    ("dve", lambda e: e.tensor_scalar(out=W(4), in0=W(3), scalar1=1.0 / 11, scalar2=1.0 / 9, op0=ALU.mult, op1=ALU.add)),
                    ("dve", lambda e: e.tensor_tensor(out=W(4), in0=W(4), in1=W(3), op=ALU.mult)),
                    ("dve", lambda e: e.tensor_scalar(out=W(4), in0=W(4), scalar1=1.0 / 7, scalar2=None, op0=ALU.add)),
                    ("dve", lambda e: e.tensor_tensor(out=W(4), in0=W(4), in1=W(3), op=ALU.mult)),
                    ("dve", lambda e: e.tensor_scalar(out=W(4), in0=W(4), scalar1=1.0 / 5, scalar2=None, op0=ALU.add)),
                    ("dve", lambda e: e.tensor_tensor(out=W(4), in0=W(4), in1=W(3), op=ALU.mult)),
                    ("dve", lambda e: e.tensor_scalar(out=W(4), in0=W(4), scalar1=1.0 / 3, scalar2=None, op0=ALU.add)),
                    ("dve", lambda e: e.tensor_tensor(out=W(4), in0=W(4), in1=W(3), op=ALU.mult)),
                    ("dve", lambda e: e.tensor_scalar(out=W(4), in0=W(4), scalar1=1.0, scalar2=None, op0=ALU.add)),
                    ("dve", lambda e: e.tensor_tensor(out=W(4), in0=W(4), in1=W(2), op=ALU.mult)),
                    ("dve", lambda e, lam=lam: e.tensor_scalar(out=W(5), in0=lam, scalar1=-1.0, scalar2=0.0, op0=ALU.mult, op1=ALU.max)),
                    ("dve", lambda e: e.tensor_scalar(out=W(4), in0=W(4), scalar1=2.0, scalar2=None, op0=ALU.mult)),
                    ("dve", lambda e: e.tensor_tensor(out=W(4), in0=W(4), in1=W(5), op=ALU.add)),
                    ("dve", lambda e, l=l: e.tensor_scalar(out=clam[:, l, :], in0=W(4), scalar1=-8.0, scalar2=None, op0=ALU.mult)),
                ]
                for eng, fn in seq:
                    P.add(eng, fn, reads=["vT", "lamw"], writes=["lamw", "clam"])

            OG = 8
            gi = 0
            for l in range(L):
                pi = P.ps()
                for og in range(144 // OG):
                    b = gi % 2
                    gi += 1
                    w_ada.load(P, "sp", wadat[b][:, :, :], l, og * OG * 128, OG * 128, writes=[f"wadat{b}"], dep=True)
                    for oo in range(OG):
                        o = og * OG + oo
                        for k in range(KC):
                            P.add("pe", lambda e, b=b, oo=oo, o=o, k=k, pi=pi: e.matmul(
                                PS[pi][:, 2 * o:2 * o + 2], lhsT=wadat[b][:, k, oo * 128:(oo + 1) * 128], rhs=scT[:, k, :], start=(k == 0), stop=(k == KC - 1)),
                                reads=[f"wadat{b}", "scT"], writes=[f"ps{pi}"])
                bo = voff(l, "b_ada")
                for col in range(2):
                    P.add("dve", lambda e, l=l, col=col, pi=pi, bo=bo: e.tensor_tensor(
                        out=mod[:, l, :, col], in0=PS[pi][:, 0:288].rearrange("p (o c) -> p o c", c=2)[:, :, col],
                        in1=vT[:, bo:bo + 144], op=ALU.add), reads=[f"ps{pi}", "vT"], writes=["mod"])
                for j in range(3):
                    go = voff(l, "norm_g") + j * 16
                    for col in range(2):
                        P.add("dve", lambda e, l=l, j=j, col=col, go=go: e.scalar_tensor_tensor(
                            out=Aco[:, l, j, :, col], in0=mod[:, l, (3 * j + 1) * 16:(3 * j + 2) * 16, col], scalar=1.0,
                            in1=vT[:, go:go + 16], op0=ALU.add, op1=ALU.mult), reads=["mod", "vT"], writes=["Aco"])
                    P.add("dve", lambda e, l=l, j=j: e.tensor_scalar(
                        out=Gco[:, l, j, :, :], in0=mod[:, l, (3 * j + 2) * 16:(3 * j + 3) * 16, :], scalar1=(1.0 if j == 1 else 0.5),
                        scalar2=None, op0=ALU.mult), reads=["mod"], writes=["Gco"])

            for (t0, N, isctx) in tiles:
                nb = N // 128
                src = ctx_in if isctx else x_in
                r0 = 0 if isctx else t0 - CTX
                P.add("sp", lambda e, src=src, r0=r0, nb=nb, N=N: e.dma_start(
                    out=xin[:, :nb, :], in_=src[r0:r0 + N, :].rearrange("(b p) d -> p b d", p=128)), writes=["xin"], kind="d")
                for k in range(KC):
                    pi = P.ps()
                    for bb in range(nb):
                        P.add("pe", lambda e, k=k, bb=bb, pi=pi: e.transpose(
                            out=PS[pi][:, bb * 128:(bb + 1) * 128], in_=xin[:, bb, k * 128:(k + 1) * 128], identity=ident[:]),
                            reads=["xin", "ident"], writes=[f"ps{pi}"])
                    eng = "act" if k % 2 else "dve"
                    if eng == "act":
                        P.add("act", lambda e, k=k, pi=pi, N=N: e.activation(out=xs0[:, k, :N], in_=PS[pi][:, :N], func=AF.Identity),
                              reads=[f"ps{pi}"], writes=[f"xs0_{k}"])
                    else:
                        P.add("dve", lambda e, k=k, pi=pi, N=N: e.tensor_copy(out=xs0[:, k, :N], in_=PS[pi][:, :N]),
                              reads=[f"ps{pi}"], writes=[f"xs0_{k}"])
                P.add("sp", lambda e, t0=t0, N=N: e.dma_start(
                    out=xT[:, t0:t0 + N].rearrange("(k p) n -> p k n", p=128), in_=xs0[:, :, :N]),
                    reads=[f"xs0_{k}" for k in range(KC)], writes=["xT"], kind="d")
            run_phase(P)

        def phase_ffn(l, jf, tl):
            j = 0 if jf == 0 else 2
            mi = l * 2 + jf
            with contextlib.ExitStack() as ph:
                P = newP()
                xq = sb(ph, "xq", [128, KC, 128])
                sq = [sb(ph, f"sq{i}", [128, 128], BF16) for i in range(2)]
                rstd = sb(ph, "rstd", [128, 128])
                sg = [sb(ph, f"sg{i}", [128, 512]) for i in range(3)]
                hT = sb(ph, "hT", [128, 2, KC, 512], BF16)
                aT = sb(ph, "aT", [128, FC, 1024], BF16)
                wu = [sb(ph, f"wu{i}", [128, 2, KC, 256], BF16) for i in range(2)]
                wd = [sb(ph, f"wd{i}", [128, FC, 128], BF16) for i in range(2)]
                xm = [sb(ph, f"xm{i}", [128, 512]) for i in range(3)]
                if jf == 0:
                    cast_group(P, l + 1, "up")
                groups = [[t] for t in tl if t[2]]
                lat = [t for t in tl if not t[2]]
                for i in range(0, len(lat), 2):
                    groups.append(lat[i:i + 2])
                sgc = 0

                def norm_pieces(grp):
                    colg = 1 if grp[0][2] else 0
                    pcs = []
                    for h, (t0, N, isctx) in enumerate(grp):
                        for q in range(N // 128):
                            def piece(h=h, tq=t0 + q * 128, q=q, colg=colg):
                                P.add("sp", lambda e: e.dma_start(out=xq[:, :, :], in_=xT[:, tq:tq + 128].rearrange("(k p) n -> p k n", p=128)),
                                      writes=["xs"], kind="d")
                                norm_tile(P, xq, 128, l, j, colg, sq, rstd, sg, hT[:, h, :, q * 128:(q + 1) * 128], tkey="sg")
                            pcs.append(piece)
                    return pcs

                for pc in norm_pieces(groups[0]):
                    pc()
                for gi, grp in enumerate(groups):
                    col = 1 if grp[0][2] else 0
                    nxt = norm_pieces(groups[gi + 1]) if gi + 1 < len(groups) else []
                    for f in range(FC):
                        b = (f // 2) % 2
                        jj = f % 2
                        if jj == 0:
                            nc_ = min(256, FF - f * 128)
                            w_up.load(P, "sp", wu[b][:, 0, :, :nc_], mi, f * 128, nc_, writes=[f"wu{b}g"])
                            w_up.load(P, "sp", wu[b][:, 1, :, :nc_], mi, FF + f * 128, nc_, writes=[f"wu{b}u"])
                        for h, (t0, N, isctx) in enumerate(grp):
                            pg, pu = P.ps(), P.ps()
                            for k in range(KC):
                                for (hh, pp, key) in ((0, pg, "g"), (1, pu, "u")):
                                    P.add("pe", lambda e, b=b, hh=hh, pp=pp, k=k, N=N, jj=jj, h=h: e.matmul(
                                        PS[pp][:, :N], lhsT=wu[b][:, hh, k, jj * 128:(jj + 1) * 128], rhs=hT[:, h, k, :N], start=(k == 0), stop=(k == KC - 1)),
                                        reads=[f"wu{b}{key}", f"hT{k}"], writes=[f"ps{pp}"])
                            s_ = sgc % 3
                            sgc += 1
                            P.add("act", lambda e, s_=s_, pg=pg, N=N: e.activation(out=sg[s_][:, :N], in_=PS[pg][:, :N], func=AF.Silu),
                                  reads=[f"ps{pg}"], writes=[f"sg{s_}"])
                            P.add("dve", lambda e, s_=s_, pu=pu, f=f, N=N, h=h: e.tensor_tensor(
                                out=aT[:, f, h * 512:h * 512 + N], in0=sg[s_][:, :N], in1=PS[pu][:, :N], op=ALU.mult),
                                reads=[f"sg{s_}", f"ps{pu}"], writes=[f"aT{f}_{h}"])
                    units = [(m, h) for m in range(KC) for h in range(len(grp))]

                    def xm_load(u):
                        if u < len(units):
                            m, h = units[u]
                            t0, N, _ = grp[h]
                            P.add("sp", lambda e, m=m, t0=t0, N=N, u=u: e.dma_start(out=xm[u % 3][:, :N], in_=xT[m * 128:(m + 1) * 128, t0:t0 + N]),
                                  writes=[f"xm{u % 3}"], kind="d")

                    xm_load(0)
                    xm_load(1)
                    w_dn.load(P, "act", wd[0][:, :, :], mi, 0, 128, writes=["wd0"])
                    for u, (m, h) in enumerate(units):
                        t0, N, isctx = grp[h]
                        b = m % 2
                        if h == 0 and m + 1 < KC:
                            w_dn.load(P, "act", wd[(m + 1) % 2][:, :, :], mi, (m + 1) * 128, 128, writes=[f"wd{(m + 1) % 2}"])
                        xm_load(u + 2)
                        xb = u % 3
                        pd = P.ps()
                        for f in range(FC):
                            P.add("pe", lambda e, b=b, f=f, pd=pd, N=N, h=h: e.matmul(
                                PS[pd][:, :N], lhsT=wd[b][:, f, :], rhs=aT[:, f, h * 512:h * 512 + N], start=(f == 0), stop=(f == FC - 1)),
                                reads=[f"wd{b}", f"aT{f}_{h}"], writes=[f"ps{pd}"])
                        P.add("dve", lambda e, m=m, pd=pd, N=N, col=col, xb=xb: e.scalar_tensor_tensor(
                            out=xm[xb][:, :N], in0=PS[pd][:, :N], scalar=Gco[:, l, j, m, col:col + 1], in1=xm[xb][:, :N],
                            op0=ALU.mult, op1=ALU.add), reads=[f"ps{pd}", f"xm{xb}"], writes=[f"xm{xb}"])
                        P.add("sp", lambda e, m=m, t0=t0, N=N, xb=xb: e.dma_start(out=xT[m * 128:(m + 1) * 128, t0:t0 + N], in_=xm[xb][:, :N]),
                              reads=[f"xm{xb}"], writes=[f"xTs{t0}_{m}"], kind="d")
                        if nxt:
                            nxt.pop(0)()
                    for pc in nxt:
                        pc()
                run_phase(P)

        def phase_a(l, last):
            with contextlib.ExitStack() as ph:
                P = newP()
                xs = sb(ph, "xs", [128, KC, 512])
                sq = [sb(ph, f"sq{i}", [128, 512], BF16) for i in range(2)]
                rstd = sb(ph, "rstd", [128, 512])
                tmp = [sb(ph, f"tmp{i}", [128, 512]) for i in range(2)]
                hT = sb(ph, "hT", [128, KC, 512], BF16)
                NW = 4
                wi = [sb(ph, f"wi{i}", [128, KC, 256], BF16) for i in range(NW)]
                wr = [sb(ph, f"wr{i}", [128, KC, 128], BF16) for i in range(2)]
                cosb = sb(ph, "cosb", [128, 512])
                sinb = sb(ph, "sinb", [128, 512])
                r1 = [sb(ph, f"r1_{i}", [128, 512]) for i in range(2)]
                r2 = [sb(ph, f"r2_{i}", [128, 512]) for i in range(2)]
                stb = [sb(ph, f"stb{i}", [128, 512], BF16) for i in range(2)]
                stf = [sb(ph, f"stf{i}", [128, 512]) for i in range(2)]
                wcnt = 0
                rcnt = 0
                witems = []
                for (t0, N, isctx) in tiles:
                    ch = list(range(8, 20)) if (isctx and last) else list(range(28))
                    witems += [o for o in ch if o % 2 == 0]
                wstate = {"next": 0}

                def prefetch(upto):
                    while wstate["next"] <= upto and wstate["next"] < len(witems):
                        i = wstate["next"]
                        w_in.load(P, "sp", wi[i % NW][:, :, :], l, witems[i] * 128, 256, writes=[f"wi{i % NW}"])
                        wstate["next"] += 1

                for (t0, N, isctx) in tiles:
                    col = 1 if isctx else 0
                    load_x(P, xs, t0, N)
                    norm_tile(P, xs, N, l, 1, col, sq, rstd, tmp, hT)
                    if not isctx:
                        s0 = t0 - CTX
                        P.add("sp", lambda e, s0=s0, N=N: e.dma_start(out=cosb[:, :N], in_=cos_in[:, s0:s0 + N]), writes=["cosb"], kind="d")
                        P.add("sp", lambda e, s0=s0, N=N: e.dma_start(out=sinb[:, :N], in_=sin_in[:, s0:s0 + N]), writes=["sinb"], kind="d")
                    chunks = list(range(8, 20)) if (isctx and last) else list(range(28))
                    for o in chunks:
                        jj = o % 2
                        if jj == 0:
                            b = wcnt % NW
                            prefetch(wcnt + NW - 1)
                            wcnt += 1
                        wic = wi[b][:, :, jj * 128:(jj + 1) * 128]
                        if o in (10, 11):
                            pi = P.ps()
                            for sbk in range(N // 128):
                                for k in range(KC):
                                    P.add("pe", lambda e, b=b, sbk=sbk, k=k, pi=pi, jj=jj: e.matmul(
                                        PS[pi][:, sbk * 128:(sbk + 1) * 128], lhsT=hT[:, k, sbk * 128:(sbk + 1) * 128], rhs=wi[b][:, k, jj * 128:(jj + 1) * 128],
                                        start=(k == 0), stop=(k == KC - 1)), reads=[f"wi{b}", f"hT{k}"], writes=[f"ps{pi}"])
                            s = o % 2
                            P.add("act", lambda e, s=s, pi=pi, N=N: e.activation(out=stb[s][:, :N], in_=PS[pi][:, :N], func=AF.Identity),
                                  reads=[f"ps{pi}"], writes=[f"stb{s}"])
                            hv = o - 10
                            P.add("sp", lambda e, s=s, t0=t0, N=N, hv=hv: e.dma_start(
                                out=vS[t0:t0 + N, hv * 128:(hv + 1) * 128].rearrange("(b p) c -> p b c", p=128),
                                in_=stb[s][:, :N].rearrange("p (b c) -> p b c", c=128)), reads=[f"stb{s}"], writes=["vS"], kind="d")
                            continue
                        pi = P.ps()
                        for k in range(KC):
                            P.add("pe", lambda e, b=b, k=k, pi=pi, N=N, jj=jj: e.matmul(
                                PS[pi][:, :N], lhsT=wi[b][:, k, jj * 128:(jj + 1) * 128], rhs=hT[:, k, :N], start=(k == 0), stop=(k == KC - 1)),
                                reads=[f"wi{b}", f"hT{k}"], writes=[f"ps{pi}"])
                        if o < 10:
                            s = o % 2
                            dst = qT[o * 128:(o + 1) * 128, t0:t0 + N] if o < 8 else kT[(o - 8) * 128:(o - 7) * 128, t0:t0 + N]
                            dkey = "qT" if o < 8 else "kT"
                            if isctx:
                                P.add("act", lambda e, s=s, pi=pi, N=N: e.activation(out=stb[s][:, :N], in_=PS[pi][:, :N], func=AF.Identity),
                                      reads=[f"ps{pi}"], writes=[f"stb{s}"])
                            else:
                                rb = rcnt % 2
                                rcnt += 1
                                for a in range(2):
                                    P.add("pool", lambda e, b=b, rb=rb, a=a, jj=jj: e.tensor_copy(
                                        out=wr[rb][:, :, :].rearrange("p k (a q f) -> p k a q f", a=2, q=2)[:, :, :, a, :],
                                        in_=wi[b][:, :, jj * 128:(jj + 1) * 128].rearrange("p k (a q f) -> p k a q f", a=2, q=2)[:, :, :, 1 - a, :]),
                                        reads=[f"wi{b}"], writes=[f"wr{rb}"])
                                pj = P.ps()
                                for k in range(KC):
                                    P.add("pe", lambda e, rb=rb, k=k, pj=pj, N=N: e.matmul(
                                        PS[pj][:, :N], lhsT=wr[rb][:, k, :], rhs=hT[:, k, :N], start=(k == 0), stop=(k == KC - 1)),
                                        reads=[f"wr{rb}", f"hT{k}"], writes=[f"ps{pj}"])
                                P.add("dve", lambda e, s=s, pi=pi, N=N: e.tensor_tensor(out=r1[s][:, :N], in0=PS[pi][:, :N], in1=cosb[:, :N], op=ALU.mult),
                                      reads=[f"ps{pi}", "cosb"], writes=[f"r1_{s}"])
                                P.add("dve", lambda e, s=s, pj=pj, N=N: e.tensor_tensor(out=r2[s][:, :N], in0=PS[pj][:, :N], in1=sinb[:, :N], op=ALU.mult),
                                      reads=[f"ps{pj}", "sinb"], writes=[f"r2_{s}"])
                                P.add("pool", lambda e, s=s, N=N: e.tensor_tensor(out=stb[s][:, :N], in0=r1[s][:, :N], in1=r2[s][:, :N], op=ALU.add),
                                      reads=[f"r1_{s}", f"r2_{s}"], writes=[f"stb{s}"])
                            P.add("sp", lambda e, s=s, dst=dst, N=N: e.dma_start(out=dst, in_=stb[s][:, :N]), reads=[f"stb{s}"], writes=[dkey], kind="d")
                        else:
                            s = o % 2
                            if o < 20:
                                dst = uT[(o - 12) * 128:(o - 11) * 128, t0:t0 + N]
                                dkey = "uT"
                                P.add("dve", lambda e, s=s, pi=pi, N=N: e.tensor_copy(out=stf[s][:, :N], in_=PS[pi][:, :N]),
                                      reads=[f"ps{pi}"], writes=[f"stf{s}"])
                            else:
                                dst = ggT[(o - 20) * 128:(o - 19) * 128, t0:t0 + N]
                                dkey = "ggT"
                                P.add("act", lambda e, s=s, pi=pi, N=N: e.activation(out=stf[s][:, :N], in_=PS[pi][:, :N], func=AF.Gelu_apprx_tanh),
                                      reads=[f"ps{pi}"], writes=[f"stf{s}"])
                            P.add("sp", lambda e, s=s, dst=dst, N=N: e.dma_start(out=dst, in_=stf[s][:, :N]), reads=[f"stf{s}"], writes=[dkey], kind="d")
                run_phase(P)

        def phase_t(l, last):
            with contextlib.ExitStack() as ph:
                P = newP()
                cast_group(P, l + 1, "dn")
                ksb = sb(ph, "ksb", [128, 2, T], BF16)
                vsb = sb(ph, "vsb", [128, NB, 256], BF16)
                qsb = [sb(ph, f"qsb{i}", [128, 8, 128], BF16) for i in range(2)]
                rd = [sb(ph, f"rd{i}", [128, 512]) for i in range(2)]
                ya = [sb(ph, f"ya{i}", [128, 512], BF16) for i in range(2)]
                esk = sb(ph, "esk", [1, 8, 128])
                mskf = sb(ph, "mskf", [128, 2, 512])
                masks = sb(ph, "masks", [128, 2, 512], BF16)
                P.add("sp", lambda e: e.dma_start(out=mskf[:], in_=mask_in), writes=["mskf"], kind="d")
                P.add("dve", lambda e: e.tensor_copy(out=masks[:], in_=mskf[:]), reads=["mskf"], writes=["masks"])
                esk_hi = sb(ph, "esk_hi", [1, 8, 128], BF16)
                esk_lo = sb(ph, "esk_lo", [1, 8, 128], BF16)
                P.add("dve", lambda e: e.tensor_copy(out=esk[:], in_=skr[:, l * 8:(l + 1) * 8].unsqueeze(2).to_broadcast([1, 8, 128])), writes=["esk"])
                P.add("dve", lambda e: e.tensor_copy(out=esk_hi[:], in_=esk[:]), reads=["esk"], writes=["esk_hi"])
                P.add("dve", lambda e: e.tensor_tensor(out=esk[:], in0=esk[:], in1=esk_hi[:], op=ALU.subtract), reads=["esk", "esk_hi"], writes=["esk"])
                P.add("dve", lambda e: e.tensor_copy(out=esk_lo[:], in_=esk[:]), reads=["esk"], writes=["esk_lo"])
                P.add("sp", lambda e: e.dma_start(out=ksb[:], in_=kT.rearrange("(g p) t -> p g t", p=128)), writes=["ksb"], kind="d")
                P.add("sp", lambda e: e.dma_start(out=vsb[:], in_=vS.rearrange("(b p) c -> p b c", p=128)), writes=["vsb"], kind="d")
                qblocks = list(range(2, NB)) if last else list(range(NB))
                NPT = 12
                ptx = [sb(ph, f"ptx{i}", [128, 512], BF16) for i in range(NPT)]
                st = {"pc": 0, "uc": 0}
                units = []
                for qi, tb in enumerate(qblocks):
                    if tb < 2:
                        kbs = [(0, None), (1, None)]
                    else:
                        kbs = [(0, None), (1, None)]
                        if tb - 1 >= 2:
                            kbs.append((tb - 1, 0))
                        kbs.append((tb, None))
                        if tb + 1 < NB:
                            kbs.append((tb + 1, 1))
                    for g in range(2):
                        units.append({"qi": qi, "tb": tb, "g": g, "kbs": kbs, "pbs": []})

                def stage1(u):
                    qb_ = u["qi"] % 2
                    tb, g = u["tb"], u["g"]
                    if g == 0:
                        P.add("sp", lambda e: e.dma_start(
                            out=qsb[qb_][:], in_=qT[:, tb * 128:(tb + 1) * 128].rearrange("(h p) n -> p h n", p=128)),
                            writes=[f"qsb{qb_}"], kind="d")
                    for ki, (kb, mk) in enumerate(u["kbs"]):
                        pss = ki
                        P.add("pe", lambda e, kb=kb, pss=pss: e.matmul(
                            PS[pss][:, :], lhsT=ksb[:, g, kb * 128:(kb + 1) * 128],
                            rhs=qsb[qb_][:, g * 4:(g + 1) * 4, :].rearrange("p h n -> p (h n)"), start=True, stop=True),
                            reads=["ksb", f"qsb{qb_}"], writes=[f"ps{pss}"])
                        pb = st["pc"] % NPT
                        st["pc"] += 1
                        u["pbs"].append(pb)
                        P.add("act", lambda e, pb=pb, pss=pss: e.activation(out=ptx[pb][:], in_=PS[pss][:, :], func=AF.Exp, scale=float(128 ** -0.5)),
                              reads=[f"ps{pss}"], writes=[f"pt{pb}"])
                        if mk is not None:
                            P.add("dve", lambda e, pb=pb, mk=mk: e.tensor_tensor(out=ptx[pb][:], in0=ptx[pb][:], in1=masks[:, mk, :], op=ALU.mult),
                                  reads=[f"pt{pb}", "masks"], writes=[f"pt{pb}"])

                def stage2(u):
                    tb, g = u["tb"], u["g"]
                    po, pd = 5, 6
                    nk = len(u["kbs"])
                    for ki, (kb, mk) in enumerate(u["kbs"]):
                        pb = u["pbs"][ki]
                        P.add("pe", lambda e, kb=kb, pb=pb, ki=ki: e.matmul(
                            PS[po][:, :], lhsT=vsb[:, kb, g * 128:(g + 1) * 128], rhs=ptx[pb][:], start=(ki == 0), stop=(ki == nk - 1)),
                            reads=["vsb", f"pt{pb}"], writes=[f"ps{po}"])
                        P.add("pe", lambda e, pb=pb, ki=ki: e.matmul(
                            PS[pd][:, :], lhsT=ones_bf[:], rhs=ptx[pb][:], start=(ki == 0), stop=False),
                            reads=[f"pt{pb}", "ones_bf"], writes=[f"ps{pd}"])
                    hs = g * 4
                    P.add("pe", lambda e: e.matmul(
                        PS[pd][:, :], lhsT=ones_bf[0:1, :], rhs=esk_hi[0:1, hs:hs + 4, :].rearrange("p h n -> p (h n)"), start=False, stop=False),
                        reads=["esk_hi", "ones_bf"], writes=[f"ps{pd}"])
                    P.add("pe", lambda e: e.matmul(
                        PS[pd][:, :], lhsT=ones_bf[0:1, :], rhs=esk_lo[0:1, hs:hs + 4, :].rearrange("p h n -> p (h n)"), start=False, stop=True),
                        reads=["esk_lo", "ones_bf"], writes=[f"ps{pd}"])
                    ub = st["uc"] % 2
                    st["uc"] += 1
                    P.add("dve", lambda e: e.reciprocal(out=rd[ub][:], in_=PS[pd][:, :]), reads=[f"ps{pd}"], writes=[f"rd{ub}"])
                    P.add("dve", lambda e: e.tensor_tensor(out=ya[ub][:], in0=PS[po][:, :], in1=rd[ub][:], op=ALU.mult),
                          reads=[f"ps{po}", f"rd{ub}"], writes=[f"ya{ub}"])
                    P.add("sp", lambda e: e.dma_start(
                        out=yaT[g * 512:(g + 1) * 512, tb * 128:(tb + 1) * 128].rearrange("(h p) n -> p h n", p=128),
                        in_=ya[ub][:].rearrange("p (h n) -> p h n", n=128)), reads=[f"ya{ub}"], writes=["yaT"], kind="d")

                for i, u in enumerate(units):
                    stage1(u)
                    if i > 0:
                        stage2(units[i - 1])
                stage2(units[-1])
                run_phase(P)

        def phase_r(l, last):
            with contextlib.ExitStack() as ph:
                P = newP()
                names = ["ur", "u", "rr0", "rr1", "ii0", "ii1", "ww", "hf", "hb", "gg"]
                B = {n: sb(ph, "R" + n, [128, T]) for n in names}
                yo = sb(ph, "Ryo", [128, T], BF16)
                rgt = sb(ph, "rgt", [128, 32, 128])
                cast_group(P, l + 1, "mix")
                rg_w.load(P, "sp", rgt[:, :, :], l, 0, 128, writes=["rgt"])
                segs = [(0, CTX), (CTX, T)]
                ttiles = [(t0, N) for (t0, N, _) in tiles]
                for cb in range(8):
                    P.add("sp", lambda e, cb=cb: e.dma_start(out=B["ur"][:], in_=uT[cb * 128:(cb + 1) * 128, :]), reads=["uT"], writes=["ur"], kind="d")
                    P.add("sp", lambda e, cb=cb: e.dma_start(out=B["gg"][:], in_=ggT[cb * 128:(cb + 1) * 128, :]), reads=["ggT"], writes=["gg"], kind="d")
                    cw = lambda tap, cb=cb: vT[:, voff(l, "conv_w") + tap * 8 + cb: voff(l, "conv_w") + tap * 8 + cb + 1]
                    cbias = vT[:, voff(l, "conv_b") + cb: voff(l, "conv_b") + cb + 1]
                    P.add("dve", lambda e, cw=cw, cbias=cbias: e.tensor_scalar(out=B["u"][:], in0=B["ur"][:], scalar1=cw(2), scalar2=cbias, op0=ALU.mult, op1=ALU.add),
                          reads=["ur", "vT"], writes=["u"])
                    for (a0, a1) in segs:
                        for tap, off in ((0, -2), (1, -1), (3, 1)):
                            lo = max(a0, a0 - off)
                            hi = min(a1, a1 - off)
                            P.add("dve", lambda e, cw=cw, tap=tap, off=off, lo=lo, hi=hi: e.scalar_tensor_tensor(
                                out=B["u"][:, lo:hi], in0=B["ur"][:, lo + off:hi + off], scalar=cw(tap), in1=B["u"][:, lo:hi],
                                op0=ALU.mult, op1=ALU.add), reads=["ur", "u", "vT"], writes=["u"])
                    for d in range(2):
                        hbuf = B["hf"] if d == 0 else B["hb"]
                        hkey = "hf" if d == 0 else "hb"
                        RR, II = f"rr{d}", f"ii{d}"
                        for gi_, (gname, gkey) in enumerate(((RR, RR), (II, II))):
                            mm = (d * 2 + gi_) * 8 + cb
                            bcol = voff(l, "rg_b") + (d * 2 + gi_) * 8 + cb
                            for (t0, N) in ttiles:
                                pi = P.ps()
                                P.add("pe", lambda e, mm=mm, t0=t0, N=N, pi=pi: e.matmul(
                                    PS[pi][:, :N], lhsT=rgt[:, mm, :], rhs=B["u"][:, t0:t0 + N], start=True, stop=True),
                                    reads=["rgt", "u"], writes=[f"ps{pi}"])
                                P.add("act", lambda e, gname=gname, t0=t0, N=N, pi=pi, bcol=bcol: e.activation(
                                    out=B[gname][:, t0:t0 + N], in_=PS[pi][:, :N], func=AF.Sigmoid, bias=vT[:, bcol:bcol + 1], scale=1.0),
                                    reads=[f"ps{pi}", "vT"], writes=[gkey])
                    for d in range(2):
                        hbuf = B["hf"] if d == 0 else B["hb"]
                        hkey = "hf" if d == 0 else "hb"
                        RR, II = f"rr{d}", f"ii{d}"
                        cl = clam[:, l, d * 8 + cb: d * 8 + cb + 1]
                        P.add("act", lambda e, cl=cl, RR=RR: e.activation(out=B[RR][:], in_=B[RR][:], func=AF.Exp, scale=cl), reads=[RR, "clam"], writes=[RR])
                        P.add("dve", lambda e, RR=RR: e.scalar_tensor_tensor(out=B["ww"][:], in0=B[RR][:], scalar=-1.0, in1=B[RR][:], op0=ALU.mult, op1=ALU.mult),
                              reads=[RR], writes=["ww"])
                        P.add("dve", lambda e: e.tensor_scalar(out=B["ww"][:], in0=B["ww"][:], scalar1=1.0, scalar2=0.0, op0=ALU.add, op1=ALU.max),
                              reads=["ww"], writes=["ww"])
                        P.add("act", lambda e: e.activation(out=B["ww"][:], in_=B["ww"][:], func=AF.Sqrt), reads=["ww"], writes=["ww"])
                        P.add("dve", lambda e, II=II: e.tensor_tensor(out=B[II][:], in0=B[II][:], in1=B["u"][:], op=ALU.mult), reads=[II, "u"], writes=[II])
                        P.add("dve", lambda e, II=II: e.tensor_tensor(out=B["ww"][:], in0=B["ww"][:], in1=B[II][:], op=ALU.mult), reads=["ww", II], writes=["ww"])
                        if d == 0:
                            P.add("dve", lambda e, hbuf=hbuf, RR=RR: e.tensor_tensor_scan(
                                out=hbuf[:, 0:CTX], data0=B[RR][:, 0:CTX], data1=B["ww"][:, 0:CTX], initial=0.0, op0=ALU.mult, op1=ALU.add),
                                reads=[RR, "ww"], writes=[hkey])
                            P.add("dve", lambda e, hbuf=hbuf, RR=RR: e.tensor_tensor_scan(
                                out=hbuf[:, CTX:T], data0=B[RR][:, CTX:T], data1=B["ww"][:, CTX:T], initial=hbuf[:, CTX - 1:CTX], op0=ALU.mult, op1=ALU.add),
                                reads=[RR, "ww", hkey], writes=[hkey])
                        else:
                            P.add("dve", lambda e, hbuf=hbuf, RR=RR: e.tensor_tensor_scan(
                                out=hbuf[:, 0:CTX][:, ::-1], data0=B[RR][:, 0:CTX][:, ::-1], data1=B["ww"][:, 0:CTX][:, ::-1], initial=0.0,
                                op0=ALU.mult, op1=ALU.add), reads=[RR, "ww"], writes=[hkey])
                            P.add("dve", lambda e, hbuf=hbuf, RR=RR: e.tensor_tensor_scan(
                                out=hbuf[:, CTX:T][:, ::-1], data0=B[RR][:, CTX:T][:, ::-1], data1=B["ww"][:, CTX:T][:, ::-1], initial=hbuf[:, 0:1],
                                op0=ALU.mult, op1=ALU.add), reads=[RR, "ww", hkey], writes=[hkey])
                    lo = CTX if last else 0
                    P.add("pool", lambda e, lo=lo: e.tensor_tensor(out=B["hf"][:, lo:T], in0=B["hf"][:, lo:T], in1=B["hb"][:, lo:T], op=ALU.add),
                          reads=["hf", "hb"], writes=["hf"])
                    P.add("dve", lambda e, lo=lo: e.tensor_tensor(out=yo[:, lo:T], in0=B["hf"][:, lo:T], in1=B["gg"][:, lo:T], op=ALU.mult),
                          reads=["hf", "gg"], writes=["yo"])
                    P.add("sp", lambda e, cb=cb, lo=lo: e.dma_start(out=yrT[cb * 128:(cb + 1) * 128, lo:T], in_=yo[:, lo:T]), reads=["yo"], writes=["yrT"], kind="d")
                run_phase(P)

        def phase_c(l, last):
            with contextlib.ExitStack() as ph:
                P = newP()
                xs = sb(ph, "xs", [128, KC, 512])
                sq = [sb(ph, f"sq{i}", [128, 512], BF16) for i in range(2)]
                rstd = sb(ph, "rstd", [128, 512])
                tmp = [sb(ph, f"tmp{i}", [128, 512]) for i in range(2)]
                hT = sb(ph, "hT", [128, KC, 512], BF16)
                zT = sb(ph, "zT", [128, KC, 512], BF16)
                yas = sb(ph, "yas", [128, 8, 512], BF16)
                yrs = sb(ph, "yrs", [128, 8, 512], BF16)
                wga = [sb(ph, f"wga{i}", [128, KC, 256], BF16) for i in range(2)]
                wgb = [sb(ph, f"wgb{i}", [128, KC, 256], BF16) for i in range(2)]
                woa = [sb(ph, f"woa{i}", [128, 8, 256], BF16) for i in range(2)]
                wor = [sb(ph, f"wor{i}", [128, 8, 256], BF16) for i in range(2)]
                wo = [sb(ph, f"wo{i}", [128, KC, 256], BF16) for i in range(2)]
                s1 = [sb(ph, f"s1_{i}", [128, 512]) for i in range(2)]
                s2 = [sb(ph, f"s2_{i}", [128, 512]) for i in range(2)]
                t1 = [sb(ph, f"t1_{i}", [128, 512]) for i in range(2)]
                t2 = [sb(ph, f"t2_{i}", [128, 512]) for i in range(2)]
                tl = [t for t in tiles if not (last and t[2])]
                for (t0, N, isctx) in tl:
                    col = 1 if isctx else 0
                    load_x(P, xs, t0, N)
                    P.add("sp", lambda e, t0=t0, N=N: e.dma_start(out=yas[:, :, :N], in_=yaT[:, t0:t0 + N].rearrange("(k p) n -> p k n", p=128)),
                          reads=["yaT"], writes=["yas"], kind="d")
                    P.add("sp", lambda e, t0=t0, N=N: e.dma_start(out=yrs[:, :, :N], in_=yrT[:, t0:t0 + N].rearrange("(k p) n -> p k n", p=128)),
                          reads=["yrT"], writes=["yrs"], kind="d")
                    norm_tile(P, xs, N, l, 1, col, sq, rstd, tmp, hT)
                    for o in range(KC):
                        b = (o // 2) % 2
                        jj = o % 2
                        cs = slice(jj * 128, (jj + 1) * 128)
                        if jj == 0:
                            w_in.load(P, "sp", wga[b][:, :, :], l, (28 + o) * 128, 256, writes=[f"wga{b}"])
                            w_in.load(P, "sp", wgb[b][:, :, :], l, (44 + o) * 128, 256, writes=[f"wgb{b}"])
                            w_oa.load(P, "sp", woa[b][:, :, :], l, o * 128, 256, writes=[f"woa{b}"])
                            w_or.load(P, "sp", wor[b][:, :, :], l, o * 128, 256, writes=[f"wor{b}"])
                        p1, p2, p3, p4 = P.ps(), P.ps(), P.ps(), P.ps()
                        for k in range(KC):
                            P.add("pe", lambda e, b=b, k=k, p1=p1, N=N, cs=cs: e.matmul(PS[p1][:, :N], lhsT=wga[b][:, k, cs], rhs=hT[:, k, :N], start=(k == 0), stop=(k == KC - 1)),
                                  reads=[f"wga{b}", f"hT{k}"], writes=[f"ps{p1}"])
                        for k in range(KC):
                            P.add("pe", lambda e, b=b, k=k, p2=p2, N=N, cs=cs: e.matmul(PS[p2][:, :N], lhsT=wgb[b][:, k, cs], rhs=hT[:, k, :N], start=(k == 0), stop=(k == KC - 1)),
                                  reads=[f"wgb{b}", f"hT{k}"], writes=[f"ps{p2}"])
                        for k in range(8):
                            P.add("pe", lambda e, b=b, k=k, p3=p3, N=N, cs=cs: e.matmul(PS[p3][:, :N], lhsT=woa[b][:, k, cs], rhs=yas[:, k, :N], start=(k == 0), stop=(k == 7)),
                                  reads=[f"woa{b}", "yas"], writes=[f"ps{p3}"])
                        for k in range(8):
                            P.add("pe", lambda e, b=b, k=k, p4=p4, N=N, cs=cs: e.matmul(PS[p4][:, :N], lhsT=wor[b][:, k, cs], rhs=yrs[:, k, :N], start=(k == 0), stop=(k == 7)),
                                  reads=[f"wor{b}", "yrs"], writes=[f"ps{p4}"])
                        P.add("act", lambda e, b=b, p1=p1, N=N: e.activation(out=s1[b][:, :N], in_=PS[p1][:, :N], func=AF.Sigmoid), reads=[f"ps{p1}"], writes=[f"s1_{b}"])
                        P.add("act", lambda e, b=b, p2=p2, N=N: e.activation(out=s2[b][:, :N], in_=PS[p2][:, :N], func=AF.Sigmoid), reads=[f"ps{p2}"], writes=[f"s2_{b}"])
                        P.add("dve", lambda e, b=b, p3=p3, N=N: e.tensor_tensor(out=t1[b][:, :N], in0=s1[b][:, :N], in1=PS[p3][:, :N], op=ALU.mult),
                              reads=[f"s1_{b}", f"ps{p3}"], writes=[f"t1_{b}"])
                        P.add("dve", lambda e, b=b, p4=p4, N=N: e.tensor_tensor(out=t2[b][:, :N], in0=s2[b][:, :N], in1=PS[p4][:, :N], op=ALU.mult),
                              reads=[f"s2_{b}", f"ps{p4}"], writes=[f"t2_{b}"])
                        P.add("pool", lambda e, b=b, o=o, N=N: e.tensor_tensor(out=zT[:, o, :N], in0=t1[b][:, :N], in1=t2[b][:, :N], op=ALU.add),
                              reads=[f"t1_{b}", f"t2_{b}"], writes=[f"zT{o}"])
                    for m in range(KC):
                        b = (m // 2) % 2
                        jj = m % 2
                        cs = slice(jj * 128, (jj + 1) * 128)
                        if jj == 0:
                            w_out.load(P, "sp", wo[b][:, :, :], l, m * 128, 256, writes=[f"wo{b}"])
                        pd = P.ps()
                        for k in range(KC):
                            P.add("pe", lambda e, b=b, k=k, pd=pd, N=N, cs=cs: e.matmul(PS[pd][:, :N], lhsT=wo[b][:, k, cs], rhs=zT[:, k, :N], start=(k == 0), stop=(k == KC - 1)),
                                  reads=[f"wo{b}", f"zT{k}"], writes=[f"ps{pd}"])
                        P.add("dve", lambda e, m=m, pd=pd, N=N, col=col: e.scalar_tensor_tensor(
                            out=xs[:, m, :N], in0=PS[pd][:, :N], scalar=Gco[:, l, 1, m, col:col + 1], in1=xs[:, m, :N],
                            op0=ALU.mult, op1=ALU.add), reads=[f"ps{pd}", "xs"], writes=["xs"])
                    store_x(P, xs, t0, N)
                run_phase(P)

        def phase_final():
            with contextlib.ExitStack() as ph:
                P = newP()
                xs = sb(ph, "xs", [128, KC, 512])
                sq = [sb(ph, f"sq{i}", [128, 512], BF16) for i in range(2)]
                rstd = sb(ph, "rstd", [128, 512])
                ot = [sb(ph, f"ot{i}", [128, D]) for i in range(2)]
                oc = 0
                for (t0, N, isctx) in tiles:
                    if isctx:
                        continue
                    load_x(P, xs, t0, N)
                    rstd_tile(P, xs, N, sq, rstd)
                    for k in range(KC):
                        P.add("dve", lambda e, k=k, N=N: e.scalar_tensor_tensor(
                            out=xs[:, k, :N], in0=xs[:, k, :N], scalar=vT[:, 32 + k:33 + k], in1=rstd[:, :N], op0=ALU.mult, op1=ALU.mult),
                            reads=["xs", "rstd", "vT"], writes=["xs"])
                    for bb in range(N // 128):
                        ob = oc % 2
                        oc += 1
                        for kq in range(4):
                            pi = P.ps()
                            for kk in range(4):
                                k = kq * 4 + kk
                                P.add("pe", lambda e, k=k, kk=kk, bb=bb, pi=pi: e.transpose(
                                    out=PS[pi][:, kk * 128:(kk + 1) * 128], in_=xs[:, k, bb * 128:(bb + 1) * 128], identity=ident[:]),
                                    reads=["xs", "ident"], writes=[f"ps{pi}"])
                            if kq % 2:
                                P.add("act", lambda e, ob=ob, kq=kq, pi=pi: e.activation(out=ot[ob][:, kq * 512:(kq + 1) * 512], in_=PS[pi][:, :], func=AF.Identity),
                                      reads=[f"ps{pi}"], writes=[f"ot{ob}"])
                            else:
                                P.add("dve", lambda e, ob=ob, kq=kq, pi=pi: e.tensor_copy(out=ot[ob][:, kq * 512:(kq + 1) * 512], in_=PS[pi][:, :]),
                                      reads=[f"ps{pi}"], writes=[f"ot{ob}"])
                        r0 = t0 - CTX + bb * 128
                        P.add("sp", lambda e, ob=ob, r0=r0: e.dma_start(out=out[r0:r0 + 128, :], in_=ot[ob][:]), reads=[f"ot{ob}"], writes=["out"], kind="d")
                run_phase(P)

        for l in range(L):
            last = (l == L - 1)
            phase_ffn(l, 0, tiles)
            phase_a(l, last)
            phase_t(l, last)
            phase_r(l, last)
            phase_c(l, last)
            phase_ffn(l, 1, [t for t in tiles if not (last and t[2])])
        phase_final()
    return nc


def _consts(S):
    ident = np.eye(128, dtype=np.float32)
    rows = S // 64
    pos = np.stack([np.repeat(np.arange(rows, dtype=np.float32), 64), np.tile(np.arange(64, dtype=np.float32), rows)], 0)
    inv = (10000.0 ** (-np.arange(32, dtype=np.float32) / 32)).astype(np.float32)
    cos = np.zeros((128, S), np.float32)
    sin = np.zeros((128, S), np.float32)
    for a in range(2):
        ang = (pos[a][None, :] * inv[:, None]).astype(np.float32)
        for q in range(2):
            sl = slice(a * 64 + q * 32, a * 64 + q * 32 + 32)
            cos[sl] = np.cos(ang)
            sin[sl] = np.sin(ang) * (-1.0 if q == 0 else 1.0)
    j = np.arange(128)[:, None]
    i = np.arange(128)[None, :]
    mp = (j >= i).astype(np.float32)
    mn = (j <= i).astype(np.float32)
    masks = np.stack([np.tile(mp, (1, 4)), np.tile(mn, (1, 4))], 1).astype(np.float32)
    return ident, cos, sin, masks


def make_in_maps(inp, S, L, NR, cores):
    ident, cos, sin, masks = _consts(S)
    f = lambda a: np.ascontiguousarray(np.asarray(a, dtype=np.float32))
    w2d = {
        "w_ada": f(inp["w_ada"][:L]).reshape(L * D, 18432),
        "w_up": f(inp["w_ffn_up"][:L]).reshape(L * 2 * D, 2 * FF),
        "w_dn": f(inp["w_ffn_down"][:L]).reshape(L * 2 * FF, D),
        "w_in": f(inp["w_in"][:L]).reshape(L * D, INW),
        "w_oa": f(inp["w_o_attn"][:L]).reshape(L * 1024, D),
        "w_or": f(inp["w_o_rnn"][:L]).reshape(L * 1024, D),
        "w_out": f(inp["w_out"][:L]).reshape(L * D, D),
        "rg_w": f(inp["rg_w"][:L]).reshape(L * 32 * 128, 128),
    }
    maps = []
    for ci, b in enumerate(cores):
        rows = [f(inp["c"][b]).reshape(16, 128), f(inp["c_ctx"]).reshape(16, 128), f(inp["final_g"]).reshape(16, 128)]
        for l in range(L):
            rows += [f(inp["norm_g"][l]).reshape(48, 128), f(inp["b_ada"][l]).reshape(144, 128), f(inp["conv_w"][l]).reshape(32, 128),
                     f(inp["conv_b"][l]).reshape(8, 128), f(inp["rg_b"][l]).reshape(32, 128), f(inp["rg_lambda"][l]).reshape(16, 128)]
        m = {
            "x_b": f(inp["x"][b][:S]), "ctx_b": f(inp["ctx"][b]), "vecs": np.concatenate(rows, 0),
            "sink": f(inp["attn_sink"][:L]).reshape(1, L * 8), "ident": ident, "rope_cos": cos, "rope_sin": sin, "masks": masks,
        }
        for k, w in w2d.items():
            m[k + "_s"] = w
        maps.append(m)
    return maps


_NC_CACHE = {}


def kernel(**inputs):
    S, L, NR = 4096, 4, 8
    key = (S, L, NR)
    if key not in _NC_CACHE:
        _NC_CACHE[key] = build(S, L, NR)
    nc = _NC_CACHE[key]
    maps = make_in_maps(inputs, S, L, NR, list(range(8)))
    res = run_bass_kernel_spmd(nc, maps, core_ids=list(range(8)))
    return np.stack([np.asarray(r["out"], dtype=np.float32) for r in res.results], 0)
```

```python
import contextlib
import numpy as np
import concourse.bass as bass
import concourse.mybir as mybir
from concourse.bass_utils import run_bass_kernel_spmd

F32 = mybir.dt.float32
BF16 = mybir.dt.bfloat16
AF = mybir.ActivationFunctionType
ALU = mybir.AluOpType

D = 2048
KC = 16
FF = 5504
FC = 43
CTX = 256
INW = 7680
EPS = 1e-6
NQH = 8
ENGS = ("pe", "act", "dve", "pool", "sp")


class _Op:
    __slots__ = ("eng", "fn", "deps", "kind", "sig", "needed", "dsem")

    def __init__(self, eng, fn, kind):
        self.eng = eng
        self.fn = fn
        self.kind = kind
        self.deps = []
        self.sig = None
        self.needed = False
        self.dsem = None


class Prog:
    NDSEM = 8

    def __init__(self, sems, base):
        self.sems = sems
        self.base = dict(base)
        self.ops = {e: [] for e in ENGS}
        self.last_w = {}
        self.readers = {}
        self.dma_rr = {e: 0 for e in ENGS}
        self.dma_last = {}
        self.psi = 0

    def ps(self):
        i = self.psi % 8
        self.psi += 1
        return i

    def add(self, eng, fn, reads=(), writes=(), kind="c"):
        op = _Op(eng, fn, kind)
        deps = []
        for k in reads:
            w = self.last_w.get(k)
            if w is not None:
                deps.append((w, True))
        for k in writes:
            w = self.last_w.get(k)
            if w is not None:
                deps.append((w, False))
            for r in self.readers.get(k, ()):
                deps.append((r, False))
        if kind == "d":
            i = self.dma_rr[eng] % self.NDSEM
            self.dma_rr[eng] += 1
            op.dsem = ("dma", eng, i)
            prev = self.dma_last.get(op.dsem)
            if prev is not None:
                deps.append((prev, True))
            self.dma_last[op.dsem] = op
        elif kind == "cc":
            op.dsem = "cc"
        seen = set()
        for d, raw in deps:
            if d is op or id(d) in seen:
                continue
            if d.kind == "c" and kind == "c" and d.eng == eng and (eng == "pe" or not raw):
                continue
            seen.add(id(d))
            op.deps.append(d)
            d.needed = True
        for k in writes:
            self.last_w[k] = op
            self.readers[k] = []
        for k in reads:
            self.readers.setdefault(k, []).append(op)
        self.ops[eng].append(op)
        return op

    def finalize(self):
        cnt = dict(self.base)
        for e in ENGS:
            last = None
            for op in self.ops[e]:
                if op.kind == "c":
                    last = op
            if last is not None:
                last.needed = True
            for op in self.ops[e]:
                if op.kind == "d":
                    cnt[op.dsem] += 16
                    op.sig = (op.dsem, cnt[op.dsem])
                elif op.kind == "cc":
                    cnt["cc"] += 1
                    op.sig = ("cc", cnt["cc"])
                elif op.needed:
                    cnt[e] += 1
                    op.sig = (e, cnt[e])
        self.final = cnt

    def emit(self, eng, eo):
        known = {}
        for op in self.ops[eng]:
            for d in op.deps:
                s, v = d.sig
                if known.get(s, -1) < v:
                    eo.wait_ge(self.sems[s], v)
                    known[s] = v
            ins = op.fn(eo)
            if op.kind == "d":
                ins.then_inc(self.sems[op.dsem], 16)
            elif op.kind == "cc":
                ins.then_inc(self.sems["cc"], 1)
            elif op.needed:
                ins.then_inc(self.sems[eng], 1)
        for s, v in self.final.items():
            if v > self.base[s]:
                eo.wait_ge(self.sems[s], v)


class TW:
    def __init__(self, nc, name, nmat, rows, cols, dt):
        self.name = name
        self.nmat = nmat
        self.rows = rows
        self.cols = cols
        self.dt = dt
        self.src = nc.dram_tensor(name + "_s", [nmat * rows, cols], F32, kind="ExternalInput").ap()
        self.nat = [nc.dram_tensor(f"{name}_n{m}", [rows, cols], dt, kind="Internal").ap() for m in range(nmat)]

    def prologue(self, P, m):
        RB = 512
        for r0 in range(0, self.rows, RB):
            r1 = min(self.rows, r0 + RB)
            P.add("pool", lambda e, r0=r0, r1=r1: e.dma_start(
                out=self.nat[m][r0:r1, :], in_=self.src[m * self.rows + r0:m * self.rows + r1, :]),
                writes=[f"{self.name}_n{m}_{r0}"], kind="d")

    def keys(self, m):
        return [f"{self.name}_n{m}_{r0}" for r0 in range(0, self.rows, 512)]

    def load(self, P, eng, dst, m, c0, ncols, writes, dep=False):
        P.add(eng, lambda e: e.dma_start(out=dst, in_=self.nat[m][:, c0:c0 + ncols].rearrange("(k p) c -> p k c", p=128)),
              reads=(self.keys(m) if dep else []), writes=list(writes), kind="d")


def build(S, L, NR):
    T = CTX + S
    NB = T // 128
    tiles = [(0, CTX, True)] + [(CTX + 512 * i, 512, False) for i in range(S // 512)]
    NROW = 48 + 280 * L
    nc = bass.Bass("TRN2", target_bir_lowering=False)

    def din(name, shape, dt=F32):
        return nc.dram_tensor(name, list(shape), dt, kind="ExternalInput").ap()

    def dscr(name, shape, dt):
        return nc.dram_tensor(name, list(shape), dt, kind="Internal").ap()

    x_in = din("x_b", [S, D])
    ctx_in = din("ctx_b", [CTX, D])
    vecs_in = din("vecs", [NROW, 128])
    sink_in = din("sink", [1, L * 8])
    ident_in = din("ident", [128, 128])
    cos_in = din("rope_cos", [128, S])
    sin_in = din("rope_sin", [128, S])
    mask_in = din("masks", [128, 2, 512])
    out = nc.dram_tensor("out", [S, D], F32, kind="ExternalOutput").ap()

    w_ada = TW(nc, "w_ada", L, D, 18432, BF16)
    w_up = TW(nc, "w_up", L * 2, D, 2 * FF, BF16)
    w_dn = TW(nc, "w_dn", L * 2, FF, D, BF16)
    w_in = TW(nc, "w_in", L, D, INW, BF16)
    w_oa = TW(nc, "w_oa", L, 1024, D, BF16)
    w_or = TW(nc, "w_or", L, 1024, D, BF16)
    w_out = TW(nc, "w_out", L, D, D, BF16)
    rg_w = TW(nc, "rg_w", L, 32 * 128, 128, F32)
    allw = [w_ada, w_up, w_dn, w_in, w_oa, w_or, w_out, rg_w]

    xT = dscr("xT", [D, T], F32)
    qT = dscr("qT", [1024, T], BF16)
    kT = dscr("kT", [256, T], BF16)
    vS = dscr("vS", [T, 256], BF16)
    uT = dscr("uT", [1024, T], F32)
    ggT = dscr("ggT", [1024, T], F32)
    yaT = dscr("yaT", [1024, T], BF16)
    yrT = dscr("yrT", [1024, T], BF16)

    es = contextlib.ExitStack()
    with es:
        semnames = list(ENGS) + ["cc"] + [("dma", e, i) for e in ENGS for i in range(Prog.NDSEM)]
        sems = {}
        for k in semnames:
            nm = k if isinstance(k, str) else f"d_{k[1]}_{k[2]}"
            sems[k] = es.enter_context(nc.semaphore("s_" + nm))
        state = {"base": {k: 0 for k in semnames}}

        def sb(stack, name, shape, dt=F32):
            state["nid"] = state.get("nid", 0) + 1
            return stack.enter_context(nc.sbuf_tensor(f"sb{state['nid']}_{name}", list(shape), dt))

        PS = [es.enter_context(nc.psum_tensor(f"psb{i}", [128, 512], F32)) for i in range(8)]

        ident = sb(es, "ident", [128, 128])
        ones_bf = sb(es, "ones_bf", [128, 128], BF16)
        cst = sb(es, "cst", [128, 4])
        vT = sb(es, "vecsT", [128, NROW])
        mod = sb(es, "mod", [128, L, 144, 2])
        Aco = sb(es, "Aco", [128, L, 3, 16, 2])
        Gco = sb(es, "Gco", [128, L, 3, 16, 2])
        clam = sb(es, "clam", [128, L, 16])
        skr = sb(es, "skr", [1, L * 8])
        scT = sb(es, "scT", [128, 16, 2], BF16)

        def run_phase(P):
            P.finalize()
            with nc.Block() as blk:
                @blk.tensor
                def _(e):
                    P.emit("pe", e)

                @blk.scalar
                def _(e):
                    P.emit("act", e)

                @blk.vector
                def _(e):
                    P.emit("dve", e)

                @blk.gpsimd
                def _(e):
                    P.emit("pool", e)

                @blk.sync
                def _(e):
                    P.emit("sp", e)
            state["base"] = dict(P.final)

        def newP():
            return Prog(sems, state["base"])

        def voff(l, what):
            return 48 + l * 280 + {"norm_g": 0, "b_ada": 48, "conv_w": 192, "conv_b": 224, "rg_b": 232, "rg_lam": 264}[what]

        def load_x(P, xs, t0, N, tag="xs"):
            P.add("sp", lambda e: e.dma_start(out=xs[:, :, :N], in_=xT[:, t0:t0 + N].rearrange("(k p) n -> p k n", p=128)),
                  reads=["xT"], writes=[tag], kind="d")

        def store_x(P, xs, t0, N, tag="xs"):
            P.add("sp", lambda e: e.dma_start(out=xT[:, t0:t0 + N].rearrange("(k p) n -> p k n", p=128), in_=xs[:, :, :N]),
                  reads=[tag], writes=["xT"], kind="d")

        def rstd_tile(P, xs, N, sq, rstd, tag="xs"):
            pi = P.ps()
            for k in range(KC):
                P.add("act", lambda e, k=k: e.activation(out=sq[k % 2][:, :N], in_=xs[:, k, :N], func=AF.Square),
                      reads=[tag], writes=[f"sq{k % 2}"])
                P.add("pe", lambda e, k=k: e.matmul(PS[pi][:, :N], lhsT=ones_bf[:], rhs=sq[k % 2][:, :N], start=(k == 0), stop=(k == KC - 1)),
                      reads=[f"sq{k % 2}"], writes=[f"ps{pi}"])
            P.add("act", lambda e: e.activation(out=rstd[:, :N], in_=PS[pi][:, :N], func=AF.Sqrt, scale=1.0 / D, bias=cst[:, 0:1]),
                  reads=[f"ps{pi}"], writes=["rstd"])
            P.add("dve", lambda e: e.reciprocal(out=rstd[:, :N], in_=rstd[:, :N]), reads=["rstd"], writes=["rstd"])

        def norm_tile(P, xs, N, l, j, col, sq, rstd, tmp, hT, tkey="tmp"):
            rstd_tile(P, xs, N, sq, rstd)
            for k in range(KC):
                P.add("dve", lambda e, k=k: e.scalar_tensor_tensor(
                    out=tmp[k % 2][:, :N], in0=xs[:, k, :N], scalar=Aco[:, l, j, k, col:col + 1], in1=rstd[:, :N],
                    op0=ALU.mult, op1=ALU.mult), reads=["xs", "rstd"], writes=[f"{tkey}{k % 2}"])
                P.add("act", lambda e, k=k: e.activation(
                    out=hT[:, k, :N], in_=tmp[k % 2][:, :N], func=AF.Identity,
                    bias=mod[:, l, (3 * j) * 16 + k, col:col + 1], scale=1.0), reads=[f"{tkey}{k % 2}"], writes=[f"hT{k}"])

        HT_KEYS = [f"hT{k}" for k in range(KC)]

        def cast_group(P, l, which):
            if l >= L:
                return
            if which == "up":
                w_up.prologue(P, 2 * l)
                w_up.prologue(P, 2 * l + 1)
            elif which == "dn":
                w_dn.prologue(P, 2 * l)
                w_dn.prologue(P, 2 * l + 1)
            else:
                for w in (w_in, rg_w, w_oa, w_or, w_out):
                    w.prologue(P, l)

        with contextlib.ExitStack() as ph:
            P = newP()
            rows = sb(ph, "rows", [128, 2, 128])
            lamw = sb(ph, "lamw", [128, 8, 16])
            wadat = [sb(ph, f"wadat{i}", [128, 16, 1024], BF16) for i in range(2)]
            xin = sb(ph, "xin", [128, 4, D])
            xs0 = sb(ph, "xs0", [128, KC, 512])

            for m in range(L):
                w_ada.prologue(P, m)
            cast_group(P, 0, "up")
            cast_group(P, 0, "dn")
            cast_group(P, 0, "mix")

            P.add("sp", lambda e: e.dma_start(out=ident[:], in_=ident_in), writes=["ident"], kind="d")
            P.add("sp", lambda e: e.dma_start(out=skr[:], in_=sink_in), writes=["skr"], kind="d")
            P.add("dve", lambda e: e.memset(ones_bf[:], 1.0), writes=["ones_bf"])
            P.add("dve", lambda e: e.memset(cst[:, 0:1], EPS), writes=["cst"])
            P.add("dve", lambda e: e.memset(cst[:, 1:2], 1.0), writes=["cst"])
            P.add("dve", lambda e: e.memset(cst[:, 2:3], 0.0), writes=["cst"])
            P.add("act", lambda e: e.activation(out=skr[:], in_=skr[:], func=AF.Exp), reads=["skr"], writes=["skr"])

            for i, r0 in enumerate(range(0, NROW, 128)):
                n = min(128, NROW - r0)
                b = i % 2
                pi = P.ps()
                P.add("sp", lambda e, r0=r0, n=n, b=b: e.dma_start(out=rows[:n, b, :], in_=vecs_in[r0:r0 + n, :]), writes=[f"rows{b}"], kind="d")
                P.add("pe", lambda e, n=n, b=b, pi=pi: e.transpose(out=PS[pi][:, :n], in_=rows[:n, b, :], identity=ident[:n, :n]),
                      reads=[f"rows{b}", "ident"], writes=[f"ps{pi}"])
                P.add("dve", lambda e, r0=r0, n=n, pi=pi: e.tensor_copy(out=vT[:, r0:r0 + n], in_=PS[pi][:, :n]), reads=[f"ps{pi}"], writes=["vT"])
            for col in range(2):
                P.add("act", lambda e, col=col: e.activation(out=scT[:, :, col], in_=vT[:, col * 16:(col + 1) * 16], func=AF.Silu),
                      reads=["vT"], writes=["scT"])
            for l in range(L):
                lam = vT[:, voff(l, "rg_lam"):voff(l, "rg_lam") + 16]
                W = lambda i: lamw[:, i, :]
                seq = [
                    ("dve", lambda e, lam=lam: e.tensor_scalar(out=W(0), in0=lam, scalar1=-1.0, scalar2=None, op0=ALU.mult)),
                    ("dve", lambda e, lam=lam: e.tensor_tensor(out=W(0), in0=W(0), in1=lam, op=ALU.max)),
                    ("act", lambda e: e.activation(out=W(1), in_=W(0), func=AF.Exp, scale=-1.0)),
                    ("dve", lambda e: e.tensor_scalar(out=W(2), in0=W(1), scalar1=2.0, scalar2=None, op0=ALU.add)),
                    ("dve", lambda e: e.reciprocal(out=W(2), in_=W(2))),
                    ("dve", lambda e: e.tensor_tensor(out=W(2), in0=W(2), in1=W(1), op=ALU.mult)),
                    ("dve", lambda e: e.tensor_tensor(out=W(3), in0=W(2), in1=W(2), op=ALU.mult)),
                    ("dve", lambda e: e.tensor_scalar(out=W(4), in0=W(3), scalar1=1.0 / 11, scalar2=1.0 / 9, op0=ALU.mult, op1=ALU.add)),
                    ("dve", lambda e: e.tensor_tensor(out=W(4), in0=W(4), in1=W(3), op=ALU.mult)),
                    ("dve", lambda e: e.tensor_scalar(out=W(4), in0=W(4), scalar1=1.0 / 7, scalar2=None, op0=ALU.add)),
                    ("dve", lambda e: e.tensor_tensor(out=W(4), in0=W(4), in1=W(3), op=ALU.mult)),
                    ("dve", lambda e: e.tensor_scalar(out=W(4), in0=W(4), scalar1=1.0 / 5, scalar2=None, op0=ALU.add)),
                    ("dve", lambda e: e.tensor_tensor(out=W(4), in0=W(4), in1=W(3), op=ALU.mult)),
                    ("dve", lambda e: e.tensor_scalar(out=W(4), in0=W(4), scalar1=1.0 / 3, scalar2=None, op0=ALU.add)),
                    ("dve", lambda e: e.tensor_tensor(out=W(4), in0=W(4), in1=W(3), op=ALU.mult)),
                    ("dve", lambda e: e.tensor_scalar(out=W(4), in0=W(4), scalar1=1.0, scalar2=None, op0=ALU.add)),
                    ("dve", lambda e: e.tensor_tensor(out=W(4), in0=W(4), in1=W(2), op=ALU.mult)),
                    ("dve", lambda e, lam=lam: e.tensor_scalar(out=W(5), in0=lam, scalar1=-1.0, scalar2=0.0, op0=ALU.mult, op1=ALU.max)),
                    ("dve", lambda e: e.tensor_scalar(out=W(4), in0=W(4), scalar1=2.0, scalar2=None, op0=ALU.mult)),
                    ("dve", lambda e: e.tensor_tensor(out=W(4), in0=W(4), in1=W(5), op=ALU.add)),
                    ("dve", lambda e, l=l: e.tensor_scalar(out=clam[:, l, :], in0=W(4), scalar1=-8.0, scalar2=None, op0=ALU.mult)),
                ]
                for eng, fn in seq:
                    P.add(eng, fn, reads=["vT", "lamw"], writes=["lamw", "clam"])

            OG = 8
            gi = 0
            for l in range(L):
                pi = P.ps()
                for og in range(144 // OG):
                    b = gi % 2
                    gi += 1
                    w_ada.load(P, "sp", wadat[b][:, :, :], l, og * OG * 128, OG * 128, writes=[f"wadat{b}"], dep=True)
                    for oo in range(OG):
                        o = og * OG + oo
                        for k in range(KC):
                            P.add("pe", lambda e, b=b, oo=oo, o=o, k=k, pi=pi: e.matmul(
                                PS[pi][:, 2 * o:2 * o + 2], lhsT=wadat[b][:, k, oo * 128:(oo + 1) * 128], rhs=scT[:, k, :], start=(k == 0), stop=(k == KC - 1)),
                                reads=[f"wadat{b}", "scT"], writes=[f"ps{pi}"])
                bo = voff(l, "b_ada")
                for col in range(2):
                    P.add("dve", lambda e, l=l, col=col, pi=pi, bo=bo: e.tensor_tensor(
                        out=mod[:, l, :, col], in0=PS[pi][:, 0:288].rearrange("p (o c) -> p o c", c=2)[:, :, col],
                        in1=vT[:, bo:bo + 144], op=ALU.add), reads=[f"ps{pi}", "vT"], writes=["mod"])
                for j in range(3):
                    go = voff(l, "norm_g") + j * 16
                    for col in range(2):
                        P.add("dve", lambda e, l=l, j=j, col=col, go=go: e.scalar_tensor_tensor(
                            out=Aco[:, l, j, :, col], in0=mod[:, l, (3 * j + 1) * 16:(3 * j + 2) * 16, col], scalar=1.0,
                            in1=vT[:, go:go + 16], op0=ALU.add, op1=ALU.mult), reads=["mod", "vT"], writes=["Aco"])
                    P.add("dve", lambda e, l=l, j=j: e.tensor_scalar(
                        out=Gco[:, l, j, :, :], in0=mod[:, l, (3 * j + 2) * 16:(3 * j + 3) * 16, :], scalar1=(1.0 if j == 1 else 0.5),
                        scalar2=None, op0=ALU.mult), reads=["mod"], writes=["Gco"])

            for (t0, N, isctx) in tiles:
                nb = N // 128
                src = ctx_in if isctx else x_in
                r0 = 0 if isctx else t0 - CTX
                P.add("sp", lambda e, src=src, r0=r0, nb=nb, N=N: e.dma_start(
                    out=xin[:, :nb, :], in_=src[r0:r0 + N, :].rearrange("(b p) d -> p b d", p=128)), writes=["xin"], kind="d")
                for k in range(KC):
                    pi = P.ps()
                    for bb in range(nb):
                        P.add("pe", lambda e, k=k, bb=bb, pi=pi: e.transpose(
                            out=PS[pi][:, bb * 128:(bb + 1) * 128], in_=xin[:, bb, k * 128:(k + 1) * 128], identity=ident[:]),
                            reads=["xin", "ident"], writes=[f"ps{pi}"])
                    eng = "act" if k % 2 else "dve"
                    if eng == "act":
                        P.add("act", lambda e, k=k, pi=pi, N=N: e.activation(out=xs0[:, k, :N], in_=PS[pi][:, :N], func=AF.Identity),
                              reads=[f"ps{pi}"], writes=[f"xs0_{k}"])
                    else:
                        P.add("dve", lambda e, k=k, pi=pi, N=N: e.tensor_copy(out=xs0[:, k, :N], in_=PS[pi][:, :N]),
                              reads=[f"ps{pi}"], writes=[f"xs0_{k}"])
                P.add("sp", lambda e, t0=t0, N=N: e.dma_start(
                    out=xT[:, t0:t0 + N].rearrange("(k p) n -> p k n", p=128), in_=xs0[:, :, :N]),
                    reads=[f"xs0_{k}" for k in range(KC)], writes=["xT"], kind="d")
            run_phase(P)

        def phase_ffn(l, jf, tl):
            j = 0 if jf == 0 else 2
            mi = l * 2 + jf
            with contextlib.ExitStack() as ph:
                P = newP()
                xq = sb(ph, "xq", [128, KC, 128])
                sq = [sb(ph, f"sq{i}", [128, 128], BF16) for i in range(2)]
                rstd = sb(ph, "rstd", [128, 128])
                sg = [sb(ph, f"sg{i}", [128, 512]) for i in range(3)]
                hT = sb(ph, "hT", [128, 2, KC, 512], BF16)
                aT = sb(ph, "aT", [128, FC, 1024], BF16)
                wu = [sb(ph, f"wu{i}", [128, 2, KC, 256], BF16) for i in range(2)]
                wd = [sb(ph, f"wd{i}", [128, FC, 128], BF16) for i in range(2)]
                xm = [sb(ph, f"xm{i}", [128, 512]) for i in range(3)]
                if jf == 0:
                    cast_group(P, l + 1, "up")
                groups = [[t] for t in tl if t[2]]
                lat = [t for t in tl if not t[2]]
                for i in range(0, len(lat), 2):
                    groups.append(lat[i:i + 2])
                sgc = 0

                def norm_pieces(grp):
                    colg = 1 if grp[0][2] else 0
                    pcs = []
                    for h, (t0, N, isctx) in enumerate(grp):
                        for q in range(N // 128):
                            def piece(h=h, tq=t0 + q * 128, q=q, colg=colg):
                                P.add("sp", lambda e: e.dma_start(out=xq[:, :, :], in_=xT[:, tq:tq + 128].rearrange("(k p) n -> p k n", p=128)),
                                      writes=["xs"], kind="d")
                                norm_tile(P, xq, 128, l, j, colg, sq, rstd, sg, hT[:, h, :, q * 128:(q + 1) * 128], tkey="sg")
                            pcs.append(piece)
                    return pcs

                for pc in norm_pieces(groups[0]):
                    pc()
                for gi, grp in enumerate(groups):
                    col = 1 if grp[0][2] else 0
                    nxt = norm_pieces(groups[gi + 1]) if gi + 1 < len(groups) else []
                    for f in range(FC):
                        b = (f // 2) % 2
                        jj = f % 2
                        if jj == 0:
                            nc_ = min(256, FF - f * 128)
                            w_up.load(P, "sp", wu[b][:, 0, :, :nc_], mi, f * 128, nc_, writes=[f"wu{b}g"])
                            w_up.load(P, "sp", wu[b][:, 1, :, :nc_], mi, FF + f * 128, nc_, writes=[f"wu{b}u"])
                        for h, (t0, N, isctx) in enumerate(grp):
                            pg, pu = P.ps(), P.ps()
                            for k in range(KC):
                                for (hh, pp, key) in ((0, pg, "g"), (1, pu, "u")):
                                    P.add("pe", lambda e, b=b, hh=hh, pp=pp, k=k, N=N, jj=jj, h=h: e.matmul(
                                        PS[pp][:, :N], lhsT=wu[b][:, hh, k, jj * 128:(jj + 1) * 128], rhs=hT[:, h, k, :N], start=(k == 0), stop=(k == KC - 1)),
                                        reads=[f"wu{b}{key}", f"hT{k}"], writes=[f"ps{pp}"])
                            s_ = sgc % 3
                            sgc += 1
                            P.add("act", lambda e, s_=s_, pg=pg, N=N: e.activation(out=sg[s_][:, :N], in_=PS[pg][:, :N], func=AF.Silu),
                                  reads=[f"ps{pg}"], writes=[f"sg{s_}"])
                            P.add("dve", lambda e, s_=s_, pu=pu, f=f, N=N, h=h: e.tensor_tensor(
                                out=aT[:, f, h * 512:h * 512 + N], in0=sg[s_][:, :N], in1=PS[pu][:, :N], op=ALU.mult),
                                reads=[f"sg{s_}", f"ps{pu}"], writes=[f"aT{f}_{h}"])
                    units = [(m, h) for m in range(KC) for h in range(len(grp))]

                    def xm_load(u):
                        if u < len(units):
                            m, h = units[u]
                            t0, N, _ = grp[h]
                            P.add("sp", lambda e, m=m, t0=t0, N=N, u=u: e.dma_start(out=xm[u % 3][:, :N], in_=xT[m * 128:(m + 1) * 128, t0:t0 + N]),
                                  writes=[f"xm{u % 3}"], kind="d")

                    xm_load(0)
                    xm_load(1)
                    w_dn.load(P, "act", wd[0][:, :, :], mi, 0, 128, writes=["wd0"])
                    for u, (m, h) in enumerate(units):
                        t0, N, isctx = grp[h]
                        b = m % 2
                        if h == 0 and m + 1 < KC:
                            w_dn.load(P, "act", wd[(m + 1) % 2][:, :, :], mi, (m + 1) * 128, 128, writes=[f"wd{(m + 1) % 2}"])
                        xm_load(u + 2)
                        xb = u % 3
                        pd = P.ps()
                        for f in range(FC):
                            P.add("pe", lambda e, b=b, f=f, pd=pd, N=N, h=h: e.matmul(
                                PS[pd][:, :N], lhsT=wd[b][:, f, :], rhs=aT[:, f, h * 512:h * 512 + N], start=(f == 0), stop=(f == FC - 1)),
                                reads=[f"wd{b}", f"aT{f}_{h}"], writes=[f"ps{pd}"])
                        P.add("dve", lambda e, m=m, pd=pd, N=N, col=col, xb=xb: e.scalar_tensor_tensor(
                            out=xm[xb][:, :N], in0=PS[pd][:, :N], scalar=Gco[:, l, j, m, col:col + 1], in1=xm[xb][:, :N],
                            op0=ALU.mult, op1=ALU.add), reads=[f"ps{pd}", f"xm{xb}"], writes=[f"xm{xb}"])
                        P.add("sp", lambda e, m=m, t0=t0, N=N, xb=xb: e.dma_start(out=xT[m * 128:(m + 1) * 128, t0:t0 + N], in_=xm[xb][:, :N]),
                              reads=[f"xm{xb}"], writes=[f"xTs{t0}_{m}"], kind="d")
                        if nxt:
                            nxt.pop(0)()
                    for pc in nxt:
                        pc()
                run_phase(P)

        def phase_a(l, last):
            with contextlib.ExitStack() as ph:
                P = newP()
                xs = sb(ph, "xs", [128, KC, 512])
                sq = [sb(ph, f"sq{i}", [128, 512], BF16) for i in range(2)]
                rstd = sb(ph, "rstd", [128, 512])
                tmp = [sb(ph, f"tmp{i}", [128, 512]) for i in range(2)]
                hT = sb(ph, "hT", [128, KC, 512], BF16)
                NW = 4
                wi = [sb(ph, f"wi{i}", [128, KC, 256], BF16) for i in range(NW)]
                wrot = sb(ph, "wrot", [128, 10, KC, 128], BF16)
                cosb = sb(ph, "cosb", [128, 512])
                sinb = sb(ph, "sinb", [128, 512])
                r1 = [sb(ph, f"r1_{i}", [128, 512]) for i in range(2)]
                r2 = [sb(ph, f"r2_{i}", [128, 512]) for i in range(2)]
                stb = [sb(ph, f"stb{i}", [128, 512], BF16) for i in range(2)]
                stf = [sb(ph, f"stf{i}", [128, 512]) for i in range(2)]
                wcnt = 0
                rcnt = 0
                for op_ in range(5):
                    w_in.load(P, "sp", wi[op_ % NW][:, :, :], l, op_ * 256, 256, writes=[f"wi{op_ % NW}"])
                    for jj_ in range(2):
                        for a_ in range(2):
                            P.add("pool", lambda e, op_=op_, jj_=jj_, a_=a_: e.tensor_copy(
                                out=wrot[:, op_ * 2 + jj_, :, :].rearrange("p k (a q f) -> p k a q f", a=2, q=2)[:, :, :, a_, :],
                                in_=wi[op_ % NW][:, :, jj_ * 128:(jj_ + 1) * 128].rearrange("p k (a q f) -> p k a q f", a=2, q=2)[:, :, :, 1 - a_, :]),
                                reads=[f"wi{op_ % NW}"], writes=["wrot"])
                witems = []
                for (t0, N, isctx) in tiles:
                    ch = list(range(8, 20)) if (isctx and last) else list(range(28))
                    witems += [o for o in ch if o % 2 == 0]
                wstate = {"next": 0}

                def prefetch(upto):
                    while wstate["next"] <= upto and wstate["next"] < len(witems):
                        i = wstate["next"]
                        w_in.load(P, "sp", wi[i % NW][:, :, :], l, witems[i] * 128, 256, writes=[f"wi{i % NW}"])
                        wstate["next"] += 1

                for (t0, N, isctx) in tiles:
                    col = 1 if isctx else 0
                    load_x(P, xs, t0, N)
                    norm_tile(P, xs, N, l, 1, col, sq, rstd, tmp, hT)
                    if not isctx:
                        s0 = t0 - CTX
                        P.add("sp", lambda e, s0=s0, N=N: e.dma_start(out=cosb[:, :N], in_=cos_in[:, s0:s0 + N]), writes=["cosb"], kind="d")
                        P.add("sp", lambda e, s0=s0, N=N: e.dma_start(out=sinb[:, :N], in_=sin_in[:, s0:s0 + N]), writes=["sinb"], kind="d")
                    chunks = list(range(8, 20)) if (isctx and last) else list(range(28))
                    for o in chunks:
                        jj = o % 2
                        if jj == 0:
                            b = wcnt % NW
                            prefetch(wcnt + NW - 1)
                            wcnt += 1
                        wic = wi[b][:, :, jj * 128:(jj + 1) * 128]
                        if o in (10, 11):
                            pi = P.ps()
                            for sbk in range(N // 128):
                                for k in range(KC):
                                    P.add("pe", lambda e, b=b, sbk=sbk, k=k, pi=pi, jj=jj: e.matmul(
                                        PS[pi][:, sbk * 128:(sbk + 1) * 128], lhsT=hT[:, k, sbk * 128:(sbk + 1) * 128], rhs=wi[b][:, k, jj * 128:(jj + 1) * 128],
                                        start=(k == 0), stop=(k == KC - 1)), reads=[f"wi{b}", f"hT{k}"], writes=[f"ps{pi}"])
                            s = o % 2
                            P.add("act", lambda e, s=s, pi=pi, N=N: e.activation(out=stb[s][:, :N], in_=PS[pi][:, :N], func=AF.Identity),
                                  reads=[f"ps{pi}"], writes=[f"stb{s}"])
                            hv = o - 10
                            P.add("sp", lambda e, s=s, t0=t0, N=N, hv=hv: e.dma_start(
                                out=vS[t0:t0 + N, hv * 128:(hv + 1) * 128].rearrange("(b p) c -> p b c", p=128),
                                in_=stb[s][:, :N].rearrange("p (b c) -> p b c", c=128)), reads=[f"stb{s}"], writes=["vS"], kind="d")
                            continue
                        pi = P.ps()
                        for k in range(KC):
                            P.add("pe", lambda e, b=b, k=k, pi=pi, N=N, jj=jj: e.matmul(
                                PS[pi][:, :N], lhsT=wi[b][:, k, jj * 128:(jj + 1) * 128], rhs=hT[:, k, :N], start=(k == 0), stop=(k == KC - 1)),
                                reads=[f"wi{b}", f"hT{k}"], writes=[f"ps{pi}"])
                        if o < 10:
                            s = o % 2
                            dst = qT[o * 128:(o + 1) * 128, t0:t0 + N] if o < 8 else kT[(o - 8) * 128:(o - 7) * 128, t0:t0 + N]
                            dkey = "qT" if o < 8 else "kT"
                            if isctx:
                                P.add("act", lambda e, s=s, pi=pi, N=N: e.activation(out=stb[s][:, :N], in_=PS[pi][:, :N], func=AF.Identity),
                                      reads=[f"ps{pi}"], writes=[f"stb{s}"])
                            else:
                                pj = P.ps()
                                for k in range(KC):
                                    P.add("pe", lambda e, o=o, k=k, pj=pj, N=N: e.matmul(
                                        PS[pj][:, :N], lhsT=wrot[:, o, k, :], rhs=hT[:, k, :N], start=(k == 0), stop=(k == KC - 1)),
                                        reads=["wrot", f"hT{k}"], writes=[f"ps{pj}"])
                                P.add("dve", lambda e, s=s, pi=pi, N=N: e.tensor_tensor(out=r1[s][:, :N], in0=PS[pi][:, :N], in1=cosb[:, :N], op=ALU.mult),
                                      reads=[f"ps{pi}", "cosb"], writes=[f"r1_{s}"])
                                P.add("dve", lambda e, s=s, pj=pj, N=N: e.tensor_tensor(out=r2[s][:, :N], in0=PS[pj][:, :N], in1=sinb[:, :N], op=ALU.mult),
                                      reads=[f"ps{pj}", "sinb"], writes=[f"r2_{s}"])
                                P.add("pool", lambda e, s=s, N=N: e.tensor_tensor(out=stb[s][:, :N], in0=r1[s][:, :N], in1=r2[s][:, :N], op=ALU.add),
                                      reads=[f"r1_{s}", f"r2_{s}"], writes=[f"stb{s}"])
                            P.add("sp", lambda e, s=s, dst=dst, N=N: e.dma_start(out=dst, in_=stb[s][:, :N]), reads=[f"stb{s}"], writes=[dkey], kind="d")
                        else:
                            s = o % 2
                            if o < 20:
                                dst = uT[(o - 12) * 128:(o - 11) * 128, t0:t0 + N]
                                dkey = "uT"
                                P.add("dve", lambda e, s=s, pi=pi, N=N: e.tensor_copy(out=stf[s][:, :N], in_=PS[pi][:, :N]),
                                      reads=[f"ps{pi}"], writes=[f"stf{s}"])
                            else:
                                dst = ggT[(o - 20) * 128:(o - 19) * 128, t0:t0 + N]
                                dkey = "ggT"
                                P.add("act", lambda e, s=s, pi=pi, N=N: e.activation(out=stf[s][:, :N], in_=PS[pi][:, :N], func=AF.Gelu_apprx_tanh),
                                      reads=[f"ps{pi}"], writes=[f"stf{s}"])
                            P.add("sp", lambda e, s=s, dst=dst, N=N: e.dma_start(out=dst, in_=stf[s][:, :N]), reads=[f"stf{s}"], writes=[dkey], kind="d")
                run_phase(P)

        def phase_t(l, last):
            with contextlib.ExitStack() as ph:
                P = newP()
                cast_group(P, l + 1, "dn")
                ksb = sb(ph, "ksb", [128, 2, T], BF16)
                vsb = sb(ph, "vsb", [128, NB, 256], BF16)
                qsb = [sb(ph, f"qsb{i}", [128, 8, 128], BF16) for i in range(2)]
                rd = [sb(ph, f"rd{i}", [128, 512]) for i in range(2)]
                ya = [sb(ph, f"ya{i}", [128, 512], BF16) for i in range(2)]
                esk = sb(ph, "esk", [1, 8, 128])
                mskf = sb(ph, "mskf", [128, 2, 512])
                masks = sb(ph, "masks", [128, 2, 512], BF16)
                P.add("sp", lambda e: e.dma_start(out=mskf[:], in_=mask_in), writes=["mskf"], kind="d")
                P.add("dve", lambda e: e.tensor_copy(out=masks[:], in_=mskf[:]), reads=["mskf"], writes=["masks"])
                esk_hi = sb(ph, "esk_hi", [1, 8, 128], BF16)
                esk_lo = sb(ph, "esk_lo", [1, 8, 128], BF16)
                P.add("dve", lambda e: e.tensor_copy(out=esk[:], in_=skr[:, l * 8:(l + 1) * 8].unsqueeze(2).to_broadcast([1, 8, 128])), writes=["esk"])
                P.add("dve", lambda e: e.tensor_copy(out=esk_hi[:], in_=esk[:]), reads=["esk"], writes=["esk_hi"])
                P.add("dve", lambda e: e.tensor_tensor(out=esk[:], in0=esk[:], in1=esk_hi[:], op=ALU.subtract), reads=["esk", "esk_hi"], writes=["esk"])
                P.add("dve", lambda e: e.tensor_copy(out=esk_lo[:], in_=esk[:]), reads=["esk"], writes=["esk_lo"])
                P.add("sp", lambda e: e.dma_start(out=ksb[:], in_=kT.rearrange("(g p) t -> p g t", p=128)), writes=["ksb"], kind="d")
                P.add("sp", lambda e: e.dma_start(out=vsb[:], in_=vS.rearrange("(b p) c -> p b c", p=128)), writes=["vsb"], kind="d")
                qblocks = list(range(2, NB)) if last else list(range(NB))
                NPT = 12
                ptx = [sb(ph, f"ptx{i}", [128, 512], BF16) for i in range(NPT)]
                st = {"pc": 0, "uc": 0}
                units = []
                for qi, tb in enumerate(qblocks):
                    if tb < 2:
                        kbs = [(0, None), (1, None)]
                    else:
                        kbs = [(0, None), (1, None)]
                        if tb - 1 >= 2:
                            kbs.append((tb - 1, 0))
                        kbs.append((tb, None))
                        if tb + 1 < NB:
                            kbs.append((tb + 1, 1))
                    for g in range(2):
                        units.append({"qi": qi, "tb": tb, "g": g, "kbs": kbs, "pbs": []})

                def stage1(u):
                    qb_ = u["qi"] % 2
                    tb, g = u["tb"], u["g"]
                    if g == 0:
                        P.add("sp", lambda e: e.dma_start(
                            out=qsb[qb_][:], in_=qT[:, tb * 128:(tb + 1) * 128].rearrange("(h p) n -> p h n", p=128)),
                            writes=[f"qsb{qb_}"], kind="d")
                    for ki, (kb, mk) in enumerate(u["kbs"]):
                        pss = ki
                        P.add("pe", lambda e, kb=kb, pss=pss: e.matmul(
                            PS[pss][:, :], lhsT=ksb[:, g, kb * 128:(kb + 1) * 128],
                            rhs=qsb[qb_][:, g * 4:(g + 1) * 4, :].rearrange("p h n -> p (h n)"), start=True, stop=True),
                            reads=["ksb", f"qsb{qb_}"], writes=[f"ps{pss}"])
                        pb = st["pc"] % NPT
                        st["pc"] += 1
                        u["pbs"].append(pb)
                        P.add("act", lambda e, pb=pb, pss=pss: e.activation(out=ptx[pb][:], in_=PS[pss][:, :], func=AF.Exp, scale=float(128 ** -0.5)),
                              reads=[f"ps{pss}"], writes=[f"pt{pb}"])
                        if mk is not None:
                            P.add("dve", lambda e, pb=pb, mk=mk: e.tensor_tensor(out=ptx[pb][:], in0=ptx[pb][:], in1=masks[:, mk, :], op=ALU.mult),
                                  reads=[f"pt{pb}", "masks"], writes=[f"pt{pb}"])

                def stage2(u):
                    tb, g = u["tb"], u["g"]
                    po, pd = 5, 6
                    nk = len(u["kbs"])
                    for ki, (kb, mk) in enumerate(u["kbs"]):
                        pb = u["pbs"][ki]
                        P.add("pe", lambda e, kb=kb, pb=pb, ki=ki: e.matmul(
                            PS[po][:, :], lhsT=vsb[:, kb, g * 128:(g + 1) * 128], rhs=ptx[pb][:], start=(ki == 0), stop=(ki == nk - 1)),
                            reads=["vsb", f"pt{pb}"], writes=[f"ps{po}"])
                        P.add("pe", lambda e, pb=pb, ki=ki: e.matmul(
                            PS[pd][:, :], lhsT=ones_bf[:], rhs=ptx[pb][:], start=(ki == 0), stop=False),
                            reads=[f"pt{pb}", "ones_bf"], writes=[f"ps{pd}"])
                    hs = g * 4
                    P.add("pe", lambda e: e.matmul(
                        PS[pd][:, :], lhsT=ones_bf[0:1, :], rhs=esk_hi[0:1, hs:hs + 4, :].rearrange("p h n -> p (h n)"), start=False, stop=False),
                        reads=["esk_hi", "ones_bf"], writes=[f"ps{pd}"])
                    P.add("pe", lambda e: e.matmul(
                        PS[pd][:, :], lhsT=ones_bf[0:1, :], rhs=esk_lo[0:1, hs:hs + 4, :].rearrange("p h n -> p (h n)"), start=False, stop=True),
                        reads=["esk_lo", "ones_bf"], writes=[f"ps{pd}"])
                    ub = st["uc"] % 2
                    st["uc"] += 1
                    P.add("dve", lambda e: e.reciprocal(out=rd[ub][:], in_=PS[pd][:, :]), reads=[f"ps{pd}"], writes=[f"rd{ub}"])
                    P.add("dve", lambda e: e.tensor_tensor(out=ya[ub][:], in0=PS[po][:, :], in1=rd[ub][:], op=ALU.mult),
                          reads=[f"ps{po}", f"rd{ub}"], writes=[f"ya{ub}"])
                    P.add("sp", lambda e: e.dma_start(
                        out=yaT[g * 512:(g + 1) * 512, tb * 128:(tb + 1) * 128].rearrange("(h p) n -> p h n", p=128),
                        in_=ya[ub][:].rearrange("p (h n) -> p h n", n=128)), reads=[f"ya{ub}"], writes=["yaT"], kind="d")

                for i, u in enumerate(units):
                    stage1(u)
                    if i > 0:
                        stage2(units[i - 1])
                stage2(units[-1])
                run_phase(P)

        def phase_r(l, last):
            with contextlib.ExitStack() as ph:
                P = newP()
                names = ["ur", "u", "rr0", "rr1", "ii0", "ii1", "ww", "hf", "hb", "gg"]
                B = {n: sb(ph, "R" + n, [128, T]) for n in names}
                yo = sb(ph, "Ryo", [128, T], BF16)
                rgt = sb(ph, "rgt", [128, 32, 128])
                cast_group(P, l + 1, "mix")
                rg_w.load(P, "sp", rgt[:, :, :], l, 0, 128, writes=["rgt"])
                segs = [(0, CTX), (CTX, T)]
                ttiles = [(t0, N) for (t0, N, _) in tiles]
                for cb in range(8):
                    P.add("sp", lambda e, cb=cb: e.dma_start(out=B["ur"][:], in_=uT[cb * 128:(cb + 1) * 128, :]), reads=["uT"], writes=["ur"], kind="d")
                    P.add("sp", lambda e, cb=cb: e.dma_start(out=B["gg"][:], in_=ggT[cb * 128:(cb + 1) * 128, :]), reads=["ggT"], writes=["gg"], kind="d")
                    cw = lambda tap, cb=cb: vT[:, voff(l, "conv_w") + tap * 8 + cb: voff(l, "conv_w") + tap * 8 + cb + 1]
                    cbias = vT[:, voff(l, "conv_b") + cb: voff(l, "conv_b") + cb + 1]
                    P.add("dve", lambda e, cw=cw, cbias=cbias: e.tensor_scalar(out=B["u"][:], in0=B["ur"][:], scalar1=cw(2), scalar2=cbias, op0=ALU.mult, op1=ALU.add),
                          reads=["ur", "vT"], writes=["u"])
                    for (a0, a1) in segs:
                        for tap, off in ((0, -2), (1, -1), (3, 1)):
                            lo = max(a0, a0 - off)
                            hi = min(a1, a1 - off)
                            P.add("dve", lambda e, cw=cw, tap=tap, off=off, lo=lo, hi=hi: e.scalar_tensor_tensor(
                                out=B["u"][:, lo:hi], in0=B["ur"][:, lo + off:hi + off], scalar=cw(tap), in1=B["u"][:, lo:hi],
                                op0=ALU.mult, op1=ALU.add), reads=["ur", "u", "vT"], writes=["u"])
                    for d in range(2):
                        hbuf = B["hf"] if d == 0 else B["hb"]
                        hkey = "hf" if d == 0 else "hb"
                        RR, II = f"rr{d}", f"ii{d}"
                        for gi_, (gname, gkey) in enumerate(((RR, RR), (II, II))):
                            mm = (d * 2 + gi_) * 8 + cb
                            bcol = voff(l, "rg_b") + (d * 2 + gi_) * 8 + cb
                            for (t0, N) in ttiles:
                                pi = P.ps()
                                P.add("pe", lambda e, mm=mm, t0=t0, N=N, pi=pi: e.matmul(
                                    PS[pi][:, :N], lhsT=rgt[:, mm, :], rhs=B["u"][:, t0:t0 + N], start=True, stop=True),
                                    reads=["rgt", "u"], writes=[f"ps{pi}"])
                                P.add("act", lambda e, gname=gname, t0=t0, N=N, pi=pi, bcol=bcol: e.activation(
                                    out=B[gname][:, t0:t0 + N], in_=PS[pi][:, :N], func=AF.Sigmoid, bias=vT[:, bcol:bcol + 1], scale=1.0),
                                    reads=[f"ps{pi}", "vT"], writes=[gkey])
                    for d in range(2):
                        hbuf = B["hf"] if d == 0 else B["hb"]
                        hkey = "hf" if d == 0 else "hb"
                        RR, II = f"rr{d}", f"ii{d}"
                        cl = clam[:, l, d * 8 + cb: d * 8 + cb + 1]
                        P.add("act", lambda e, cl=cl, RR=RR: e.activation(out=B[RR][:], in_=B[RR][:], func=AF.Exp, scale=cl), reads=[RR, "clam"], writes=[RR])
                        P.add("dve", lambda e, RR=RR: e.scalar_tensor_tensor(out=B["ww"][:], in0=B[RR][:], scalar=-1.0, in1=B[RR][:], op0=ALU.mult, op1=ALU.mult),
                              reads=[RR], writes=["ww"])
                        P.add("dve", lambda e: e.tensor_scalar(out=B["ww"][:], in0=B["ww"][:], scalar1=1.0, scalar2=0.0, op0=ALU.add, op1=ALU.max),
                              reads=["ww"], writes=["ww"])
                        P.add("act", lambda e: e.activation(out=B["ww"][:], in_=B["ww"][:], func=AF.Sqrt), reads=["ww"], writes=["ww"])
                        P.add("dve", lambda e, II=II: e.tensor_tensor(out=B[II][:], in0=B[II][:], in1=B["u"][:], op=ALU.mult), reads=[II, "u"], writes=[II])
                        P.add("dve", lambda e, II=II: e.tensor_tensor(out=B["ww"][:], in0=B["ww"][:], in1=B[II][:], op=ALU.mult), reads=["ww", II], writes=["ww"])
                        if d == 0:
                            P.add("dve", lambda e, hbuf=hbuf, RR=RR: e.tensor_tensor_scan(
                                out=hbuf[:, 0:CTX], data0=B[RR][:, 0:CTX], data1=B["ww"][:, 0:CTX], initial=0.0, op0=ALU.mult, op1=ALU.add),
                                reads=[RR, "ww"], writes=[hkey])
                            P.add("dve", lambda e, hbuf=hbuf, RR=RR: e.tensor_tensor_scan(
                                out=hbuf[:, CTX:T], data0=B[RR][:, CTX:T], data1=B["ww"][:, CTX:T], initial=hbuf[:, CTX - 1:CTX], op0=ALU.mult, op1=ALU.add),
                                reads=[RR, "ww", hkey], writes=[hkey])
                        else:
                            P.add("dve", lambda e, hbuf=hbuf, RR=RR: e.tensor_tensor_scan(
                                out=hbuf[:, 0:CTX][:, ::-1], data0=B[RR][:, 0:CTX][:, ::-1], data1=B["ww"][:, 0:CTX][:, ::-1], initial=0.0,
                                op0=ALU.mult, op1=ALU.add), reads=[RR, "ww"], writes=[hkey])
                            P.add("dve", lambda e, hbuf=hbuf, RR=RR: e.tensor_tensor_scan(
                                out=hbuf[:, CTX:T][:, ::-1], data0=B[RR][:, CTX:T][:, ::-1], data1=B["ww"][:, CTX:T][:, ::-1], initial=hbuf[:, 0:1],
                                op0=ALU.mult, op1=ALU.add), reads=[RR, "ww", hkey], writes=[hkey])
                    lo = CTX if last else 0
                    P.add("pool", lambda e, lo=lo: e.tensor_tensor(out=B["hf"][:, lo:T], in0=B["hf"][:, lo:T], in1=B["hb"][:, lo:T], op=ALU.add),
                          reads=["hf", "hb"], writes=["hf"])
                    P.add("dve", lambda e, lo=lo: e.tensor_tensor(out=yo[:, lo:T], in0=B["hf"][:, lo:T], in1=B["gg"][:, lo:T], op=ALU.mult),
                          reads=["hf", "gg"], writes=["yo"])
                    P.add("sp", lambda e, cb=cb, lo=lo: e.dma_start(out=yrT[cb * 128:(cb + 1) * 128, lo:T], in_=yo[:, lo:T]), reads=["yo"], writes=["yrT"], kind="d")
                run_phase(P)

        def phase_c(l, last):
            with contextlib.ExitStack() as ph:
                P = newP()
                xs = sb(ph, "xs", [128, KC, 512])
                sq = [sb(ph, f"sq{i}", [128, 512], BF16) for i in range(2)]
                rstd = sb(ph, "rstd", [128, 512])
                tmp = [sb(ph, f"tmp{i}", [128, 512]) for i in range(2)]
                hT = sb(ph, "hT", [128, KC, 512], BF16)
                zT = sb(ph, "zT", [128, KC, 512], BF16)
                yas = sb(ph, "yas", [128, 8, 512], BF16)
                yrs = sb(ph, "yrs", [128, 8, 512], BF16)
                wga = [sb(ph, f"wga{i}", [128, KC, 256], BF16) for i in range(2)]
                wgb = [sb(ph, f"wgb{i}", [128, KC, 256], BF16) for i in range(2)]
                woa = [sb(ph, f"woa{i}", [128, 8, 256], BF16) for i in range(2)]
                wor = [sb(ph, f"wor{i}", [128, 8, 256], BF16) for i in range(2)]
                wo = [sb(ph, f"wo{i}", [128, KC, 256], BF16) for i in range(2)]
                s1 = [sb(ph, f"s1_{i}", [128, 512]) for i in range(2)]
                s2 = [sb(ph, f"s2_{i}", [128, 512]) for i in range(2)]
                t1 = [sb(ph, f"t1_{i}", [128, 512]) for i in range(2)]
                t2 = [sb(ph, f"t2_{i}", [128, 512]) for i in range(2)]
                tl = [t for t in tiles if not (last and t[2])]
                for (t0, N, isctx) in tl:
                    col = 1 if isctx else 0
                    load_x(P, xs, t0, N)
                    P.add("sp", lambda e, t0=t0, N=N: e.dma_start(out=yas[:, :, :N], in_=yaT[:, t0:t0 + N].rearrange("(k p) n -> p k n", p=128)),
                          reads=["yaT"], writes=["yas"], kind="d")
                    P.add("sp", lambda e, t0=t0, N=N: e.dma_start(out=yrs[:, :, :N], in_=yrT[:, t0:t0 + N].rearrange("(k p) n -> p k n", p=128)),
                          reads=["yrT"], writes=["yrs"], kind="d")
                    norm_tile(P, xs, N, l, 1, col, sq, rstd, tmp, hT)
                    for o in range(KC):
                        b = (o // 2) % 2
                        jj = o % 2
                        cs = slice(jj * 128, (jj + 1) * 128)
                        if jj == 0:
                            w_in.load(P, "sp", wga[b][:, :, :], l, (28 + o) * 128, 256, writes=[f"wga{b}"])
                            w_in.load(P, "sp", wgb[b][:, :, :], l, (44 + o) * 128, 256, writes=[f"wgb{b}"])
                            w_oa.load(P, "sp", woa[b][:, :, :], l, o * 128, 256, writes=[f"woa{b}"])
                            w_or.load(P, "sp", wor[b][:, :, :], l, o * 128, 256, writes=[f"wor{b}"])
                        p1, p2, p3, p4 = P.ps(), P.ps(), P.ps(), P.ps()
                        for k in range(KC):
                            P.add("pe", lambda e, b=b, k=k, p1=p1, N=N, cs=cs: e.matmul(PS[p1][:, :N], lhsT=wga[b][:, k, cs], rhs=hT[:, k, :N], start=(k == 0), stop=(k == KC - 1)),
                                  reads=[f"wga{b}", f"hT{k}"], writes=[f"ps{p1}"])
                        for k in range(KC):
                            P.add("pe", lambda e, b=b, k=k, p2=p2, N=N, cs=cs: e.matmul(PS[p2][:, :N], lhsT=wgb[b][:, k, cs], rhs=hT[:, k, :N], start=(k == 0), stop=(k == KC - 1)),
                                  reads=[f"wgb{b}", f"hT{k}"], writes=[f"ps{p2}"])
                        for k in range(8):
                            P.add("pe", lambda e, b=b, k=k, p3=p3, N=N, cs=cs: e.matmul(PS[p3][:, :N], lhsT=woa[b][:, k, cs], rhs=yas[:, k, :N], start=(k == 0), stop=(k == 7)),
                                  reads=[f"woa{b}", "yas"], writes=[f"ps{p3}"])
                        for k in range(8):
                            P.add("pe", lambda e, b=b, k=k, p4=p4, N=N, cs=cs: e.matmul(PS[p4][:, :N], lhsT=wor[b][:, k, cs], rhs=yrs[:, k, :N], start=(k == 0), stop=(k == 7)),
                                  reads=[f"wor{b}", "yrs"], writes=[f"ps{p4}"])
                        P.add("act", lambda e, b=b, p1=p1, N=N: e.activation(out=s1[b][:, :N], in_=PS[p1][:, :N], func=AF.Sigmoid), reads=[f"ps{p1}"], writes=[f"s1_{b}"])
                        P.add("act", lambda e, b=b, p2=p2, N=N: e.activation(out=s2[b][:, :N], in_=PS[p2][:, :N], func=AF.Sigmoid), reads=[f"ps{p2}"], writes=[f"s2_{b}"])
                        P.add("dve", lambda e, b=b, p3=p3, N=N: e.tensor_tensor(out=t1[b][:, :N], in0=s1[b][:, :N], in1=PS[p3][:, :N], op=ALU.mult),
                              reads=[f"s1_{b}", f"ps{p3}"], writes=[f"t1_{b}"])
                        P.add("dve", lambda e, b=b, p4=p4, N=N: e.tensor_tensor(out=t2[b][:, :N], in0=s2[b][:, :N], in1=PS[p4][:, :N], op=ALU.mult),
                              reads=[f"s2_{b}", f"ps{p4}"], writes=[f"t2_{b}"])
                        P.add("pool", lambda e, b=b, o=o, N=N: e.tensor_tensor(out=zT[:, o, :N], in0=t1[b][:, :N], in1=t2[b][:, :N], op=ALU.add),
                              reads=[f"t1_{b}", f"t2_{b}"], writes=[f"zT{o}"])
                    for m in range(KC):
                        b = (m // 2) % 2
                        jj = m % 2
                        cs = slice(jj * 128, (jj + 1) * 128)
                        if jj == 0:
                            w_out.load(P, "sp", wo[b][:, :, :], l, m * 128, 256, writes=[f"wo{b}"])
                        pd = P.ps()
                        for k in range(KC):
                            P.add("pe", lambda e, b=b, k=k, pd=pd, N=N, cs=cs: e.matmul(PS[pd][:, :N], lhsT=wo[b][:, k, cs], rhs=zT[:, k, :N], start=(k == 0), stop=(k == KC - 1)),
                                  reads=[f"wo{b}", f"zT{k}"], writes=[f"ps{pd}"])
                        P.add("dve", lambda e, m=m, pd=pd, N=N, col=col: e.scalar_tensor_tensor(
                            out=xs[:, m, :N], in0=PS[pd][:, :N], scalar=Gco[:, l, 1, m, col:col + 1], in1=xs[:, m, :N],
                            op0=ALU.mult, op1=ALU.add), reads=[f"ps{pd}", "xs"], writes=["xs"])
                    store_x(P, xs, t0, N)
                run_phase(P)

        def phase_final():
            with contextlib.ExitStack() as ph:
                P = newP()
                xs = sb(ph, "xs", [128, KC, 512])
                sq = [sb(ph, f"sq{i}", [128, 512], BF16) for i in range(2)]
                rstd = sb(ph, "rstd", [128, 512])
                ot = [sb(ph, f"ot{i}", [128, D]) for i in range(2)]
                oc = 0
                for (t0, N, isctx) in tiles:
                    if isctx:
                        continue
                    load_x(P, xs, t0, N)
                    rstd_tile(P, xs, N, sq, rstd)
                    for k in range(KC):
                        P.add("dve", lambda e, k=k, N=N: e.scalar_tensor_tensor(
                            out=xs[:, k, :N], in0=xs[:, k, :N], scalar=vT[:, 32 + k:33 + k], in1=rstd[:, :N], op0=ALU.mult, op1=ALU.mult),
                            reads=["xs", "rstd", "vT"], writes=["xs"])
                    for bb in range(N // 128):
                        ob = oc % 2
                        oc += 1
                        for kq in range(4):
                            pi = P.ps()
                            for kk in range(4):
                                k = kq * 4 + kk
                                P.add("pe", lambda e, k=k, kk=kk, bb=bb, pi=pi: e.transpose(
                                    out=PS[pi][:, kk * 128:(kk + 1) * 128], in_=xs[:, k, bb * 128:(bb + 1) * 128], identity=ident[:]),
                                    reads=["xs", "ident"], writes=[f"ps{pi}"])
                            if kq % 2:
                                P.add("act", lambda e, ob=ob, kq=kq, pi=pi: e.activation(out=ot[ob][:, kq * 512:(kq + 1) * 512], in_=PS[pi][:, :], func=AF.Identity),
                                      reads=[f"ps{pi}"], writes=[f"ot{ob}"])
                            else:
                                P.add("dve", lambda e, ob=ob, kq=kq, pi=pi: e.tensor_copy(out=ot[ob][:, kq * 512:(kq + 1) * 512], in_=PS[pi][:, :]),
                                      reads=[f"ps{pi}"], writes=[f"ot{ob}"])
                        r0 = t0 - CTX + bb * 128
                        P.add("sp", lambda e, ob=ob, r0=r0: e.dma_start(out=out[r0:r0 + 128, :], in_=ot[ob][:]), reads=[f"ot{ob}"], writes=["out"], kind="d")
                run_phase(P)

        for l in range(L):
            last = (l == L - 1)
            phase_ffn(l, 0, tiles)
            phase_a(l, last)
            phase_t(l, last)
            phase_r(l, last)
            phase_c(l, last)
            phase_ffn(l, 1, [t for t in tiles if not (last and t[2])])
        phase_final()
    return nc


def _consts(S):
    ident = np.eye(128, dtype=np.float32)
    rows = S // 64
    pos = np.stack([np.repeat(np.arange(rows, dtype=np.float32), 64), np.tile(np.arange(64, dtype=np.float32), rows)], 0)
    inv = (10000.0 ** (-np.arange(32, dtype=np.float32) / 32)).astype(np.float32)
    cos = np.zeros((128, S), np.float32)
    sin = np.zeros((128, S), np.float32)
    for a in range(2):
        ang = (pos[a][None, :] * inv[:, None]).astype(np.float32)
        for q in range(2):
            sl = slice(a * 64 + q * 32, a * 64 + q * 32 + 32)
            cos[sl] = np.cos(ang)
            sin[sl] = np.sin(ang) * (-1.0 if q == 0 else 1.0)
    j = np.arange(128)[:, None]
    i = np.arange(128)[None, :]
    mp = (j >= i).astype(np.float32)
    mn = (j <= i).astype(np.float32)
    masks = np.stack([np.tile(mp, (1, 4)), np.tile(mn, (1, 4))], 1).astype(np.float32)
    return ident, cos, sin, masks


def make_in_maps(inp, S, L, NR, cores):
    ident, cos, sin, masks = _consts(S)
    f = lambda a: np.ascontiguousarray(np.asarray(a, dtype=np.float32))
    w2d = {
        "w_ada": f(inp["w_ada"][:L]).reshape(L * D, 18432),
        "w_up": f(inp["w_ffn_up"][:L]).reshape(L * 2 * D, 2 * FF),
        "w_dn": f(inp["w_ffn_down"][:L]).reshape(L * 2 * FF, D),
        "w_in": f(inp["w_in"][:L]).reshape(L * D, INW),
        "w_oa": f(inp["w_o_attn"][:L]).reshape(L * 1024, D),
        "w_or": f(inp["w_o_rnn"][:L]).reshape(L * 1024, D),
        "w_out": f(inp["w_out"][:L]).reshape(L * D, D),
        "rg_w": f(inp["rg_w"][:L]).reshape(L * 32 * 128, 128),
    }
    maps = []
    for ci, b in enumerate(cores):
        rows = [f(inp["c"][b]).reshape(16, 128), f(inp["c_ctx"]).reshape(16, 128), f(inp["final_g"]).reshape(16, 128)]
        for l in range(L):
            rows += [f(inp["norm_g"][l]).reshape(48, 128), f(inp["b_ada"][l]).reshape(144, 128), f(inp["conv_w"][l]).reshape(32, 128),
                     f(inp["conv_b"][l]).reshape(8, 128), f(inp["rg_b"][l]).reshape(32, 128), f(inp["rg_lambda"][l]).reshape(16, 128)]
        m = {
            "x_b": f(inp["x"][b][:S]), "ctx_b": f(inp["ctx"][b]), "vecs": np.concatenate(rows, 0),
            "sink": f(inp["attn_sink"][:L]).reshape(1, L * 8), "ident": ident, "rope_cos": cos, "rope_sin": sin, "masks": masks,
        }
        for k, w in w2d.items():
            m[k + "_s"] = w
        maps.append(m)
    return maps


_NC_CACHE = {}


def kernel(**inputs):
    S, L, NR = 4096, 4, 8
    key = (S, L, NR)
    if key not in _NC_CACHE:
        _NC_CACHE[key] = build(S, L, NR)
    nc = _NC_CACHE[key]
    maps = make_in_maps(inputs, S, L, NR, list(range(8)))
    res = run_bass_kernel_spmd(nc, maps, core_ids=list(range(8)))
    return np.stack([np.asarray(r["out"], dtype=np.float32) for r in res.results], 0)
```

```python
import contextlib
import numpy as np
import concourse.bass as bass
import concourse.mybir as mybir
from concourse.bass_utils import run_bass_kernel_spmd

F32 = mybir.dt.float32
BF16 = mybir.dt.bfloat16
AF = mybir.ActivationFunctionType
ALU = mybir.AluOpType

D = 2048
KC = 16
FF = 5504
FC = 43
CTX = 256
INW = 7680
EPS = 1e-6
NQH = 8
ENGS = ("pe", "act", "dve", "pool", "sp")


class _Op:
    __slots__ = ("eng", "fn", "deps", "kind", "sig", "needed", "dsem")

    def __init__(self, eng, fn, kind):
        self.eng = eng
        self.fn = fn
        self.kind = kind
        self.deps = []
        self.sig = None
        self.needed = False
        self.dsem = None


class Prog:
    NDSEM = 8

    def __init__(self, sems, base):
        self.sems = sems
        self.base = dict(base)
        self.ops = {e: [] for e in ENGS}
        self.last_w = {}
        self.readers = {}
        self.dma_rr = {e: 0 for e in ENGS}
        self.dma_last = {}
        self.psi = 0

    def ps(self):
        i = self.psi % 8
        self.psi += 1
        return i

    def add(self, eng, fn, reads=(), writes=(), kind="c"):
        op = _Op(eng, fn, kind)
        deps = []
        for k in reads:
            w = self.last_w.get(k)
            if w is not None:
                deps.append((w, True))
        for k in writes:
            w = self.last_w.get(k)
            if w is not None:
                deps.append((w, False))
            for r in self.readers.get(k, ()):
                deps.append((r, False))
        if kind == "d":
            i = self.dma_rr[eng] % self.NDSEM
            self.dma_rr[eng] += 1
            op.dsem = ("dma", eng, i)
            prev = self.dma_last.get(op.dsem)
            if prev is not None:
                deps.append((prev, True))
            self.dma_last[op.dsem] = op
        elif kind == "cc":
            op.dsem = "cc"
        seen = set()
        for d, raw in deps:
            if d is op or id(d) in seen:
                continue
            if d.kind == "c" and kind == "c" and d.eng == eng and (eng == "pe" or not raw):
                continue
            seen.add(id(d))
            op.deps.append(d)
            d.needed = True
        for k in writes:
            self.last_w[k] = op
            self.readers[k] = []
        for k in reads:
            self.readers.setdefault(k, []).append(op)
        self.ops[eng].append(op)
        return op

    def finalize(self):
        cnt = dict(self.base)
        for e in ENGS:
            last = None
            for op in self.ops[e]:
                if op.kind == "c":
                    last = op
            if last is not None:
                last.needed = True
            for op in self.ops[e]:
                if op.kind == "d":
                    cnt[op.dsem] += 16
                    op.sig = (op.dsem, cnt[op.dsem])
                elif op.kind == "cc":
                    cnt["cc"] += 1
                    op.sig = ("cc", cnt["cc"])
                elif op.needed:
                    cnt[e] += 1
                    op.sig = (e, cnt[e])
        self.final = cnt

    def emit(self, eng, eo):
        known = {}
        for op in self.ops[eng]:
            for d in op.deps:
                s, v = d.sig
                if known.get(s, -1) < v:
                    eo.wait_ge(self.sems[s], v)
                    known[s] = v
            ins = op.fn(eo)
            if op.kind == "d":
                ins.then_inc(self.sems[op.dsem], 16)
            elif op.kind == "cc":
                ins.then_inc(self.sems["cc"], 1)
            elif op.needed:
                ins.then_inc(self.sems[eng], 1)
        for s, v in self.final.items():
            if v > self.base[s]:
                eo.wait_ge(self.sems[s], v)


class TW:
    def __init__(self, nc, name, nmat, rows, cols, dt):
        self.name = name
        self.nmat = nmat
        self.rows = rows
        self.cols = cols
        self.dt = dt
        self.src = nc.dram_tensor(name + "_s", [nmat * rows, cols], F32, kind="ExternalInput").ap()
        self.nat = [nc.dram_tensor(f"{name}_n{m}", [rows, cols], dt, kind="Internal").ap() for m in range(nmat)]

    def prologue(self, P, m):
        RB = 512
        for r0 in range(0, self.rows, RB):
            r1 = min(self.rows, r0 + RB)
            P.add("pool", lambda e, r0=r0, r1=r1: e.dma_start(
                out=self.nat[m][r0:r1, :], in_=self.src[m * self.rows + r0:m * self.rows + r1, :]),
                writes=[f"{self.name}_n{m}_{r0}"], kind="d")

    def keys(self, m):
        return [f"{self.name}_n{m}_{r0}" for r0 in range(0, self.rows, 512)]

    def load(self, P, eng, dst, m, c0, ncols, writes, dep=False):
        P.add(eng, lambda e: e.dma_start(out=dst, in_=self.nat[m][:, c0:c0 + ncols].rearrange("(k p) c -> p k c", p=128)),
              reads=(self.keys(m) if dep else []), writes=list(writes), kind="d")


def build(S, L, NR):
    T = CTX + S
    NB = T // 128
    tiles = [(0, CTX, True)] + [(CTX + 512 * i, 512, False) for i in range(S // 512)]
    NROW = 48 + 280 * L
    nc = bass.Bass("TRN2", target_bir_lowering=False)

    def din(name, shape, dt=F32):
        return nc.dram_tensor(name, list(shape), dt, kind="ExternalInput").ap()

    def dscr(name, shape, dt):
        return nc.dram_tensor(name, list(shape), dt, kind="Internal").ap()

    x_in = din("x_b", [S, D])
    ctx_in = din("ctx_b", [CTX, D])
    vecs_in = din("vecs", [NROW, 128])
    sink_in = din("sink", [1, L * 8])
    ident_in = din("ident", [128, 128])
    cos_in = din("rope_cos", [128, S])
    sin_in = din("rope_sin", [128, S])
    mask_in = din("masks", [128, 2, 512])
    out = nc.dram_tensor("out", [S, D], F32, kind="ExternalOutput").ap()

    w_ada = TW(nc, "w_ada", L, D, 18432, BF16)
    w_up = TW(nc, "w_up", L * 2, D, 2 * FF, BF16)
    w_dn = TW(nc, "w_dn", L * 2, FF, D, BF16)
    w_in = TW(nc, "w_in", L, D, INW, BF16)
    w_oa = TW(nc, "w_oa", L, 1024, D, BF16)
    w_or = TW(nc, "w_or", L, 1024, D, BF16)
    w_out = TW(nc, "w_out", L, D, D, BF16)
    rg_w = TW(nc, "rg_w", L, 32 * 128, 128, F32)
    allw = [w_ada, w_up, w_dn, w_in, w_oa, w_or, w_out, rg_w]

    xT = dscr("xT", [D, T], F32)
    qT = dscr("qT", [1024, T], BF16)
    kT = dscr("kT", [256, T], BF16)
    vS = dscr("vS", [T, 256], BF16)
    uT = dscr("uT", [1024, T], F32)
    ggT = dscr("ggT", [1024, T], F32)
    yaT = dscr("yaT", [1024, T], BF16)
    yrT = dscr("yrT", [1024, T], BF16)

    es = contextlib.ExitStack()
    with es:
        semnames = list(ENGS) + ["cc"] + [("dma", e, i) for e in ENGS for i in range(Prog.NDSEM)]
        sems = {}
        for k in semnames:
            nm = k if isinstance(k, str) else f"d_{k[1]}_{k[2]}"
            sems[k] = es.enter_context(nc.semaphore("s_" + nm))
        state = {"base": {k: 0 for k in semnames}}

        def sb(stack, name, shape, dt=F32):
            state["nid"] = state.get("nid", 0) + 1
            return stack.enter_context(nc.sbuf_tensor(f"sb{state['nid']}_{name}", list(shape), dt))

        PS = [es.enter_context(nc.psum_tensor(f"psb{i}", [128, 512], F32)) for i in range(8)]

        ident = sb(es, "ident", [128, 128])
        ones_bf = sb(es, "ones_bf", [128, 128], BF16)
        cst = sb(es, "cst", [128, 4])
        vT = sb(es, "vecsT", [128, NROW])
        mod = sb(es, "mod", [128, L, 144, 2])
        Aco = sb(es, "Aco", [128, L, 3, 16, 2])
        Gco = sb(es, "Gco", [128, L, 3, 16, 2])
        clam = sb(es, "clam", [128, L, 16])
        skr = sb(es, "skr", [1, L * 8])
        scT = sb(es, "scT", [128, 16, 2], BF16)

        def run_phase(P):
            P.finalize()
            with nc.Block() as blk:
                @blk.tensor
                def _(e):
                    P.emit("pe", e)

                @blk.scalar
                def _(e):
                    P.emit("act", e)

                @blk.vector
                def _(e):
                    P.emit("dve", e)

                @blk.gpsimd
                def _(e):
                    P.emit("pool", e)

                @blk.sync
                def _(e):
                    P.emit("sp", e)
            state["base"] = dict(P.final)

        def newP():
            return Prog(sems, state["base"])

        def voff(l, what):
            return 48 + l * 280 + {"norm_g": 0, "b_ada": 48, "conv_w": 192, "conv_b": 224, "rg_b": 232, "rg_lam": 264}[what]

        def load_x(P, xs, t0, N, tag="xs"):
            P.add("sp", lambda e: e.dma_start(out=xs[:, :, :N], in_=xT[:, t0:t0 + N].rearrange("(k p) n -> p k n", p=128)),
                  reads=["xT"], writes=[tag], kind="d")

        def store_x(P, xs, t0, N, tag="xs"):
            P.add("sp", lambda e: e.dma_start(out=xT[:, t0:t0 + N].rearrange("(k p) n -> p k n", p=128), in_=xs[:, :, :N]),
                  reads=[tag], writes=["xT"], kind="d")

        def rstd_tile(P, xs, N, sq, rstd, tag="xs"):
            pi = P.ps()
            for k in range(KC):
                P.add("act", lambda e, k=k: e.activation(out=sq[k % 2][:, :N], in_=xs[:, k, :N], func=AF.Square),
                      reads=[tag], writes=[f"sq{k % 2}"])
                P.add("pe", lambda e, k=k: e.matmul(PS[pi][:, :N], lhsT=ones_bf[:], rhs=sq[k % 2][:, :N], start=(k == 0), stop=(k == KC - 1)),
                      reads=[f"sq{k % 2}"], writes=[f"ps{pi}"])
            P.add("act", lambda e: e.activation(out=rstd[:, :N], in_=PS[pi][:, :N], func=AF.Sqrt, scale=1.0 / D, bias=cst[:, 0:1]),
                  reads=[f"ps{pi}"], writes=["rstd"])
            P.add("dve", lambda e: e.reciprocal(out=rstd[:, :N], in_=rstd[:, :N]), reads=["rstd"], writes=["rstd"])

        def norm_tile(P, xs, N, l, j, col, sq, rstd, tmp, hT, tkey="tmp"):
            rstd_tile(P, xs, N, sq, rstd)
            for k in range(KC):
                P.add("dve", lambda e, k=k: e.scalar_tensor_tensor(
                    out=tmp[k % 2][:, :N], in0=xs[:, k, :N], scalar=Aco[:, l, j, k, col:col + 1], in1=rstd[:, :N],
                    op0=ALU.mult, op1=ALU.mult), reads=["xs", "rstd"], writes=[f"{tkey}{k % 2}"])
                P.add("act", lambda e, k=k: e.activation(
                    out=hT[:, k, :N], in_=tmp[k % 2][:, :N], func=AF.Identity,
                    bias=mod[:, l, (3 * j) * 16 + k, col:col + 1], scale=1.0), reads=[f"{tkey}{k % 2}"], writes=[f"hT{k}"])

        HT_KEYS = [f"hT{k}" for k in range(KC)]

        def cast_group(P, l, which):
            if l >= L:
                return
            if which == "up":
                w_up.prologue(P, 2 * l)
                w_up.prologue(P, 2 * l + 1)
            elif which == "dn":
                w_dn.prologue(P, 2 * l)
                w_dn.prologue(P, 2 * l + 1)
            else:
                for w in (w_in, rg_w, w_oa, w_or, w_out):
                    w.prologue(P, l)

        with contextlib.ExitStack() as ph:
            P = newP()
            rows = sb(ph, "rows", [128, 2, 128])
            lamw = sb(ph, "lamw", [128, 8, 16])
            wadat = [sb(ph, f"wadat{i}", [128, 16, 1024], BF16) for i in range(2)]
            xin = sb(ph, "xin", [128, 4, D])
            xs0 = sb(ph, "xs0", [128, KC, 512])

            for m in range(L):
                w_ada.prologue(P, m)
            cast_group(P, 0, "up")
            cast_group(P, 0, "dn")
            cast_group(P, 0, "mix")

            P.add("sp", lambda e: e.dma_start(out=ident[:], in_=ident_in), writes=["ident"], kind="d")
            P.add("sp", lambda e: e.dma_start(out=skr[:], in_=sink_in), writes=["skr"], kind="d")
            P.add("dve", lambda e: e.memset(ones_bf[:], 1.0), writes=["ones_bf"])
            P.add("dve", lambda e: e.memset(cst[:, 0:1], EPS), writes=["cst"])
            P.add("dve", lambda e: e.memset(cst[:, 1:2], 1.0), writes=["cst"])
            P.add("dve", lambda e: e.memset(cst[:, 2:3], 0.0), writes=["cst"])
            P.add("act", lambda e: e.activation(out=skr[:], in_=skr[:], func=AF.Exp), reads=["skr"], writes=["skr"])

            for i, r0 in enumerate(range(0, NROW, 128)):
                n = min(128, NROW - r0)
                b = i % 2
                pi = P.ps()
                P.add("sp", lambda e, r0=r0, n=n, b=b: e.dma_start(out=rows[:n, b, :], in_=vecs_in[r0:r0 + n, :]), writes=[f"rows{b}"], kind="d")
                P.add("pe", lambda e, n=n, b=b, pi=pi: e.transpose(out=PS[pi][:, :n], in_=rows[:n, b, :], identity=ident[:n, :n]),
                      reads=[f"rows{b}", "ident"], writes=[f"ps{pi}"])
                P.add("dve", lambda e, r0=r0, n=n, pi=pi: e.tensor_copy(out=vT[:, r0:r0 + n], in_=PS[pi][:, :n]), reads=[f"ps{pi}"], writes=["vT"])
            for col in range(2):
                P.add("act", lambda e, col=col: e.activation(out=scT[:, :, col], in_=vT[:, col * 16:(col + 1) * 16], func=AF.Silu),
                      reads=["vT"], writes=["scT"])
            for l in range(L):
                lam = vT[:, voff(l, "rg_lam"):voff(l, "rg_lam") + 16]
                W = lambda i: lamw[:, i, :]
                seq = [
                    ("dve", lambda e, lam=lam: e.tensor_scalar(out=W(0), in0=lam, scalar1=-1.0, scalar2=None, op0=ALU.mult)),
                    ("dve", lambda e, lam=lam: e.tensor_tensor(out=W(0), in0=W(0), in1=lam, op=ALU.max)),
                    ("act", lambda e: e.activation(out=W(1), in_=W(0), func=AF.Exp, scale=-1.0)),
                    ("dve", lambda e: e.tensor_scalar(out=W(2), in0=W(1), scalar1=2.0, scalar2=None, op0=ALU.add)),
                    ("dve", lambda e: e.reciprocal(out=W(2), in_=W(2))),
                    ("dve", lambda e: e.tensor_tensor(out=W(2), in0=W(2), in1=W(1), op=ALU.mult)),
                    ("dve", lambda e: e.tensor_tensor(out=W(3), in0=W(2), in1=W(2), op=ALU.mult)),
                    ("dve", lambda e: e.tensor_scalar(out=W(4), in0=W(3), scalar1=1.0 / 11, scalar2=1.0 / 9, op0=ALU.mult, op1=ALU.add)),
                    ("dve", lambda e: e.tensor_tensor(out=W(4), in0=W(4), in1=W(3), op=ALU.mult)),
                    ("dve", lambda e: e.tensor_scalar(out=W(4), in0=W(4), scalar1=1.0 / 7, scalar2=None, op0=ALU.add)),
                    ("dve", lambda e: e.tensor_tensor(out=W(4), in0=W(4), in1=W(3), op=ALU.mult)),
                    ("dve", lambda e: e.tensor_scalar(out=W(4), in0=W(4), scalar1=1.0 / 5, scalar2=None, op0=ALU.add)),
                    ("dve", lambda e: e.tensor_tensor(out=W(4), in0=W(4), in1=W(3), op=ALU.mult)),
                    ("dve", lambda e: e.tensor_scalar(out=W(4), in0=W(4), scalar1=1.0 / 3, scalar2=None, op0=ALU.add)),
                    ("dve", lambda e: e.tensor_tensor(out=W(4), in0=W(4), in1=W(3), op=ALU.mult)),
                    ("dve", lambda e: e.tensor_scalar(out=W(4), in0=W(4), scalar1=1.0, scalar2=None, op0=ALU.add)),
                    ("dve", lambda e: e.tensor_tensor(out=W(4), in0=W(4), in1=W(2), op=ALU.mult)),
                    ("dve", lambda e, lam=lam: e.tensor_scalar(out=W(5), in0=lam, scalar1=-1.0, scalar2=0.0, op0=ALU.mult, op1=ALU.max)),
                    ("dve", lambda e: e.tensor_scalar(out=W(4), in0=W(4), scalar1=2.0, scalar2=None, op0=ALU.mult)),
                    ("dve", lambda e: e.tensor_tensor(out=W(4), in0=W(4), in1=W(5), op=ALU.add)),
                    ("dve", lambda e, l=l: e.tensor_scalar(out=clam[:, l, :], in0=W(4), scalar1=-8.0, scalar2=None, op0=ALU.mult)),
                ]
                for eng, fn in seq:
                    P.add(eng, fn, reads=["vT", "lamw"], writes=["lamw", "clam"])

            OG = 8
            gi = 0
            for l in range(L):
                pi = P.ps()
                for og in range(144 // OG):
                    b = gi % 2
                    gi += 1
                    w_ada.load(P, "sp", wadat[b][:, :, :], l, og * OG * 128, OG * 128, writes=[f"wadat{b}"], dep=True)
                    for oo in range(OG):
                        o = og * OG + oo
                        for k in range(KC):
                            P.add("pe", lambda e, b=b, oo=oo, o=o, k=k, pi=pi: e.matmul(
                                PS[pi][:, 2 * o:2 * o + 2], lhsT=wadat[b][:, k, oo * 128:(oo + 1) * 128], rhs=scT[:, k, :], start=(k == 0), stop=(k == KC - 1)),
                                reads=[f"wadat{b}", "scT"], writes=[f"ps{pi}"])
                bo = voff(l, "b_ada")
                for col in range(2):
                    P.add("dve", lambda e, l=l, col=col, pi=pi, bo=bo: e.tensor_tensor(
                        out=mod[:, l, :, col], in0=PS[pi][:, 0:288].rearrange("p (o c) -> p o c", c=2)[:, :, col],
                        in1=vT[:, bo:bo + 144], op=ALU.add), reads=[f"ps{pi}", "vT"], writes=["mod"])
                for j in range(3):
                    go = voff(l, "norm_g") + j * 16
                    for col in range(2):
                        P.add("dve", lambda e, l=l, j=j, col=col, go=go: e.scalar_tensor_tensor(
                            out=Aco[:, l, j, :, col], in0=mod[:, l, (3 * j + 1) * 16:(3 * j + 2) * 16, col], scalar=1.0,
                            in1=vT[:, go:go + 16], op0=ALU.add, op1=ALU.mult), reads=["mod", "vT"], writes=["Aco"])
                    P.add("dve", lambda e, l=l, j=j: e.tensor_scalar(
                        out=Gco[:, l, j, :, :], in0=mod[:, l, (3 * j + 2) * 16:(3 * j + 3) * 16, :], scalar1=(1.0 if j == 1 else 0.5),
                        scalar2=None, op0=ALU.mult), reads=["mod"], writes=["Gco"])

            for (t0, N, isctx) in tiles:
                nb = N // 128
                src = ctx_in if isctx else x_in
                r0 = 0 if isctx else t0 - CTX
                P.add("sp", lambda e, src=src, r0=r0, nb=nb, N=N: e.dma_start(
                    out=xin[:, :nb, :], in_=src[r0:r0 + N, :].rearrange("(b p) d -> p b d", p=128)), writes=["xin"], kind="d")
                for k in range(KC):
                    pi = P.ps()
                    for bb in range(nb):
                        P.add("pe", lambda e, k=k, bb=bb, pi=pi: e.transpose(
                            out=PS[pi][:, bb * 128:(bb + 1) * 128], in_=xin[:, bb, k * 128:(k + 1) * 128], identity=ident[:]),
                            reads=["xin", "ident"], writes=[f"ps{pi}"])
                    eng = "act" if k % 2 else "dve"
                    if eng == "act":
                        P.add("act", lambda e, k=k, pi=pi, N=N: e.activation(out=xs0[:, k, :N], in_=PS[pi][:, :N], func=AF.Identity),
                              reads=[f"ps{pi}"], writes=[f"xs0_{k}"])
                    else:
                        P.add("dve", lambda e, k=k, pi=pi, N=N: e.tensor_copy(out=xs0[:, k, :N], in_=PS[pi][:, :N]),
                              reads=[f"ps{pi}"], writes=[f"xs0_{k}"])
                P.add("sp", lambda e, t0=t0, N=N: e.dma_start(
                    out=xT[:, t0:t0 + N].rearrange("(k p) n -> p k n", p=128), in_=xs0[:, :, :N]),
                    reads=[f"xs0_{k}" for k in range(KC)], writes=["xT"], kind="d")
            run_phase(P)

        def phase_ffn(l, jf, tl):
            j = 0 if jf == 0 else 2
            mi = l * 2 + jf
            with contextlib.ExitStack() as ph:
                P = newP()
                xq = sb(ph, "xq", [128, KC, 128])
                sq = [sb(ph, f"sq{i}", [128, 128], BF16) for i in range(2)]
                rstd = sb(ph, "rstd", [128, 128])
                sg = [sb(ph, f"sg{i}", [128, 512]) for i in range(3)]
                hT = sb(ph, "hT", [128, 2, KC, 512], BF16)
                aT = sb(ph, "aT", [128, FC, 1024], BF16)
                wu = [sb(ph, f"wu{i}", [128, 2, KC, 256], BF16) for i in range(2)]
                wd = [sb(ph, f"wd{i}", [128, FC, 128], BF16) for i in range(2)]
                xm = [sb(ph, f"xm{i}", [128, 512]) for i in range(3)]
                if jf == 0:
                    cast_group(P, l + 1, "up")
                groups = [[t] for t in tl if t[2]]
                lat = [t for t in tl if not t[2]]
                for i in range(0, len(lat), 2):
                    groups.append(lat[i:i + 2])
                sgc = 0

                def norm_pieces(grp):
                    colg = 1 if grp[0][2] else 0
                    pcs = []
                    for h, (t0, N, isctx) in enumerate(grp):
                        for q in range(N // 128):
                            def piece(h=h, tq=t0 + q * 128, q=q, colg=colg):
                                P.add("sp", lambda e: e.dma_start(out=xq[:, :, :], in_=xT[:, tq:tq + 128].rearrange("(k p) n -> p k n", p=128)),
                                      writes=["xs"], kind="d")
                                norm_tile(P, xq, 128, l, j, colg, sq, rstd, sg, hT[:, h, :, q * 128:(q + 1) * 128], tkey="sg")
                            pcs.append(piece)
                    return pcs

                for pc in norm_pieces(groups[0]):
                    pc()
                for gi, grp in enumerate(groups):
                    col = 1 if grp[0][2] else 0
                    nxt = norm_pieces(groups[gi + 1]) if gi + 1 < len(groups) else []
                    for f in range(FC):
                        b = (f // 2) % 2
                        jj = f % 2
                        if jj == 0:
                            nc_ = min(256, FF - f * 128)
                            w_up.load(P, "sp", wu[b][:, 0, :, :nc_], mi, f * 128, nc_, writes=[f"wu{b}g"])
                            w_up.load(P, "sp", wu[b][:, 1, :, :nc_], mi, FF + f * 128, nc_, writes=[f"wu{b}u"])
                        for h, (t0, N, isctx) in enumerate(grp):
                            pg, pu = P.ps(), P.ps()
                            for k in range(KC):
                                for (hh, pp, key) in ((0, pg, "g"), (1, pu, "u")):
                                    P.add("pe", lambda e, b=b, hh=hh, pp=pp, k=k, N=N, jj=jj, h=h: e.matmul(
                                        PS[pp][:, :N], lhsT=wu[b][:, hh, k, jj * 128:(jj + 1) * 128], rhs=hT[:, h, k, :N], start=(k == 0), stop=(k == KC - 1)),
                                        reads=[f"wu{b}{key}", f"hT{k}"], writes=[f"ps{pp}"])
                            s_ = sgc % 3
                            sgc += 1
                            P.add("act", lambda e, s_=s_, pg=pg, N=N: e.activation(out=sg[s_][:, :N], in_=PS[pg][:, :N], func=AF.Silu),
                                  reads=[f"ps{pg}"], writes=[f"sg{s_}"])
                            P.add("dve", lambda e, s_=s_, pu=pu, f=f, N=N, h=h: e.tensor_tensor(
                                out=aT[:, f, h * 512:h * 512 + N], in0=sg[s_][:, :N], in1=PS[pu][:, :N], op=ALU.mult),
                                reads=[f"sg{s_}", f"ps{pu}"], writes=[f"aT{f}_{h}"])
                    units = [(m, h) for m in range(KC) for h in range(len(grp))]

                    def xm_load(u):
                        if u < len(units):
                            m, h = units[u]
                            t0, N, _ = grp[h]
                            P.add("sp", lambda e, m=m, t0=t0, N=N, u=u: e.dma_start(out=xm[u % 3][:, :N], in_=xT[m * 128:(m + 1) * 128, t0:t0 + N]),
                                  writes=[f"xm{u % 3}"], kind="d")

                    xm_load(0)
                    xm_load(1)
                    w_dn.load(P, "act", wd[0][:, :, :], mi, 0, 128, writes=["wd0"])
                    for u, (m, h) in enumerate(units):
                        t0, N, isctx = grp[h]
                        b = m % 2
                        if h == 0 and m + 1 < KC:
                            w_dn.load(P, "act", wd[(m + 1) % 2][:, :, :], mi, (m + 1) * 128, 128, writes=[f"wd{(m + 1) % 2}"])
                        xm_load(u + 2)
                        xb = u % 3
                        pd = P.ps()
                        for f in range(FC):
                            P.add("pe", lambda e, b=b, f=f, pd=pd, N=N, h=h: e.matmul(
                                PS[pd][:, :N], lhsT=wd[b][:, f, :], rhs=aT[:, f, h * 512:h * 512 + N], start=(f == 0), stop=(f == FC - 1)),
                                reads=[f"wd{b}", f"aT{f}_{h}"], writes=[f"ps{pd}"])
                        P.add("dve", lambda e, m=m, pd=pd, N=N, col=col, xb=xb: e.scalar_tensor_tensor(
                            out=xm[xb][:, :N], in0=PS[pd][:, :N], scalar=Gco[:, l, j, m, col:col + 1], in1=xm[xb][:, :N],
                            op0=ALU.mult, op1=ALU.add), reads=[f"ps{pd}", f"xm{xb}"], writes=[f"xm{xb}"])
                        P.add("sp", lambda e, m=m, t0=t0, N=N, xb=xb: e.dma_start(out=xT[m * 128:(m + 1) * 128, t0:t0 + N], in_=xm[xb][:, :N]),
                              reads=[f"xm{xb}"], writes=[f"xTs{t0}_{m}"], kind="d")
                        if nxt:
                            nxt.pop(0)()
                    for pc in nxt:
                        pc()
                run_phase(P)

        def phase_a(l, last):
            with contextlib.ExitStack() as ph:
                P = newP()
                xs = sb(ph, "xs", [128, KC, 512])
                sq = [sb(ph, f"sq{i}", [128, 512], BF16) for i in range(2)]
                rstd = sb(ph, "rstd", [128, 512])
                tmp = [sb(ph, f"tmp{i}", [128, 512]) for i in range(2)]
                hT = sb(ph, "hT", [128, KC, 512], BF16)
                NW = 4
                wi = [sb(ph, f"wi{i}", [128, KC, 256], BF16) for i in range(NW)]
                wrot = sb(ph, "wrot", [128, 10, KC, 128], BF16)
                cosb = sb(ph, "cosb", [128, 512])
                sinb = sb(ph, "sinb", [128, 512])
                r1 = [sb(ph, f"r1_{i}", [128, 512]) for i in range(2)]
                r2 = [sb(ph, f"r2_{i}", [128, 512]) for i in range(2)]
                stb = [sb(ph, f"stb{i}", [128, 512], BF16) for i in range(2)]
                stf = [sb(ph, f"stf{i}", [128, 512]) for i in range(2)]
                wcnt = 0
                rcnt = 0
                for op_ in range(5):
                    w_in.load(P, "sp", wi[op_ % NW][:, :, :], l, op_ * 256, 256, writes=[f"wi{op_ % NW}"])
                    for jj_ in range(2):
                        for a_ in range(2):
                            P.add("pool", lambda e, op_=op_, jj_=jj_, a_=a_: e.tensor_copy(
                                out=wrot[:, op_ * 2 + jj_, :, :].rearrange("p k (a q f) -> p k a q f", a=2, q=2)[:, :, :, a_, :],
                                in_=wi[op_ % NW][:, :, jj_ * 128:(jj_ + 1) * 128].rearrange("p k (a q f) -> p k a q f", a=2, q=2)[:, :, :, 1 - a_, :]),
                                reads=[f"wi{op_ % NW}"], writes=["wrot"])
                witems = []
                for (t0, N, isctx) in tiles:
                    ch = list(range(8, 20)) if (isctx and last) else list(range(28))
                    witems += [o for o in ch if o % 2 == 0]
                wstate = {"next": 0}

                def prefetch(upto):
                    while wstate["next"] <= upto and wstate["next"] < len(witems):
                        i = wstate["next"]
                        w_in.load(P, ("sp" if i % 2 == 0 else "act"), wi[i % NW][:, :, :], l, witems[i] * 128, 256, writes=[f"wi{i % NW}"])
                        wstate["next"] += 1

                for (t0, N, isctx) in tiles:
                    col = 1 if isctx else 0
                    load_x(P, xs, t0, N)
                    norm_tile(P, xs, N, l, 1, col, sq, rstd, tmp, hT)
                    if not isctx:
                        s0 = t0 - CTX
                        P.add("sp", lambda e, s0=s0, N=N: e.dma_start(out=cosb[:, :N], in_=cos_in[:, s0:s0 + N]), writes=["cosb"], kind="d")
                        P.add("sp", lambda e, s0=s0, N=N: e.dma_start(out=sinb[:, :N], in_=sin_in[:, s0:s0 + N]), writes=["sinb"], kind="d")
                    chunks = list(range(8, 20)) if (isctx and last) else list(range(28))
                    for o in chunks:
                        jj = o % 2
                        if jj == 0:
                            b = wcnt % NW
                            prefetch(wcnt + NW - 1)
                            wcnt += 1
                        wic = wi[b][:, :, jj * 128:(jj + 1) * 128]
                        if o in (10, 11):
                            pi = P.ps()
                            for sbk in range(N // 128):
                                for k in range(KC):
                                    P.add("pe", lambda e, b=b, sbk=sbk, k=k, pi=pi, jj=jj: e.matmul(
                                        PS[pi][:, sbk * 128:(sbk + 1) * 128], lhsT=hT[:, k, sbk * 128:(sbk + 1) * 128], rhs=wi[b][:, k, jj * 128:(jj + 1) * 128],
                                        start=(k == 0), stop=(k == KC - 1)), reads=[f"wi{b}", f"hT{k}"], writes=[f"ps{pi}"])
                            s = o % 2
                            P.add("act", lambda e, s=s, pi=pi, N=N: e.activation(out=stb[s][:, :N], in_=PS[pi][:, :N], func=AF.Identity),
                                  reads=[f"ps{pi}"], writes=[f"stb{s}"])
                            hv = o - 10
                            P.add("sp", lambda e, s=s, t0=t0, N=N, hv=hv: e.dma_start(
                                out=vS[t0:t0 + N, hv * 128:(hv + 1) * 128].rearrange("(b p) c -> p b c", p=128),
                                in_=stb[s][:, :N].rearrange("p (b c) -> p b c", c=128)), reads=[f"stb{s}"], writes=["vS"], kind="d")
                            continue
                        pi = P.ps()
                        for k in range(KC):
                            P.add("pe", lambda e, b=b, k=k, pi=pi, N=N, jj=jj: e.matmul(
                                PS[pi][:, :N], lhsT=wi[b][:, k, jj * 128:(jj + 1) * 128], rhs=hT[:, k, :N], start=(k == 0), stop=(k == KC - 1)),
                                reads=[f"wi{b}", f"hT{k}"], writes=[f"ps{pi}"])
                        if o < 10:
                            s = o % 2
                            dst = qT[o * 128:(o + 1) * 128, t0:t0 + N] if o < 8 else kT[(o - 8) * 128:(o - 7) * 128, t0:t0 + N]
                            dkey = "qT" if o < 8 else "kT"
                            if isctx:
                                P.add("act", lambda e, s=s, pi=pi, N=N: e.activation(out=stb[s][:, :N], in_=PS[pi][:, :N], func=AF.Identity),
                                      reads=[f"ps{pi}"], writes=[f"stb{s}"])
                            else:
                                pj = P.ps()
                                for k in range(KC):
                                    P.add("pe", lambda e, o=o, k=k, pj=pj, N=N: e.matmul(
                                        PS[pj][:, :N], lhsT=wrot[:, o, k, :], rhs=hT[:, k, :N], start=(k == 0), stop=(k == KC - 1)),
                                        reads=["wrot", f"hT{k}"], writes=[f"ps{pj}"])
                                P.add("dve", lambda e, s=s, pi=pi, N=N: e.tensor_tensor(out=r1[s][:, :N], in0=PS[pi][:, :N], in1=cosb[:, :N], op=ALU.mult),
                                      reads=[f"ps{pi}", "cosb"], writes=[f"r1_{s}"])
                                P.add("dve", lambda e, s=s, pj=pj, N=N: e.tensor_tensor(out=r2[s][:, :N], in0=PS[pj][:, :N], in1=sinb[:, :N], op=ALU.mult),
                                      reads=[f"ps{pj}", "sinb"], writes=[f"r2_{s}"])
                                P.add("pool", lambda e, s=s, N=N: e.tensor_tensor(out=stb[s][:, :N], in0=r1[s][:, :N], in1=r2[s][:, :N], op=ALU.add),
                                      reads=[f"r1_{s}", f"r2_{s}"], writes=[f"stb{s}"])
                            P.add("sp", lambda e, s=s, dst=dst, N=N: e.dma_start(out=dst, in_=stb[s][:, :N]), reads=[f"stb{s}"], writes=[dkey], kind="d")
                        else:
                            s = o % 2
                            if o < 20:
                                dst = uT[(o - 12) * 128:(o - 11) * 128, t0:t0 + N]
                                dkey = "uT"
                                P.add("dve", lambda e, s=s, pi=pi, N=N: e.tensor_copy(out=stf[s][:, :N], in_=PS[pi][:, :N]),
                                      reads=[f"ps{pi}"], writes=[f"stf{s}"])
                            else:
                                dst = ggT[(o - 20) * 128:(o - 19) * 128, t0:t0 + N]
                                dkey = "ggT"
                                P.add("act", lambda e, s=s, pi=pi, N=N: e.activation(out=stf[s][:, :N], in_=PS[pi][:, :N], func=AF.Gelu_apprx_tanh),
                                      reads=[f"ps{pi}"], writes=[f"stf{s}"])
                            P.add("sp", lambda e, s=s, dst=dst, N=N: e.dma_start(out=dst, in_=stf[s][:, :N]), reads=[f"stf{s}"], writes=[dkey], kind="d")
                run_phase(P)

        def phase_t(l, last):
            with contextlib.ExitStack() as ph:
                P = newP()
                cast_group(P, l + 1, "dn")
                ksb = sb(ph, "ksb", [128, 2, T], BF16)
                vsb = sb(ph, "vsb", [128, NB, 256], BF16)
                qsb = [sb(ph, f"qsb{i}", [128, 8, 128], BF16) for i in range(2)]
                rd = [sb(ph, f"rd{i}", [128, 512]) for i in range(2)]
                ya = [sb(ph, f"ya{i}", [128, 512], BF16) for i in range(2)]
                esk = sb(ph, "esk", [1, 8, 128])
                mskf = sb(ph, "mskf", [128, 2, 512])
                masks = sb(ph, "masks", [128, 2, 512], BF16)
                P.add("sp", lambda e: e.dma_start(out=mskf[:], in_=mask_in), writes=["mskf"], kind="d")
                P.add("dve", lambda e: e.tensor_copy(out=masks[:], in_=mskf[:]), reads=["mskf"], writes=["masks"])
                esk_hi = sb(ph, "esk_hi", [1, 8, 128], BF16)
                esk_lo = sb(ph, "esk_lo", [1, 8, 128], BF16)
                P.add("dve", lambda e: e.tensor_copy(out=esk[:], in_=skr[:, l * 8:(l + 1) * 8].unsqueeze(2).to_broadcast([1, 8, 128])), writes=["esk"])
                P.add("dve", lambda e: e.tensor_copy(out=esk_hi[:], in_=esk[:]), reads=["esk"], writes=["esk_hi"])
                P.add("dve", lambda e: e.tensor_tensor(out=esk[:], in0=esk[:], in1=esk_hi[:], op=ALU.subtract), reads=["esk", "esk_hi"], writes=["esk"])
                P.add("dve", lambda e: e.tensor_copy(out=esk_lo[:], in_=esk[:]), reads=["esk"], writes=["esk_lo"])
                P.add("sp", lambda e: e.dma_start(out=ksb[:], in_=kT.rearrange("(g p) t -> p g t", p=128)), writes=["ksb"], kind="d")
                P.add("sp", lambda e: e.dma_start(out=vsb[:], in_=vS.rearrange("(b p) c -> p b c", p=128)), writes=["vsb"], kind="d")
                qblocks = list(range(2, NB)) if last else list(range(NB))
                NPT = 12
                ptx = [sb(ph, f"ptx{i}", [128, 512], BF16) for i in range(NPT)]
                st = {"pc": 0, "uc": 0}
                units = []
                for qi, tb in enumerate(qblocks):
                    if tb < 2:
                        kbs = [(0, None), (1, None)]
                    else:
                        kbs = [(0, None), (1, None)]
                        if tb - 1 >= 2:
                            kbs.append((tb - 1, 0))
                        kbs.append((tb, None))
                        if tb + 1 < NB:
                            kbs.append((tb + 1, 1))
                    for g in range(2):
                        units.append({"qi": qi, "tb": tb, "g": g, "kbs": kbs, "pbs": []})

                def stage1(u):
                    qb_ = u["qi"] % 2
                    tb, g = u["tb"], u["g"]
                    if g == 0:
                        P.add("sp", lambda e: e.dma_start(
                            out=qsb[qb_][:], in_=qT[:, tb * 128:(tb + 1) * 128].rearrange("(h p) n -> p h n", p=128)),
                            writes=[f"qsb{qb_}"], kind="d")
                    for ki, (kb, mk) in enumerate(u["kbs"]):
                        pss = ki
                        P.add("pe", lambda e, kb=kb, pss=pss: e.matmul(
                            PS[pss][:, :], lhsT=ksb[:, g, kb * 128:(kb + 1) * 128],
                            rhs=qsb[qb_][:, g * 4:(g + 1) * 4, :].rearrange("p h n -> p (h n)"), start=True, stop=True),
                            reads=["ksb", f"qsb{qb_}"], writes=[f"ps{pss}"])
                        pb = st["pc"] % NPT
                        st["pc"] += 1
                        u["pbs"].append(pb)
                        P.add("act", lambda e, pb=pb, pss=pss: e.activation(out=ptx[pb][:], in_=PS[pss][:, :], func=AF.Exp, scale=float(128 ** -0.5)),
                              reads=[f"ps{pss}"], writes=[f"pt{pb}"])
                        if mk is not None:
                            P.add("dve", lambda e, pb=pb, mk=mk: e.tensor_tensor(out=ptx[pb][:], in0=ptx[pb][:], in1=masks[:, mk, :], op=ALU.mult),
                                  reads=[f"pt{pb}", "masks"], writes=[f"pt{pb}"])

                def stage2(u):
                    tb, g = u["tb"], u["g"]
                    po, pd = 5, 6
                    nk = len(u["kbs"])
                    for ki, (kb, mk) in enumerate(u["kbs"]):
                        pb = u["pbs"][ki]
                        P.add("pe", lambda e, kb=kb, pb=pb, ki=ki: e.matmul(
                            PS[po][:, :], lhsT=vsb[:, kb, g * 128:(g + 1) * 128], rhs=ptx[pb][:], start=(ki == 0), stop=(ki == nk - 1)),
                            reads=["vsb", f"pt{pb}"], writes=[f"ps{po}"])
                        P.add("pe", lambda e, pb=pb, ki=ki: e.matmul(
                            PS[pd][:, :], lhsT=ones_bf[:], rhs=ptx[pb][:], start=(ki == 0), stop=False),
                            reads=[f"pt{pb}", "ones_bf"], writes=[f"ps{pd}"])
                    hs = g * 4
                    P.add("pe", lambda e: e.matmul(
                        PS[pd][:, :], lhsT=ones_bf[0:1, :], rhs=esk_hi[0:1, hs:hs + 4, :].rearrange("p h n -> p (h n)"), start=False, stop=False),
                        reads=["esk_hi", "ones_bf"], writes=[f"ps{pd}"])
                    P.add("pe", lambda e: e.matmul(
                        PS[pd][:, :], lhsT=ones_bf[0:1, :], rhs=esk_lo[0:1, hs:hs + 4, :].rearrange("p h n -> p (h n)"), start=False, stop=True),
                        reads=["esk_lo", "ones_bf"], writes=[f"ps{pd}"])
                    ub = st["uc"] % 2
                    st["uc"] += 1
                    P.add("dve", lambda e: e.reciprocal(out=rd[ub][:], in_=PS[pd][:, :]), reads=[f"ps{pd}"], writes=[f"rd{ub}"])
                    P.add("dve", lambda e: e.tensor_tensor(out=ya[ub][:], in0=PS[po][:, :], in1=rd[ub][:], op=ALU.mult),
                          reads=[f"ps{po}", f"rd{ub}"], writes=[f"ya{ub}"])
                    P.add("sp", lambda e: e.dma_start(
                        out=yaT[g * 512:(g + 1) * 512, tb * 128:(tb + 1) * 128].rearrange("(h p) n -> p h n", p=128),
                        in_=ya[ub][:].rearrange("p (h n) -> p h n", n=128)), reads=[f"ya{ub}"], writes=["yaT"], kind="d")

                for i, u in enumerate(units):
                    stage1(u)
                    if i > 0:
                        stage2(units[i - 1])
                stage2(units[-1])
                run_phase(P)

        def phase_r(l, last):
            with contextlib.ExitStack() as ph:
                P = newP()
                names = ["ur", "u", "rr0", "rr1", "ii0", "ii1", "ww", "hf", "hb", "gg"]
                B = {n: sb(ph, "R" + n, [128, T]) for n in names}
                yo = sb(ph, "Ryo", [128, T], BF16)
                rgt = sb(ph, "rgt", [128, 32, 128])
                cast_group(P, l + 1, "mix")
                rg_w.load(P, "sp", rgt[:, :, :], l, 0, 128, writes=["rgt"])
                segs = [(0, CTX), (CTX, T)]
                ttiles = [(t0, N) for (t0, N, _) in tiles]
                for cb in range(8):
                    P.add("sp", lambda e, cb=cb: e.dma_start(out=B["ur"][:], in_=uT[cb * 128:(cb + 1) * 128, :]), reads=["uT"], writes=["ur"], kind="d")
                    P.add("sp", lambda e, cb=cb: e.dma_start(out=B["gg"][:], in_=ggT[cb * 128:(cb + 1) * 128, :]), reads=["ggT"], writes=["gg"], kind="d")
                    cw = lambda tap, cb=cb: vT[:, voff(l, "conv_w") + tap * 8 + cb: voff(l, "conv_w") + tap * 8 + cb + 1]
                    cbias = vT[:, voff(l, "conv_b") + cb: voff(l, "conv_b") + cb + 1]
                    P.add("dve", lambda e, cw=cw, cbias=cbias: e.tensor_scalar(out=B["u"][:], in0=B["ur"][:], scalar1=cw(2), scalar2=cbias, op0=ALU.mult, op1=ALU.add),
                          reads=["ur", "vT"], writes=["u"])
                    for (a0, a1) in segs:
                        for tap, off in ((0, -2), (1, -1), (3, 1)):
                            lo = max(a0, a0 - off)
                            hi = min(a1, a1 - off)
                            P.add("dve", lambda e, cw=cw, tap=tap, off=off, lo=lo, hi=hi: e.scalar_tensor_tensor(
                                out=B["u"][:, lo:hi], in0=B["ur"][:, lo + off:hi + off], scalar=cw(tap), in1=B["u"][:, lo:hi],
                                op0=ALU.mult, op1=ALU.add), reads=["ur", "u", "vT"], writes=["u"])
                    for d in range(2):
                        hbuf = B["hf"] if d == 0 else B["hb"]
                        hkey = "hf" if d == 0 else "hb"
                        RR, II = f"rr{d}", f"ii{d}"
                        for gi_, (gname, gkey) in enumerate(((RR, RR), (II, II))):
                            mm = (d * 2 + gi_) * 8 + cb
                            bcol = voff(l, "rg_b") + (d * 2 + gi_) * 8 + cb
                            for (t0, N) in ttiles:
                                pi = P.ps()
                                P.add("pe", lambda e, mm=mm, t0=t0, N=N, pi=pi: e.matmul(
                                    PS[pi][:, :N], lhsT=rgt[:, mm, :], rhs=B["u"][:, t0:t0 + N], start=True, stop=True),
                                    reads=["rgt", "u"], writes=[f"ps{pi}"])
                                P.add("act", lambda e, gname=gname, t0=t0, N=N, pi=pi, bcol=bcol: e.activation(
                                    out=B[gname][:, t0:t0 + N], in_=PS[pi][:, :N], func=AF.Sigmoid, bias=vT[:, bcol:bcol + 1], scale=1.0),
                                    reads=[f"ps{pi}", "vT"], writes=[gkey])
                    for d in range(2):
                        hbuf = B["hf"] if d == 0 else B["hb"]
                        hkey = "hf" if d == 0 else "hb"
                        RR, II = f"rr{d}", f"ii{d}"
                        cl = clam[:, l, d * 8 + cb: d * 8 + cb + 1]
                        P.add("act", lambda e, cl=cl, RR=RR: e.activation(out=B[RR][:], in_=B[RR][:], func=AF.Exp, scale=cl), reads=[RR, "clam"], writes=[RR])
                        P.add("dve", lambda e, RR=RR: e.scalar_tensor_tensor(out=B["ww"][:], in0=B[RR][:], scalar=-1.0, in1=B[RR][:], op0=ALU.mult, op1=ALU.mult),
                              reads=[RR], writes=["ww"])
                        P.add("dve", lambda e: e.tensor_scalar(out=B["ww"][:], in0=B["ww"][:], scalar1=1.0, scalar2=0.0, op0=ALU.add, op1=ALU.max),
                              reads=["ww"], writes=["ww"])
                        P.add("act", lambda e: e.activation(out=B["ww"][:], in_=B["ww"][:], func=AF.Sqrt), reads=["ww"], writes=["ww"])
                        P.add("dve", lambda e, II=II: e.tensor_tensor(out=B[II][:], in0=B[II][:], in1=B["u"][:], op=ALU.mult), reads=[II, "u"], writes=[II])
                        P.add("dve", lambda e, II=II: e.tensor_tensor(out=B["ww"][:], in0=B["ww"][:], in1=B[II][:], op=ALU.mult), reads=["ww", II], writes=["ww"])
                        if d == 0:
                            P.add("dve", lambda e, hbuf=hbuf, RR=RR: e.tensor_tensor_scan(
                                out=hbuf[:, 0:CTX], data0=B[RR][:, 0:CTX], data1=B["ww"][:, 0:CTX], initial=0.0, op0=ALU.mult, op1=ALU.add),
                                reads=[RR, "ww"], writes=[hkey])
                            P.add("dve", lambda e, hbuf=hbuf, RR=RR: e.tensor_tensor_scan(
                                out=hbuf[:, CTX:T], data0=B[RR][:, CTX:T], data1=B["ww"][:, CTX:T], initial=hbuf[:, CTX - 1:CTX], op0=ALU.mult, op1=ALU.add),
                                reads=[RR, "ww", hkey], writes=[hkey])
                        else:
                            P.add("dve", lambda e, hbuf=hbuf, RR=RR: e.tensor_tensor_scan(
                                out=hbuf[:, 0:CTX][:, ::-1], data0=B[RR][:, 0:CTX][:, ::-1], data1=B["ww"][:, 0:CTX][:, ::-1], initial=0.0,
                                op0=ALU.mult, op1=ALU.add), reads=[RR, "ww"], writes=[hkey])
                            P.add("dve", lambda e, hbuf=hbuf, RR=RR: e.tensor_tensor_scan(
                                out=hbuf[:, CTX:T][:, ::-1], data0=B[RR][:, CTX:T][:, ::-1], data1=B["ww"][:, CTX:T][:, ::-1], initial=hbuf[:, 0:1],
                                op0=ALU.mult, op1=ALU.add), reads=[RR, "ww", hkey], writes=[hkey])
                    lo = CTX if last else 0
                    P.add("pool", lambda e, lo=lo: e.tensor_tensor(out=B["hf"][:, lo:T], in0=B["hf"][:, lo:T], in1=B["hb"][:, lo:T], op=ALU.add),
                          reads=["hf", "hb"], writes=["hf"])
                    P.add("dve", lambda e, lo=lo: e.tensor_tensor(out=yo[:, lo:T], in0=B["hf"][:, lo:T], in1=B["gg"][:, lo:T], op=ALU.mult),
                          reads=["hf", "gg"], writes=["yo"])
                    P.add("sp", lambda e, cb=cb, lo=lo: e.dma_start(out=yrT[cb * 128:(cb + 1) * 128, lo:T], in_=yo[:, lo:T]), reads=["yo"], writes=["yrT"], kind="d")
                run_phase(P)

        def phase_c(l, last):
            with contextlib.ExitStack() as ph:
                P = newP()
                xs = sb(ph, "xs", [128, KC, 512])
                sq = [sb(ph, f"sq{i}", [128, 512], BF16) for i in range(2)]
                rstd = sb(ph, "rstd", [128, 512])
                tmp = [sb(ph, f"tmp{i}", [128, 512]) for i in range(2)]
                hT = sb(ph, "hT", [128, KC, 512], BF16)
                zT = sb(ph, "zT", [128, KC, 512], BF16)
                yas = sb(ph, "yas", [128, 8, 512], BF16)
                yrs = sb(ph, "yrs", [128, 8, 512], BF16)
                wga = [sb(ph, f"wga{i}", [128, KC, 256], BF16) for i in range(2)]
                wgb = [sb(ph, f"wgb{i}", [128, KC, 256], BF16) for i in range(2)]
                woa = [sb(ph, f"woa{i}", [128, 8, 256], BF16) for i in range(2)]
                wor = [sb(ph, f"wor{i}", [128, 8, 256], BF16) for i in range(2)]
                wo = [sb(ph, f"wo{i}", [128, KC, 256], BF16) for i in range(2)]
                s1 = [sb(ph, f"s1_{i}", [128, 512]) for i in range(2)]
                s2 = [sb(ph, f"s2_{i}", [128, 512]) for i in range(2)]
                t1 = [sb(ph, f"t1_{i}", [128, 512]) for i in range(2)]
                t2 = [sb(ph, f"t2_{i}", [128, 512]) for i in range(2)]
                tl = [t for t in tiles if not (last and t[2])]
                for (t0, N, isctx) in tl:
                    col = 1 if isctx else 0
                    load_x(P, xs, t0, N)
                    P.add("sp", lambda e, t0=t0, N=N: e.dma_start(out=yas[:, :, :N], in_=yaT[:, t0:t0 + N].rearrange("(k p) n -> p k n", p=128)),
                          reads=["yaT"], writes=["yas"], kind="d")
                    P.add("sp", lambda e, t0=t0, N=N: e.dma_start(out=yrs[:, :, :N], in_=yrT[:, t0:t0 + N].rearrange("(k p) n -> p k n", p=128)),
                          reads=["yrT"], writes=["yrs"], kind="d")
                    norm_tile(P, xs, N, l, 1, col, sq, rstd, tmp, hT)
                    for o in range(KC):
                        b = (o // 2) % 2
                        jj = o % 2
                        cs = slice(jj * 128, (jj + 1) * 128)
                        if jj == 0:
                            w_in.load(P, "sp", wga[b][:, :, :], l, (28 + o) * 128, 256, writes=[f"wga{b}"])
                            w_in.load(P, "sp", wgb[b][:, :, :], l, (44 + o) * 128, 256, writes=[f"wgb{b}"])
                            w_oa.load(P, "sp", woa[b][:, :, :], l, o * 128, 256, writes=[f"woa{b}"])
                            w_or.load(P, "sp", wor[b][:, :, :], l, o * 128, 256, writes=[f"wor{b}"])
                        p1, p2, p3, p4 = P.ps(), P.ps(), P.ps(), P.ps()
                        for k in range(KC):
                            P.add("pe", lambda e, b=b, k=k, p1=p1, N=N, cs=cs: e.matmul(PS[p1][:, :N], lhsT=wga[b][:, k, cs], rhs=hT[:, k, :N], start=(k == 0), stop=(k == KC - 1)),
                                  reads=[f"wga{b}", f"hT{k}"], writes=[f"ps{p1}"])
                        for k in range(KC):
                            P.add("pe", lambda e, b=b, k=k, p2=p2, N=N, cs=cs: e.matmul(PS[p2][:, :N], lhsT=wgb[b][:, k, cs], rhs=hT[:, k, :N], start=(k == 0), stop=(k == KC - 1)),
                                  reads=[f"wgb{b}", f"hT{k}"], writes=[f"ps{p2}"])
                        for k in range(8):
                            P.add("pe", lambda e, b=b, k=k, p3=p3, N=N, cs=cs: e.matmul(PS[p3][:, :N], lhsT=woa[b][:, k, cs], rhs=yas[:, k, :N], start=(k == 0), stop=(k == 7)),
                                  reads=[f"woa{b}", "yas"], writes=[f"ps{p3}"])
                        for k in range(8):
                            P.add("pe", lambda e, b=b, k=k, p4=p4, N=N, cs=cs: e.matmul(PS[p4][:, :N], lhsT=wor[b][:, k, cs], rhs=yrs[:, k, :N], start=(k == 0), stop=(k == 7)),
                                  reads=[f"wor{b}", "yrs"], writes=[f"ps{p4}"])
                        P.add("act", lambda e, b=b, p1=p1, N=N: e.activation(out=s1[b][:, :N], in_=PS[p1][:, :N], func=AF.Sigmoid), reads=[f"ps{p1}"], writes=[f"s1_{b}"])
                        P.add("act", lambda e, b=b, p2=p2, N=N: e.activation(out=s2[b][:, :N], in_=PS[p2][:, :N], func=AF.Sigmoid), reads=[f"ps{p2}"], writes=[f"s2_{b}"])
                        P.add("dve", lambda e, b=b, p3=p3, N=N: e.tensor_tensor(out=t1[b][:, :N], in0=s1[b][:, :N], in1=PS[p3][:, :N], op=ALU.mult),
                              reads=[f"s1_{b}", f"ps{p3}"], writes=[f"t1_{b}"])
                        P.add("dve", lambda e, b=b, p4=p4, N=N: e.tensor_tensor(out=t2[b][:, :N], in0=s2[b][:, :N], in1=PS[p4][:, :N], op=ALU.mult),
                              reads=[f"s2_{b}", f"ps{p4}"], writes=[f"t2_{b}"])
                        P.add("pool", lambda e, b=b, o=o, N=N: e.tensor_tensor(out=zT[:, o, :N], in0=t1[b][:, :N], in1=t2[b][:, :N], op=ALU.add),
                              reads=[f"t1_{b}", f"t2_{b}"], writes=[f"zT{o}"])
                    for m in range(KC):
                        b = (m // 2) % 2
                        jj = m % 2
                        cs = slice(jj * 128, (jj + 1) * 128)
                        if jj == 0:
                            w_out.load(P, "sp", wo[b][:, :, :], l, m * 128, 256, writes=[f"wo{b}"])
                        pd = P.ps()
                        for k in range(KC):
                            P.add("pe", lambda e, b=b, k=k, pd=pd, N=N, cs=cs: e.matmul(PS[pd][:, :N], lhsT=wo[b][:, k, cs], rhs=zT[:, k, :N], start=(k == 0), stop=(k == KC - 1)),
                                  reads=[f"wo{b}", f"zT{k}"], writes=[f"ps{pd}"])
                        P.add("dve", lambda e, m=m, pd=pd, N=N, col=col: e.scalar_tensor_tensor(
                            out=xs[:, m, :N], in0=PS[pd][:, :N], scalar=Gco[:, l, 1, m, col:col + 1], in1=xs[:, m, :N],
                            op0=ALU.mult, op1=ALU.add), reads=[f"ps{pd}", "xs"], writes=["xs"])
                    store_x(P, xs, t0, N)
                run_phase(P)

        def phase_final():
            with contextlib.ExitStack() as ph:
                P = newP()
                xs = sb(ph, "xs", [128, KC, 512])
                sq = [sb(ph, f"sq{i}", [128, 512], BF16) for i in range(2)]
                rstd = sb(ph, "rstd", [128, 512])
                ot = [sb(ph, f"ot{i}", [128, D]) for i in range(2)]
                oc = 0
                for (t0, N, isctx) in tiles:
                    if isctx:
                        continue
                    load_x(P, xs, t0, N)
                    rstd_tile(P, xs, N, sq, rstd)
                    for k in range(KC):
                        P.add("dve", lambda e, k=k, N=N: e.scalar_tensor_tensor(
                            out=xs[:, k, :N], in0=xs[:, k, :N], scalar=vT[:, 32 + k:33 + k], in1=rstd[:, :N], op0=ALU.mult, op1=ALU.mult),
                            reads=["xs", "rstd", "vT"], writes=["xs"])
                    for bb in range(N // 128):
                        ob = oc % 2
                        oc += 1
                        for kq in range(4):
                            pi = P.ps()
                            for kk in range(4):
                                k = kq * 4 + kk
                                P.add("pe", lambda e, k=k, kk=kk, bb=bb, pi=pi: e.transpose(
                                    out=PS[pi][:, kk * 128:(kk + 1) * 128], in_=xs[:, k, bb * 128:(bb + 1) * 128], identity=ident[:]),
                                    reads=["xs", "ident"], writes=[f"ps{pi}"])
                            if kq % 2:
                                P.add("act", lambda e, ob=ob, kq=kq, pi=pi: e.activation(out=ot[ob][:, kq * 512:(kq + 1) * 512], in_=PS[pi][:, :], func=AF.Identity),
                                      reads=[f"ps{pi}"], writes=[f"ot{ob}"])
                            else:
                                P.add("dve", lambda e, ob=ob, kq=kq, pi=pi: e.tensor_copy(out=ot[ob][:, kq * 512:(kq + 1) * 512], in_=PS[pi][:, :]),
                                      reads=[f"ps{pi}"], writes=[f"ot{ob}"])
                        r0 = t0 - CTX + bb * 128
                        P.add("sp", lambda e, ob=ob, r0=r0: e.dma_start(out=out[r0:r0 + 128, :], in_=ot[ob][:]), reads=[f"ot{ob}"], writes=["out"], kind="d")
                run_phase(P)

        for l in range(L):
            last = (l == L - 1)
            phase_ffn(l, 0, tiles)
            phase_a(l, last)
            phase_t(l, last)
            phase_r(l, last)
            phase_c(l, last)
            phase_ffn(l, 1, [t for t in tiles if not (last and t[2])])
        phase_final()
    return nc


def _consts(S):
    ident = np.eye(128, dtype=np.float32)
    rows = S // 64
    pos = np.stack([np.repeat(np.arange(rows, dtype=np.float32), 64), np.tile(np.arange(64, dtype=np.float32), rows)], 0)
    inv = (10000.0 ** (-np.arange(32, dtype=np.float32) / 32)).astype(np.float32)
    cos = np.zeros((128, S), np.float32)
    sin = np.zeros((128, S), np.float32)
    for a in range(2):
        ang = (pos[a][None, :] * inv[:, None]).astype(np.float32)
        for q in range(2):
            sl = slice(a * 64 + q * 32, a * 64 + q * 32 + 32)
            cos[sl] = np.cos(ang)
            sin[sl] = np.sin(ang) * (-1.0 if q == 0 else 1.0)
    j = np.arange(128)[:, None]
    i = np.arange(128)[None, :]
    mp = (j >= i).astype(np.float32)
    mn = (j <= i).astype(np.float32)
    masks = np.stack([np.tile(mp, (1, 4)), np.tile(mn, (1, 4))], 1).astype(np.float32)
    return ident, cos, sin, masks


def make_in_maps(inp, S, L, NR, cores):
    ident, cos, sin, masks = _consts(S)
    f = lambda a: np.ascontiguousarray(np.asarray(a, dtype=np.float32))
    w2d = {
        "w_ada": f(inp["w_ada"][:L]).reshape(L * D, 18432),
        "w_up": f(inp["w_ffn_up"][:L]).reshape(L * 2 * D, 2 * FF),
        "w_dn": f(inp["w_ffn_down"][:L]).reshape(L * 2 * FF, D),
        "w_in": f(inp["w_in"][:L]).reshape(L * D, INW),
        "w_oa": f(inp["w_o_attn"][:L]).reshape(L * 1024, D),
        "w_or": f(inp["w_o_rnn"][:L]).reshape(L * 1024, D),
        "w_out": f(inp["w_out"][:L]).reshape(L * D, D),
        "rg_w": f(inp["rg_w"][:L]).reshape(L * 32 * 128, 128),
    }
    maps = []
    for ci, b in enumerate(cores):
        rows = [f(inp["c"][b]).reshape(16, 128), f(inp["c_ctx"]).reshape(16, 128), f(inp["final_g"]).reshape(16, 128)]
        for l in range(L):
            rows += [f(inp["norm_g"][l]).reshape(48, 128), f(inp["b_ada"][l]).reshape(144, 128), f(inp["conv_w"][l]).reshape(32, 128),
                     f(inp["conv_b"][l]).reshape(8, 128), f(inp["rg_b"][l]).reshape(32, 128), f(inp["rg_lambda"][l]).reshape(16, 128)]
        m = {
            "x_b": f(inp["x"][b][:S]), "ctx_b": f(inp["ctx"][b]), "vecs": np.concatenate(rows, 0),
            "sink": f(inp["attn_sink"][:L]).reshape(1, L * 8), "ident": ident, "rope_cos": cos, "rope_sin": sin, "masks": masks,
        }
        for k, w in w2d.items():
            m[k + "_s"] = w
        maps.append(m)
    return maps


_NC_CACHE = {}


def kernel(**inputs):
    S, L, NR = 4096, 4, 8
    key = (S, L, NR)
    if key not in _NC_CACHE:
        _NC_CACHE[key] = build(S, L, NR)
    nc = _NC_CACHE[key]
    maps = make_in_maps(inputs, S, L, NR, list(range(8)))
    res = run_bass_kernel_spmd(nc, maps, core_ids=list(range(8)))
    return np.stack([np.asarray(r["out"], dtype=np.float32) for r in res.results], 0)
```
